# Optimizing a Trainium2 kernel written in Bass

```python
import jax, jax.numpy as jnp
from jax import lax
import numpy as np

D_MODEL = 1024
BATCH = 1
SEQ = 16384
DEPTH = 2
DEC_BATCH = 8
DEC_SEQ = 8192
PAST_LEN = 128

D_LRU = D_MODEL
N_LRU_HEADS = 8
LRU_BLOCK = D_LRU // N_LRU_HEADS
CONV_WIDTH = 4
CONV_LEFT = CONV_WIDTH // 2
LRU_C = 8.0
N_DIRS = 2
D_FOURIER = D_MODEL
N_FOURIER_GROUPS = 4
FOURIER_GROUP = D_FOURIER // N_FOURIER_GROUPS
N_BRANCHES = 2
D_IN = 2 * D_LRU + 2 * D_FOURIER + N_BRANCHES * D_MODEL
EPS = 1e-6

kernel_name = "hawk_fnet_parallel_adaln_encoder"


def rms_norm(x, g):
    xf = x.astype(jnp.float32)
    y = xf * lax.rsqrt(jnp.mean(xf * xf, axis=-1, keepdims=True) + EPS)
    return (y * g.astype(jnp.float32)).astype(x.dtype)


def centred_dwconv(x, w, b):
    S = x.shape[1]
    xp = jnp.pad(x, ((0, 0), (CONV_LEFT, CONV_WIDTH - 1 - CONV_LEFT), (0, 0)))
    y = xp[:, 0:S] * w[0] + b
    for k in range(1, CONV_WIDTH):
        y = y + xp[:, k:k + S] * w[k]
    return y


def _lin_combine(left, right):
    a1, b1 = left
    a2, b2 = right
    return a1 * a2, a2 * b1 + b2


def rg_lru(x, w_gates, b_gates, lam, reverse):
    Bn, S, _ = x.shape
    xf = x.astype(jnp.float32)
    xh = xf.reshape(Bn, S, N_LRU_HEADS, LRU_BLOCK)
    g = jnp.einsum('bshi,ghij->gbshj', xh, w_gates.astype(jnp.float32)).reshape(2, Bn, S, D_LRU)
    g = g + b_gates.astype(jnp.float32)[:, None, None, :]
    r = jax.nn.sigmoid(g[0])
    i = jax.nn.sigmoid(g[1])
    log_a = -LRU_C * r * jax.nn.softplus(-lam.astype(jnp.float32))
    a = jnp.exp(log_a)
    mult = jnp.sqrt(-jnp.expm1(2.0 * log_a))
    start = (S - 1) if reverse else 0
    is_start = (jnp.arange(S) == start)[None, :, None]
    mult = jnp.where(is_start, 1.0, mult)
    _, h = lax.associative_scan(_lin_combine, (a, mult * i * xf), reverse=reverse, axis=1)
    return h


def fourier_mix(x):
    Bn, S, _ = x.shape
    xg = x.astype(jnp.float32).reshape(Bn, S, N_FOURIER_GROUPS, FOURIER_GROUP)
    y = jnp.fft.fftn(xg, axes=(1, 3), norm="ortho").real
    return y.reshape(Bn, S, D_FOURIER)


def layer(x, c, norm_g, w_ada, b_ada, w_in, conv_w, conv_b, w_rg, b_rg, lam, w_a_out, w_b_out, w_o):
    mod = jax.nn.silu(c) @ w_ada + b_ada
    shift, scale, gate = jnp.split(mod, 3, axis=-1)
    h = rms_norm(x, norm_g) * (1.0 + scale[:, None, :]) + shift[:, None, :]
    proj = h @ w_in
    xa, ga, xb, gb, merge = jnp.split(
        proj, [D_LRU, 2 * D_LRU, 2 * D_LRU + D_FOURIER, 2 * D_LRU + 2 * D_FOURIER], axis=-1)
    xa = centred_dwconv(xa, conv_w, conv_b)
    ya = rg_lru(xa, w_rg[0], b_rg[0], lam[0], False) + rg_lru(xa, w_rg[1], b_rg[1], lam[1], True)
    ya = (ya.astype(x.dtype) * jax.nn.silu(ga)) @ w_a_out
    yb = (fourier_mix(xb).astype(x.dtype) * jax.nn.silu(gb)) @ w_b_out
    sa, sb = jnp.split(jax.nn.sigmoid(merge), 2, axis=-1)
    out = (sa * ya + sb * yb) @ w_o
    return x + gate[:, None, :] * out


def trunk(x, c, norm_g, w_ada, b_ada, w_in, conv_w, conv_b, w_rg, b_rg, lam, w_a_out, w_b_out, w_o, final_g):
    for l in range(DEPTH):
        x = layer(x, c, norm_g[l], w_ada[l], b_ada[l], w_in[l], conv_w[l], conv_b[l],
                  w_rg[l], b_rg[l], lam[l], w_a_out[l], w_b_out[l], w_o[l])
    return rms_norm(x, final_g)


def setup_inputs(seed: int = 0) -> dict:
    key = jax.random.key(seed)
    ks = jax.random.split(key, 20)
    f32 = jnp.float32
    x_prompt = jax.random.normal(ks[0], (BATCH, SEQ, D_MODEL), f32)
    x_sample = jax.random.normal(ks[1], (DEC_BATCH, DEC_SEQ, D_MODEL), f32)
    c_prompt = jax.random.normal(ks[2], (BATCH, D_MODEL), f32)
    c_sample = jax.random.normal(ks[3], (DEC_BATCH, D_MODEL), f32)
    norm_g = 1.0 + 0.05 * jax.random.normal(ks[4], (DEPTH, D_MODEL), f32)
    w_ada = jax.random.normal(ks[5], (DEPTH, D_MODEL, 3 * D_MODEL), f32) * (0.5 * D_MODEL ** -0.5)
    b_ada = 0.02 * jax.random.normal(ks[6], (DEPTH, 3 * D_MODEL), f32)
    w_in = jax.random.normal(ks[7], (DEPTH, D_MODEL, D_IN), f32) * D_MODEL ** -0.5
    conv_w = jax.random.normal(ks[8], (DEPTH, CONV_WIDTH, D_LRU), f32) * CONV_WIDTH ** -0.5
    conv_b = 0.02 * jax.random.normal(ks[9], (DEPTH, D_LRU), f32)
    w_rg = jax.random.normal(ks[10], (DEPTH, N_DIRS, 2, N_LRU_HEADS, LRU_BLOCK, LRU_BLOCK), f32) * LRU_BLOCK ** -0.5
    b_rg = 0.02 * jax.random.normal(ks[11], (DEPTH, N_DIRS, 2, D_LRU), f32)
    u = jax.random.uniform(ks[12], (DEPTH, N_DIRS, D_LRU), f32, minval=0.9, maxval=0.999)
    a0 = u ** (1.0 / LRU_C)
    lam = jnp.log(a0) - jnp.log1p(-a0)
    w_a_out = jax.random.normal(ks[13], (DEPTH, D_LRU, D_MODEL), f32) * D_LRU ** -0.5
    w_b_out = jax.random.normal(ks[14], (DEPTH, D_FOURIER, D_MODEL), f32) * D_FOURIER ** -0.5
    w_o = jax.random.normal(ks[15], (DEPTH, D_MODEL, D_MODEL), f32) * D_MODEL ** -0.5
    final_g = 1.0 + 0.05 * jax.random.normal(ks[16], (D_MODEL,), f32)
    return {"x_prompt": x_prompt, "x_sample": x_sample, "c_prompt": c_prompt, "c_sample": c_sample,
            "norm_g": norm_g, "w_ada": w_ada, "b_ada": b_ada, "w_in": w_in,
            "conv_w": conv_w, "conv_b": conv_b, "w_rg": w_rg, "b_rg": b_rg, "lam": lam,
            "w_a_out": w_a_out, "w_b_out": w_b_out, "w_o": w_o, "final_g": final_g}


def reference(x_prompt, x_sample, c_prompt, c_sample, norm_g, w_ada, b_ada, w_in, conv_w, conv_b,
              w_rg, b_rg, lam, w_a_out, w_b_out, w_o, final_g):
    y_prompt = trunk(x_prompt, c_prompt, norm_g, w_ada, b_ada, w_in, conv_w, conv_b, w_rg, b_rg, lam,
                     w_a_out, w_b_out, w_o, final_g)
    y_sample = trunk(x_sample, c_sample, norm_g, w_ada, b_ada, w_in, conv_w, conv_b, w_rg, b_rg, lam,
                     w_a_out, w_b_out, w_o, final_g)
    return (y_prompt, y_sample)
```

```python
import numpy as np
import concourse.bass as bass
import concourse.mybir as mybir
from concourse.bass_utils import run_bass_kernel_spmd

F32 = mybir.dt.float32
BF16 = mybir.dt.bfloat16
AF = mybir.ActivationFunctionType
ALU = mybir.AluOpType

D = 1024
DEPTH = 2
NCH = 8
EPS = 1e-6
EPOCH = 16000
DMA_EPOCH = 1000


class Buf:
    __slots__ = ("name", "w", "r", "slot", "excl")

    def __init__(self, name, excl=False):
        self.name = name
        self.w = None
        self.r = {}
        self.slot = None
        self.excl = excl


class Slot:
    __slots__ = ("sem", "count")

    def __init__(self, sem):
        self.sem = sem
        self.count = 0


class Prog:
    ENGS = ("pe", "act", "dve", "pool", "sp")

    def __init__(self, nc):
        self.nc = nc
        self.ops = {e: [] for e in self.ENGS}
        self.seq = {e: 0 for e in self.ENGS}
        self.esems = {e: [] for e in self.ENGS}
        self.known = {e: {} for e in self.ENGS}
        self.nsem = 0
        self.slots = []
        self.free = []
        self.ninst = 0

    def _newsem(self, name):
        self.nsem += 1
        return self.nc.alloc_semaphore(name)

    def _esem(self, e, k):
        ep = k // EPOCH
        while len(self.esems[e]) <= ep:
            self.esems[e].append(self._newsem(f"s_{e}_{len(self.esems[e])}"))
        return self.esems[e][ep], (k % EPOCH) + 1

    def _slot(self, b):
        if b.slot is None:
            if self.free:
                b.slot = self.free.pop()
            else:
                b.slot = Slot(self._newsem(f"d_{len(self.slots)}"))
                self.slots.append(b.slot)
        return b.slot

    def release(self, b):
        if b.slot is not None:
            if b.slot.count < 3800:
                self.free.append(b.slot)
            b.slot = None

    def _wait(self, e, ev):
        if ev is None:
            return
        if ev[0] == "eng":
            _, f, k = ev
            if f == e and e == "pe":
                return
            key = ("eng", f)
            if self.known[e].get(key, -1) >= k:
                return
            self.known[e][key] = k
            sem, val = self._esem(f, k)
        else:
            _, slot, cnt = ev
            key = ("dma", id(slot))
            if self.known[e].get(key, 0) >= cnt:
                return
            self.known[e][key] = cnt
            sem, val = slot.sem, 16 * cnt
        self.ninst += 1
        self.ops[e].append(lambda eng, sem=sem, val=val: eng.wait_ge(sem, val))

    def _deps(self, e, reads, writes):
        for b in reads:
            self._wait(e, b.w)
        for b in writes:
            self._wait(e, b.w)
            for ev in b.r.values():
                self._wait(e, ev)

    @staticmethod
    def _mark(ev, key, reads, writes):
        for b in reads:
            b.r[key] = ev
        for b in writes:
            b.w = ev
            b.r = {}

    def op(self, e, fn, reads=(), writes=(), inc=True):
        if any(b.excl for b in reads):
            writes = list(writes) + [b for b in reads if b.excl]
            reads = [b for b in reads if not b.excl]
        self._deps(e, reads, writes)
        k = self.seq[e]
        self.ninst += 1
        if inc:
            self.seq[e] += 1
            sem, _ = self._esem(e, k)
            self.ops[e].append(lambda eng, fn=fn, sem=sem: fn(eng).then_inc(sem, 1))
        else:
            self.ops[e].append(lambda eng, fn=fn: fn(eng))
        ev = ("eng", e, k)
        self._mark(ev, ("eng", e), reads, writes)
        return ev

    def dma(self, q, out, in_, reads, writes, sb, **kw):
        self._deps(q, reads, writes)
        slot = self._slot(sb)
        if slot.count > 0:
            self._wait(q, ("dma", slot, slot.count))
        slot.count += 1
        cnt = slot.count
        sem = slot.sem
        self.ninst += 1
        self.ops[q].append(
            lambda eng, out=out, in_=in_, sem=sem, kw=kw: eng.dma_start(out=out, in_=in_, **kw).then_inc(sem, 16))
        ev = ("dma", slot, cnt)
        self._mark(ev, ("dma", id(slot)), reads, writes)
        return ev

    def barrier(self):
        evs = []
        for f in self.ENGS:
            if self.seq[f] > 0:
                evs.append(("eng", f, self.seq[f] - 1))
        for sl in self.slots:
            if sl.count > 0:
                evs.append(("dma", sl, sl.count))
        for e in self.ENGS:
            for ev in evs:
                self._wait(e, ev)

    def emit(self):
        nc = self.nc
        self.barrier()
        with nc.Block() as block:
            @block.tensor
            def _(eng):
                for t in self.ops["pe"]:
                    t(eng)

            @block.scalar
            def _(eng):
                for t in self.ops["act"]:
                    t(eng)

            @block.vector
            def _(eng):
                for t in self.ops["dve"]:
                    t(eng)

            @block.gpsimd
            def _(eng):
                for t in self.ops["pool"]:
                    t(eng)

            @block.sync
            def _(eng):
                for t in self.ops["sp"]:
                    t(eng)


class Tl:
    def __init__(self, t, name, nparts=0):
        self.t = t
        self.b = Buf(name)
        self.p = [Buf(f"{name}_{i}") for i in range(nparts)]


V_NG = 0
V_BADA = 16
V_CW = 64
V_CB = 128
V_BRG = 144
V_LAM = 208
V_N = 240


def build(SS, SP):
    nc = bass.Bass("TRN2", target_bir_lowering=False)
    P = Prog(nc)
    SMAX = max(SS, SP)

    def din(name, shape, dt=F32):
        return nc.dram_tensor(name, list(shape), dt, kind="ExternalInput").ap()

    def dout(name, shape, dt=F32):
        return nc.dram_tensor(name, list(shape), dt, kind="ExternalOutput").ap()

    def dscr(name, shape, dt=BF16):
        return nc.dram_tensor(name, list(shape), dt).ap()

    xs_d = din("xs", [SS, D]); xp_d = din("xp", [SP, D])
    cv_d = din("cv", [128, NCH, 2]); vec_d = din("vec", [128, V_N]); fgb_d = din("fgb", [128, D])
    w_ada_d = din("w_ada", [DEPTH, D, 3 * D]); w_in_d = din("w_in", [DEPTH, D, 6 * D])
    w_rg_d = din("w_rg", [DEPTH, 2, 2, NCH, 128, 128])
    w_ao_d = din("w_a_out", [DEPTH, D, D]); w_bo_d = din("w_b_out", [DEPTH, D, D]); w_o_d = din("w_o", [DEPTH, D, D])
    ident_d = din("ident", [128, 128]); cs_d = din("cs", [256, 512]); f3_d = din("f3", [128, 2, 128])
    N2S, N2P = SS // 128, SP // 128
    def gshape(S_):
        N2_ = S_ // 128
        J_ = max(1, 128 // N2_)
        NST_ = 128 // J_
        SB_ = min(16, NST_)
        return [NST_ // SB_, J_ * N2_, SB_, 3, J_ * N2_]
    gts_d = din("gts", gshape(SS)); gtp_d = din("gtp", gshape(SP))
    ys_d = dout("ys", [SS, D]); yp_d = dout("yp", [SP, D])

    w_in_bf = dscr("w_in_bf", [DEPTH, 12, 128, NCH, 512])
    w_ao_bf = dscr("w_ao_bf", [DEPTH, 128, NCH, D]); w_bo_bf = dscr("w_bo_bf", [DEPTH, 128, NCH, D])
    w_o_bf = dscr("w_o_bf", [DEPTH, 128, NCH, D])
    w_rg_bf = dscr("w_rg_bf", [DEPTH, 128, 2, 2, NCH, 128])
    gts_bf = dscr("gts_bf", gshape(SS)); gtp_bf = dscr("gtp_bf", gshape(SP))
    XA = dscr("XA", [NCH, 128, SMAX]); SGA = dscr("SGA", [NCH, 128, SMAX])
    XB = dscr("XB", [NCH, 128, SMAX]); SGB = dscr("SGB", [NCH, 128, SMAX])
    SM = dscr("SM", [16, 128, SMAX]); AT = dscr("AT", [NCH, 128, SMAX]); BT = dscr("BT", [NCH, 128, SMAX])
    FB = dscr("FB", [4, SMAX // 128, 128, 512])
    Y1 = dscr("Y1", [SMAX, D], F32)
    DB = {}

    def dbuf(*key):
        if key not in DB:
            DB[key] = Buf("d" + "_".join(str(k) for k in key))
        return DB[key]

    class Arena:
        def __init__(self, base, limit):
            self.base, self.limit, self.cur, self.n, self.live = base, limit, base, 0, []

        def reset(self):
            self.cur = self.base
            for tl in self.live:
                P.release(tl.b)
                for pb_ in tl.p:
                    P.release(pb_)
            self.live = []

        def alloc(self, name, shape, dt, nparts=0):
            esz = 4 if dt == F32 else 2
            nbytes = int(np.prod(shape[1:])) * esz
            nbytes = (nbytes + 63) // 64 * 64
            assert self.cur + nbytes <= self.limit, (name, self.cur, nbytes, self.limit)
            self.n += 1
            t = nc.alloc_sbuf_tensor_at(f"{name}_{self.n}", list(shape), dt, offset=self.cur)
            self.cur += nbytes
            tl = Tl(t, f"{name}_{self.n}", nparts)
            self.live.append(tl)
            return tl

    PERS = Arena(16640, 33024)
    AR = Arena(33024, 229000)

    psum = nc.alloc_psum_tensor("psum_all", [128, 4096], F32)
    PB = [Buf(f"psb{i}", excl=True) for i in range(8)]

    def bank(i, n=512, p=128):
        return psum[0:p, i * 512:i * 512 + n]

    def act(out, in_, func, reads, writes, scale=1.0, bias=None, accum_out=None):
        kw = {"scale": scale}
        if bias is not None:
            kw["bias"] = bias
        if accum_out is not None:
            kw["accum_out"] = accum_out
        return P.op("act", lambda e: e.activation(out=out, in_=in_, func=func, **kw), reads, writes)

    def mm(out, lhsT, rhs, start, stop, reads, writes, inc):
        return P.op("pe", lambda e: e.matmul(out, lhsT=lhsT, rhs=rhs, start=start, stop=stop), reads, writes, inc=inc)

    def ts(eng, out, in0, s1, s2, op0, op1, reads, writes):
        return P.op(eng, lambda e: e.tensor_scalar(out=out, in0=in0, scalar1=s1, scalar2=s2, op0=op0, op1=op1), reads, writes)

    def tt(eng, out, in0, in1, op, reads, writes):
        return P.op(eng, lambda e: e.tensor_tensor(out=out, in0=in0, in1=in1, op=op), reads, writes)

    def stt(out, in0, scalar, in1, op0, op1, reads, writes):
        return P.op("dve", lambda e: e.scalar_tensor_tensor(out=out, in0=in0, scalar=scalar, in1=in1, op0=op0, op1=op1), reads, writes)

    def cp(eng, out, in_, reads, writes):
        return P.op(eng, lambda e: e.tensor_copy(out=out, in_=in_), reads, writes)

    def ld(out, in_, reads, tl, q="sp", wr=None):
        return P.dma(q, out, in_, reads, wr if wr is not None else [tl.b], tl.b)

    def st(out, in_, rd, tl, dwrites, q="sp"):
        return P.dma(q, out, in_, rd, dwrites, tl.b)

    ident_f = PERS.alloc("ident_f", [128, 128], F32)
    ident_b = PERS.alloc("ident_b", [128, 128], BF16)
    ones_f = PERS.alloc("ones_f", [128, 128], F32)
    vec = PERS.alloc("vec", [128, V_N], F32)
    cv = PERS.alloc("cv", [128, NCH, 2], F32)
    scv = PERS.alloc("scv", [128, NCH, 2], F32)
    hbrg = PERS.alloc("hbrg", [128, 64], F32)
    kk = PERS.alloc("kk", [128, 32], F32)
    kh = PERS.alloc("kh", [128, 32], F32)
    mod = PERS.alloc("mod", [128, DEPTH, 24, 2], F32)
    gm = PERS.alloc("gm", [128, DEPTH, NCH, 2], F32)
    one1 = PERS.alloc("one1", [128, 1], F32)
    half1 = PERS.alloc("half1", [128, 1], F32)
    cs_b = PERS.alloc("cs_b", [128, 2, 512], BF16)
    f3_b = PERS.alloc("f3_b", [128, 2, 128], BF16)
    fgb = PERS.alloc("fgb", [128, D], F32)
    tmpv = [PERS.alloc(f"tmpv{i}", [128, 32], F32) for i in range(4)]

    def setup():
        ld(ident_f.t[:], ident_d, [], ident_f)
        ld(vec.t[:], vec_d, [], vec)
        ld(cv.t[:], cv_d, [], cv)
        ld(fgb.t[:], fgb_d, [], fgb)
        ld(ident_b.t[:], ident_d, [], ident_b, q="pool")
        ld(cs_b.t[:], cs_d.rearrange("(cc p) n -> p cc n", p=128), [], cs_b, q="pool")
        ld(f3_b.t[:], f3_d, [], f3_b, q="pool")
        P.op("dve", lambda e: e.memset(ones_f.t[:], 1.0), [], [ones_f.b])
        P.op("dve", lambda e: e.memset(one1.t[:], 1.0), [], [one1.b])
        P.op("dve", lambda e: e.memset(half1.t[:], 0.5), [], [half1.b])
        wsems = [Buf(f"wconv{i}") for i in range(8)]
        wn = [0]

        class _W:
            pass

        def wsem_next():
            wn[0] += 1
            return wsems[wn[0] % 8]
        for l in range(DEPTH):
            for g in range(12):
                src = w_in_d[l, :, g * 512:(g + 1) * 512].rearrange("(ic p) n -> p ic n", p=128)
                P.dma("pool", w_in_bf[l, g], src, [], [dbuf("w_in", l, g)], wsem_next())
            for nm, sd, dd in (("w_ao", w_ao_d, w_ao_bf), ("w_bo", w_bo_d, w_bo_bf), ("w_o", w_o_d, w_o_bf)):
                for h in range(2):
                    src = sd[l, :, h * 512:(h + 1) * 512].rearrange("(ic p) n -> p ic n", p=128)
                    P.dma("pool", dd[l, :, :, h * 512:(h + 1) * 512], src, [], [dbuf(nm, l, h)], wsem_next())
            for d in range(2):
                for g in range(2):
                    src = w_rg_d[l, d, g].rearrange("h i j -> i h j")
                    P.dma("pool", w_rg_bf[l, :, d, g], src, [], [dbuf("w_rg", l, d, g)], wsem_next())
        for blk in range(gshape(SS)[0]):
            P.dma("pool", gts_bf[blk], gts_d[blk], [], [dbuf("gts", blk)], wsem_next())
        for blk in range(gshape(SP)[0]):
            P.dma("pool", gtp_bf[blk], gtp_d[blk], [], [dbuf("gtp", blk)], wsem_next())

        v = vec.t
        ts("dve", hbrg.t[:], v[:, V_BRG:V_BRG + 64], 0.5, None, ALU.mult, ALU.bypass, [vec.b], [hbrg.b])
        e_, z_, z2_, acc_ = tmpv
        act(e_.t[:], v[:, V_LAM:V_LAM + 32], AF.Exp, [vec.b], [e_.b], scale=-1.0)
        ts("dve", z_.t[:], e_.t[:], 2.0, None, ALU.add, ALU.bypass, [e_.b], [z_.b])
        P.op("dve", lambda e: e.reciprocal(out=z_.t[:], in_=z_.t[:]), [z_.b], [z_.b])
        tt("dve", z_.t[:], z_.t[:], e_.t[:], ALU.mult, [z_.b, e_.b], [z_.b])
        tt("dve", z2_.t[:], z_.t[:], z_.t[:], ALU.mult, [z_.b], [z2_.b])
        ts("dve", acc_.t[:], z2_.t[:], 1.0 / 11.0, 1.0 / 9.0, ALU.mult, ALU.add, [z2_.b], [acc_.b])
        for cst in (1.0 / 7.0, 1.0 / 5.0, 1.0 / 3.0, 1.0):
            tt("dve", acc_.t[:], acc_.t[:], z2_.t[:], ALU.mult, [acc_.b, z2_.b], [acc_.b])
            ts("dve", acc_.t[:], acc_.t[:], cst, None, ALU.add, ALU.bypass, [acc_.b], [acc_.b])
        tt("dve", acc_.t[:], acc_.t[:], z_.t[:], ALU.mult, [acc_.b, z_.b], [acc_.b])
        ts("dve", kk.t[:], acc_.t[:], -16.0, None, ALU.mult, ALU.bypass, [acc_.b], [kk.b])
        ts("dve", kh.t[:], acc_.t[:], -8.0, None, ALU.mult, ALU.bypass, [acc_.b], [kh.b])

        act(scv.t[:], cv.t[:], AF.Silu, [cv.b], [scv.b])
        AR.reset()
        wada = [AR.alloc("wada", [128, NCH, D], F32) for _ in range(2)]
        n = 0
        for l in range(DEPTH):
            for third in range(3):
                wt = wada[n % 2]; n += 1
                for h in range(2):
                    ld(wt.t[:, :, h * 512:(h + 1) * 512],
                       w_ada_d[l, :, third * D + h * 512: third * D + (h + 1) * 512].rearrange("(ic p) n -> p ic n", p=128),
                       [], wt)
                for jj in range(8):
                    j = third * 8 + jj
                    for ic in range(NCH):
                        mm(psum[:, 2 * j:2 * j + 2], wt.t[:, ic, jj * 128:(jj + 1) * 128], scv.t[:, ic, :],
                           ic == 0, ic == NCH - 1, [wt.b, scv.b], [PB[0]], inc=(ic == NCH - 1))
            for j in range(24):
                ts("dve", mod.t[:, l, j, :], psum[:, 2 * j:2 * j + 2], v[:, V_BADA + l * 24 + j:V_BADA + l * 24 + j + 1], None,
                   ALU.add, ALU.bypass, [PB[0], vec.b], [mod.b])
            for c in range(NCH):
                ts("dve", gm.t[:, l, c, :], mod.t[:, l, 8 + c, :], 1.0, v[:, V_NG + l * 8 + c:V_NG + l * 8 + c + 1],
                   ALU.add, ALU.mult, [mod.b, vec.b], [gm.b])
        P.barrier()

    def phase1(S, x_src, xkey, l, col):
        AR.reset()
        SBK = min(2048, S)
        NB = SBK // 512
        nsb = S // SBK
        xt = [AR.alloc("xt", [128, 4, D], F32) for _ in range(2)]
        xn = [AR.alloc("xn", [128, 4, D], F32, nparts=4) for _ in range(2)]
        hT = [AR.alloc("hT", [128, NCH, SBK], BF16, nparts=NCH * NB) for _ in range(2)]
        win = [AR.alloc("win", [128, NCH, 512], BF16) for _ in range(3)]
        stg = [AR.alloc("stg", [128, SBK], BF16, nparts=NB) for _ in range(6)]
        junk = AR.alloc("junk", [128, D], BF16)
        ss = [AR.alloc("ss", [128, 4], F32) for _ in range(2)]
        rs = [AR.alloc("rs", [128, 4], F32) for _ in range(2)]
        v = vec.t
        cnt = {"blk": 0, "tb": 0, "pb": 0, "w": 0, "stg": 0}

        def prep_nonpe(sb, blk):
            i = cnt["blk"]
            cnt["blk"] += 1
            x_, n_, s_, r_ = xt[i % 2], xn[i % 2], ss[i % 2], rs[i % 2]
            tok0 = sb * SBK + blk * 512
            rd = [dbuf(xkey, tok0 // 512)] if xkey else []
            ld(x_.t[:], x_src[tok0:tok0 + 512, :].rearrange("(j p) c -> p j c", p=128), rd, x_)
            for j in range(4):
                act(junk.t[:], x_.t[:, j, :], AF.Square, [x_.b], [junk.b, s_.b], accum_out=s_.t[:, j:j + 1])
            ts("dve", r_.t[:], s_.t[:], 1.0 / D, EPS, ALU.mult, ALU.add, [s_.b], [r_.b])
            act(r_.t[:], r_.t[:], AF.Sqrt, [r_.b], [r_.b])
            P.op("dve", lambda e: e.reciprocal(out=r_.t[:], in_=r_.t[:]), [r_.b], [r_.b])
            for j in range(4):
                ts("pool", n_.t[:, j, :], x_.t[:, j, :], r_.t[:, j:j + 1], 1.0, ALU.mult, ALU.mult, [x_.b, r_.b], [n_.p[j]])
            return n_

        def prep_pe(sb, blk, n_):
            h_ = hT[sb % 2]
            for c in range(NCH):
                bk = cnt["tb"] % 2
                cnt["tb"] += 1
                for j in range(4):
                    P.op("pe", lambda e, j=j, c=c, bk=bk: e.transpose(psum[:, bk * 512 + j * 128: bk * 512 + (j + 1) * 128],
                                                                       n_.t[:, j, c * 128:(c + 1) * 128], ident_f.t[:]),
                         [n_.p[j], ident_f.b], [PB[bk]], inc=(j == 3))
                act(h_.t[:, c, blk * 512:(blk + 1) * 512], bank(bk), AF.Identity, [PB[bk], gm.b, mod.b], [h_.p[c * NB + blk]],
                    scale=gm.t[:, l, c, col:col + 1], bias=mod.t[:, l, c, col:col + 1])

        def jc_info(jc):
            if jc < 8:
                return "copy", XA, "XA", jc
            if jc < 16:
                return "silu", SGA, "SGA", jc - 8
            if jc < 24:
                return "copy", XB, "XB", jc - 16
            if jc < 32:
                return "silu", SGB, "SGB", jc - 24
            return "sigm", SM, "SM", jc - 32

        def load_w(idx):
            if idx < nsb * 12:
                g_ = idx % 12
                w__ = win[idx % 3]
                ld(w__.t[:], w_in_bf[l, g_], [dbuf("w_in", l, g_)], w__)

        def proj_group(sb, g):
            h_ = hT[sb % 2]
            idx = sb * 12 + g
            w_ = win[idx % 3]
            if idx == 0:
                load_w(0)
                load_w(1)
            load_w(idx + 2)
            for jcl in range(4):
                jc = g * 4 + jcl
                kind, dten, dname, ch = jc_info(jc)
                s_ = stg[cnt["stg"] % 6]
                cnt["stg"] += 1
                for tile in range(NB):
                    bk = 2 + cnt["pb"] % 6
                    cnt["pb"] += 1
                    for ic in range(NCH):
                        mm(bank(bk), w_.t[:, ic, jcl * 128:(jcl + 1) * 128], h_.t[:, ic, tile * 512:(tile + 1) * 512],
                           ic == 0, ic == NCH - 1, [w_.b, h_.p[ic * NB + tile]], [PB[bk]], inc=(ic == NCH - 1))
                    o = s_.t[:, tile * 512:(tile + 1) * 512]
                    if kind == "copy":
                        cp("dve", o, bank(bk), [PB[bk]], [s_.p[tile]])
                    elif kind == "silu":
                        act(o, bank(bk), AF.Silu, [PB[bk]], [s_.p[tile]])
                    else:
                        act(o, bank(bk), AF.Sigmoid, [PB[bk]], [s_.p[tile]])
                P.dma("sp", dten[ch, :, sb * SBK:(sb + 1) * SBK], s_.t[:], s_.p, [dbuf(dname, ch, sb)], s_.b)

        for blk in range(NB):
            n_ = prep_nonpe(0, blk)
            prep_pe(0, blk, n_)
        gper = 12 // NB
        for sb in range(nsb):
            cur = None
            for g in range(12):
                if sb + 1 < nsb and g % gper == 0:
                    cur = (g // gper, prep_nonpe(sb + 1, g // gper))
                proj_group(sb, g)
                if sb + 1 < nsb and g % gper == gper - 1:
                    prep_pe(sb + 1, cur[0], cur[1])
        P.barrier()

    def lru(S, l):
        AR.reset()
        SL = min(1024, S)
        NT = SL // 512
        nsl = S // SL
        SBK = min(2048, S)
        hb = AR.alloc("hb", [128, S], F32, nparts=nsl)
        wrg = AR.alloc("wrg", [128, 2, 2, NCH, 128], BF16)
        dk = [AR.alloc("dk", [128, 4, 128], BF16) for _ in range(2)]
        xap = [AR.alloc("xap", [128, SL + 4], BF16) for _ in range(3)]
        ring = lambda nm, dt: [AR.alloc(nm, [128, SL], dt) for _ in range(2)]
        xcb = [AR.alloc("xcb", [128, SL], BF16) for _ in range(3)]
        tr, ti, ip, aa, m_, u_, bb = (ring(n, F32) for n in ("tr", "ti", "ip", "aa", "m", "u", "bb"))
        hh, s_ = ring("hh", F32), ring("s", F32)
        sga = [AR.alloc("sga", [128, SL], BF16) for _ in range(4)]
        sA = ring("sA", BF16)
        v = vec.t
        for d in range(2):
            for g in range(2):
                ld(wrg.t[:, d, g], w_rg_bf[l, :, d, g], [dbuf("w_rg", l, d, g)], wrg)
        slabs = []
        for hd in range(NCH):
            for d in (1, 0):
                order = range(nsl - 1, -1, -1) if d == 1 else range(nsl)
                for sl in order:
                    slabs.append((hd, d, sl))
        n = len(slabs)
        st_ = {"prev_h": None}
        fidx = []
        nf = 0
        for (_h, _d, _s) in slabs:
            fidx.append(nf)
            if _d == 0:
                nf += 1

        def ldx(i):
            if i >= n:
                return
            hd, d, sl = slabs[i]
            x_ = xap[i % 3]
            t0 = sl * SL
            lo, hi = max(0, t0 - 2), min(S, t0 + SL + 2)
            off = lo - (t0 - 2)
            if t0 == 0:
                P.op("pool", lambda e, x_=x_: e.memset(x_.t[:, 0:2], 0.0), [], [x_.b])
            if t0 + SL == S:
                P.op("pool", lambda e, x_=x_: e.memset(x_.t[:, SL + 2:SL + 4], 0.0), [], [x_.b])
            rd = [dbuf("XA", hd, sb) for sb in range(lo // SBK, (hi - 1) // SBK + 1)]
            ld(x_.t[:, off:off + (hi - lo)], XA[hd, :, lo:hi], rd, x_)

        def stage_a(i):
            hd, d, sl = slabs[i]
            dk_ = dk[hd % 2]
            if d == 1 and sl == nsl - 1:
                for k in range(4):
                    col = V_CW + (l * 4 + k) * 8 + hd
                    ts("dve", dk_.t[:, k, :], ident_b.t[:], v[:, col:col + 1], None, ALU.mult, ALU.bypass,
                       [ident_b.b, vec.b], [dk_.b])
            ldx(i + 2)
            x_ = xap[i % 3]
            c0 = (i % 2) * 2
            for tile in range(NT):
                for k in range(4):
                    mm(bank(c0 + tile), dk_.t[:, k, :], x_.t[:, tile * 512 + k: tile * 512 + k + 512],
                       k == 0, k == 3, [dk_.b, x_.b], [PB[c0 + tile]], inc=(k == 3))
            cps = psum[:, c0 * 512:c0 * 512 + SL]
            cpb = [PB[c0 + t_] for t_ in range(NT)]
            cbcol = v[:, V_CB + l * 8 + hd:V_CB + l * 8 + hd + 1]
            xcb_ = xcb[i % 3]
            ts("dve", xcb_.t[:], cps, cbcol, None, ALU.add, ALU.bypass, cpb + [vec.b], [xcb_.b])
            if d == 0:
                g_ = sga[fidx[i] % 4]
                t0 = sl * SL
                ld(g_.t[:], SGA[hd, :, t0:t0 + SL], [dbuf("SGA", hd, t0 // SBK)], g_)

        def stage_b(i):
            hd, d, sl = slabs[i]
            t0 = sl * SL
            kcol = (l * 2 + d) * 8 + hd
            brc = hbrg.t[:, ((l * 2 + d) * 2 + 0) * 8 + hd:((l * 2 + d) * 2 + 0) * 8 + hd + 1]
            bic = hbrg.t[:, ((l * 2 + d) * 2 + 1) * 8 + hd:((l * 2 + d) * 2 + 1) * 8 + hd + 1]
            xcb_ = xcb[i % 3]
            tr_, ti_, ip_, a_, mm_, uu_, b_ = (r[i % 2] for r in (tr, ti, ip, aa, m_, u_, bb))
            for tile in range(NT):
                mm(bank(4 + tile), wrg.t[:, d, 0, hd, :], xcb_.t[:, tile * 512:(tile + 1) * 512], True, True,
                   [wrg.b, xcb_.b], [PB[4 + tile]], inc=True)
                mm(bank(6 + tile), wrg.t[:, d, 1, hd, :], xcb_.t[:, tile * 512:(tile + 1) * 512], True, True,
                   [wrg.b, xcb_.b], [PB[6 + tile]], inc=True)
            gr = psum[:, 4 * 512:4 * 512 + SL]
            gi = psum[:, 6 * 512:6 * 512 + SL]
            grb = [PB[4 + t_] for t_ in range(NT)]
            gib = [PB[6 + t_] for t_ in range(NT)]
            act(tr_.t[:], gr, AF.Tanh, grb + [hbrg.b], [tr_.b], scale=0.5, bias=brc)
            act(ti_.t[:], gi, AF.Tanh, gib + [hbrg.b], [ti_.b], scale=0.5, bias=bic)
            act(a_.t[:], tr_.t[:], AF.Exp, [tr_.b, kh.b], [a_.b], scale=kh.t[:, kcol:kcol + 1], bias=kh.t[:, kcol:kcol + 1])
            act(mm_.t[:], tr_.t[:], AF.Exp, [tr_.b, kk.b], [mm_.b], scale=kk.t[:, kcol:kcol + 1], bias=kk.t[:, kcol:kcol + 1])
            act(mm_.t[:], mm_.t[:], AF.Sqrt, [mm_.b, one1.b], [mm_.b], scale=-1.0, bias=one1.t[:, 0:1])
            if d == 1 and sl == nsl - 1:
                P.op("pool", lambda e, mm_=mm_: e.memset(mm_.t[:, SL - 1:SL], 1.0), [], [mm_.b])
            if d == 0 and sl == 0:
                P.op("pool", lambda e, mm_=mm_: e.memset(mm_.t[:, 0:1], 1.0), [], [mm_.b])
            ts("pool", ip_.t[:], ti_.t[:], 1.0, 1.0, ALU.add, ALU.mult, [ti_.b], [ip_.b])
            tt("pool", uu_.t[:], ip_.t[:], xcb_.t[:], ALU.mult, [ip_.b, xcb_.b], [uu_.b])
            stt(b_.t[:], uu_.t[:], 0.5, mm_.t[:], ALU.mult, ALU.mult, [uu_.b, mm_.b], [b_.b])
            if d == 1:
                if sl == nsl - 1:
                    init, ird = 0.0, []
                else:
                    init, ird = hb.t[:, t0 + SL:t0 + SL + 1], [hb.p[sl + 1]]
                P.op("dve", lambda e, t0=t0, a_=a_, b_=b_, init=init: e.tensor_tensor_scan(
                    out=hb.t[:, t0:t0 + SL][:, ::-1], data0=a_.t[:, ::-1], data1=b_.t[:, ::-1], initial=init,
                    op0=ALU.mult, op1=ALU.add), [a_.b, b_.b] + ird, [hb.p[sl]])
            else:
                f = fidx[i]
                h_ = hh[f % 2]
                if sl == 0:
                    init, ird = 0.0, []
                else:
                    prev_h = st_["prev_h"]
                    init, ird = prev_h.t[:, SL - 1:SL], [prev_h.b]
                P.op("dve", lambda e, h_=h_, a_=a_, b_=b_, init=init: e.tensor_tensor_scan(
                    out=h_.t[:], data0=a_.t[:], data1=b_.t[:], initial=init, op0=ALU.mult, op1=ALU.add),
                    [a_.b, b_.b] + ird, [h_.b])
                st_["prev_h"] = h_

        def stage_b2(i):
            hd, d, sl = slabs[i]
            if d != 0:
                return
            t0 = sl * SL
            f = fidx[i]
            h_ = hh[f % 2]
            g_ = sga[f % 4]
            sum_ = s_[f % 2]
            o_ = sA[f % 2]
            tt("pool", sum_.t[:], h_.t[:], hb.t[:, t0:t0 + SL], ALU.add, [h_.b, hb.p[sl]], [sum_.b])
            tt("dve", o_.t[:], sum_.t[:], g_.t[:], ALU.mult, [sum_.b, g_.b], [o_.b])
            st(AT[hd, :, t0:t0 + SL], o_.t[:], [o_.b], o_, [dbuf("AT", hd, sl)])

        ldx(0)
        ldx(1)
        stage_a(0)
        stage_a(1)
        for i in range(n):
            if i + 2 < n:
                stage_a(i + 2)
            if i >= 1 and slabs[i][1] == 1:
                stage_b2(i - 1)
            stage_b(i)
            if i >= 1 and slabs[i][1] == 0:
                stage_b2(i - 1)
        stage_b2(n - 1)
        P.barrier()

    def fourier(S, l, gt_bf, gtkey):
        AR.reset()
        N2 = S // 128
        J = max(1, 128 // N2)
        NSTEP = 128 // J
        M = J * N2
        SBLK = min(16, NSTEP)
        SBK = min(2048, S)
        invn = 1.0 / float(np.sqrt(S * 256.0))
        xbg = [AR.alloc("xbg", [128, 2, S], BF16) for _ in range(2)]
        gt = [AR.alloc("gt", [M, SBLK, 3, M], BF16) for _ in range(2)]
        wsb = [AR.alloc("wsb", [M, 512], BF16) for _ in range(3)]
        bst = [AR.alloc("bst", [M, 4, 512], BF16, nparts=4) for _ in range(3)]
        steps = [(g, pi) for g in range(4) for pi in range(NSTEP)]
        n = len(steps)

        def st_a(i):
            g, pi = steps[i]
            x_ = xbg[g % 2]
            if pi == 0:
                for cc in range(2):
                    ld(x_.t[:, cc, :], XB[2 * g + cc, :, 0:S], [dbuf("XB", 2 * g + cc, sb) for sb in range(S // SBK)], x_)
            if pi % SBLK == 0:
                gslot = gt[(i // SBLK) % 2]
                ld(gslot.t[:], gt_bf[pi // SBLK], [dbuf(gtkey, pi // SBLK)], gslot)
            wb = i % 2
            for cc in range(2):
                lhs = x_.t[:, cc, pi:S:NSTEP]
                mm(bank(wb, 512, M), lhs, cs_b.t[:, cc, :], cc == 0, cc == 1, [x_.b, cs_b.b], [PB[wb]], inc=(cc == 1))
            w_ = wsb[i % 3]
            if i % 2 == 0:
                act(w_.t[:], bank(wb, 512, M), AF.Identity, [PB[wb]], [w_.b])
            else:
                cp("dve", w_.t[:], bank(wb, 512, M), [PB[wb]], [w_.b])

        def st_b(i):
            g, pi = steps[i]
            gslot = gt[(i // SBLK) % 2]
            w_ = wsb[i % 3]
            bbk = 2 + i % 2
            sl16 = pi % SBLK
            Fr, Fi, Fin = gslot.t[:, sl16, 0, :], gslot.t[:, sl16, 1, :], gslot.t[:, sl16, 2, :]
            o_r = psum[0:M, bbk * 512: bbk * 512 + 256]
            o_i = psum[0:M, bbk * 512 + 256: bbk * 512 + 512]
            rd = [gslot.b, w_.b]
            mm(o_r, Fr, w_.t[:, 0:256], True, False, rd, [PB[bbk]], inc=False)
            mm(o_r, Fin, w_.t[:, 256:512], False, True, rd, [PB[bbk]], inc=False)
            mm(o_i, Fr, w_.t[:, 256:512], True, False, rd, [PB[bbk]], inc=False)
            mm(o_i, Fi, w_.t[:, 0:256], False, True, rd, [PB[bbk]], inc=True)
            b_ = bst[(i // 4) % 3]
            if i % 2 == 0:
                cp("dve", b_.t[:, pi % 4, :], bank(bbk, 512, M), [PB[bbk]], [b_.p[pi % 4]])
            else:
                act(b_.t[:, pi % 4, :], bank(bbk, 512, M), AF.Identity, [PB[bbk]], [b_.p[pi % 4]])
            if pi % 4 == 3:
                for j in range(J):
                    P.dma("sp", FB[g, 0:N2, j * NSTEP + pi - 3: j * NSTEP + pi + 1, :], b_.t[j * N2:(j + 1) * N2, :, :],
                          b_.p, [dbuf("FB", g, j, pi // 4)], b_.b)

        st_a(0)
        for i in range(n):
            if i + 1 < n:
                st_a(i + 1)
            st_b(i)
        P.barrier()
        AR.reset()
        sgb = AR.alloc("sgb", [128, 2, S], BF16)
        ybt = AR.alloc("ybt", [128, 2, S], BF16, nparts=2 * (N2 // 4))
        bin_ = [AR.alloc("bin", [128, 4, 512], BF16) for _ in range(3)]
        qi = 0
        for g in range(4):
            for mc in range(2):
                ld(sgb.t[:, mc, :], SGB[2 * g + mc, :, 0:S], [dbuf("SGB", 2 * g + mc, sb) for sb in range(S // SBK)], sgb)
            for q in range(N2 // 4):
                b_ = bin_[qi % 3]
                ld(b_.t[:], FB[g, 4 * q:4 * q + 4, :, :].rearrange("k s n -> s k n"), [dbuf("FB", g, j, i) for j in range(J) for i in range(NSTEP // 4)], b_)
                for mc in range(2):
                    bk = 4 + 2 * mc + (qi % 2)
                    for j in range(4):
                        o = psum[:, bk * 512 + j * 128: bk * 512 + (j + 1) * 128]
                        mm(o, b_.t[:, j, mc * 128:(mc + 1) * 128], f3_b.t[:, 0, :], True, False, [b_.b, f3_b.b], [PB[bk]], inc=False)
                        mm(o, b_.t[:, j, 256 + mc * 128:256 + (mc + 1) * 128], f3_b.t[:, 1, :], False, True, [b_.b, f3_b.b], [PB[bk]],
                           inc=(j == 3))
                    pv = bank(bk).rearrange("p (j k) -> p k j", j=4)
                    ov = ybt.t[:, mc, :].rearrange("p (k n) -> p k n", n=N2)[:, :, 4 * q:4 * q + 4]
                    gv = sgb.t[:, mc, :].rearrange("p (k n) -> p k n", n=N2)[:, :, 4 * q:4 * q + 4]
                    stt(ov, pv, invn, gv, ALU.mult, ALU.mult, [PB[bk], sgb.b], [ybt.p[mc * (N2 // 4) + q]])
                qi += 1
            for mc in range(2):
                st(BT[2 * g + mc, :, 0:S], ybt.t[:, mc, :], ybt.p[mc * (N2 // 4):(mc + 1) * (N2 // 4)], ybt, [dbuf("BT", 2 * g + mc)])
        P.barrier()

    def phase3(S, x_src, xkey, l, col, y_dst, ykey, final):
        AR.reset()
        SBK = min(2048, S)
        SL = min(1024, S)
        nblk = S // 512
        wa = AR.alloc("wa", [128, NCH, D], BF16)
        wb = AR.alloc("wb", [128, NCH, D], BF16)
        wog = AR.alloc("wog", [128, NCH, D], BF16)
        dg = AR.alloc("dg", [128, 128], F32)
        at = [AR.alloc("at", [128, NCH, 512], BF16) for _ in range(2)]
        bt = [AR.alloc("bt", [128, NCH, 512], BF16) for _ in range(2)]
        sm = [AR.alloc("sm", [128, 16, 512], BF16) for _ in range(2)]
        xr = [AR.alloc("xr", [128, 4, D], F32, nparts=4) for _ in range(2)]
        uT = [AR.alloc("uT", [128, NCH, 512], BF16, nparts=NCH) for _ in range(2)]
        ua = [AR.alloc("ua", [128, 512], F32) for _ in range(2)]
        ub = [AR.alloc("ub", [128, 512], F32) for _ in range(2)]
        junk = AR.alloc("junk3", [128, D], BF16)
        ss = [AR.alloc("ss3", [128, 4], F32) for _ in range(2)]
        rs = [AR.alloc("rs3", [128, 4], F32) for _ in range(2)]
        for h in range(2):
            ld(wa.t[:, :, h * 512:(h + 1) * 512], w_ao_bf[l, :, :, h * 512:(h + 1) * 512], [dbuf("w_ao", l, h)], wa)
            ld(wb.t[:, :, h * 512:(h + 1) * 512], w_bo_bf[l, :, :, h * 512:(h + 1) * 512], [dbuf("w_bo", l, h)], wb)
            ld(wog.t[:, :, h * 512:(h + 1) * 512], w_o_bf[l, :, :, h * 512:(h + 1) * 512], [dbuf("w_o", l, h)], wog)
        for c in range(NCH):
            ts("dve", dg.t[:], ident_f.t[:], mod.t[:, l, 16 + c, col:col + 1], None, ALU.mult, ALU.bypass,
               [ident_f.b, mod.b], [dg.b])
            mm(psum[:, c * 128:(c + 1) * 128], ones_f.t[:], dg.t[:], True, True, [ones_f.b, dg.b], [PB[c // 4]], inc=True)
        for ic in range(NCH):
            tt("dve", wog.t[:, ic, :], wog.t[:, ic, :], psum[:, 0:D], ALU.mult, [wog.b, PB[0], PB[1]], [wog.b])

        def loads(bi):
            tok0 = bi * 512
            a_, b_, m_, x_ = at[bi % 2], bt[bi % 2], sm[bi % 2], xr[bi % 2]
            ld(a_.t[:], AT[:, :, tok0:tok0 + 512].rearrange("c p t -> p c t"), [dbuf("AT", hd, tok0 // SL) for hd in range(NCH)], a_)
            ld(b_.t[:], BT[:, :, tok0:tok0 + 512].rearrange("c p t -> p c t"), [dbuf("BT", c) for c in range(NCH)], b_)
            for h in range(2):
                ld(m_.t[:, h * 8:(h + 1) * 8, :], SM[h * 8:(h + 1) * 8, :, tok0:tok0 + 512].rearrange("c p t -> p c t"),
                   [dbuf("SM", c, tok0 // SBK) for c in range(h * 8, (h + 1) * 8)], m_)
            rd = [dbuf(xkey, bi)] if xkey else []
            ld(x_.t[:], x_src[tok0:tok0 + 512, :].rearrange("(j p) c -> p j c", p=128), rd, x_, wr=[x_.b] + x_.p)

        loads(0)
        pb = 0
        for bi in range(nblk):
            if bi + 1 < nblk:
                loads(bi + 1)
            a_, b_, m_, x_ = at[bi % 2], bt[bi % 2], sm[bi % 2], xr[bi % 2]
            u_ = uT[bi % 2]
            y_ = x_
            tok0 = bi * 512
            for jc in range(NCH):
                bka = pb % 8; pb += 1
                bkb = pb % 8; pb += 1
                for ic in range(NCH):
                    mm(bank(bka), wa.t[:, ic, jc * 128:(jc + 1) * 128], a_.t[:, ic, :], ic == 0, ic == NCH - 1,
                       [wa.b, a_.b], [PB[bka]], inc=(ic == NCH - 1))
                for ic in range(NCH):
                    mm(bank(bkb), wb.t[:, ic, jc * 128:(jc + 1) * 128], b_.t[:, ic, :], ic == 0, ic == NCH - 1,
                       [wb.b, b_.b], [PB[bkb]], inc=(ic == NCH - 1))
                ua_, ub_ = ua[jc % 2], ub[jc % 2]
                tt("dve", ua_.t[:], bank(bka), m_.t[:, jc, :], ALU.mult, [PB[bka], m_.b], [ua_.b])
                tt("dve", ub_.t[:], bank(bkb), m_.t[:, 8 + jc, :], ALU.mult, [PB[bkb], m_.b], [ub_.b])
                tt("pool", u_.t[:, jc, :], ua_.t[:], ub_.t[:], ALU.add, [ua_.b, ub_.b], [u_.p[jc]])
            for tile in range(4):
                for half in range(2):
                    bk = pb % 8; pb += 1
                    for ic in range(NCH):
                        mm(bank(bk), u_.t[:, ic, tile * 128:(tile + 1) * 128], wog.t[:, ic, half * 512:(half + 1) * 512],
                           ic == 0, ic == NCH - 1, [u_.p[ic], wog.b], [PB[bk]], inc=(ic == NCH - 1))
                    tt("dve", y_.t[:, tile, half * 512:(half + 1) * 512], bank(bk), x_.t[:, tile, half * 512:(half + 1) * 512],
                       ALU.add, [PB[bk], x_.b], [y_.p[tile]])
            if final:
                s_, r_ = ss[bi % 2], rs[bi % 2]
                for j in range(4):
                    act(junk.t[:], y_.t[:, j, :], AF.Square, [y_.p[j]], [junk.b, s_.b], accum_out=s_.t[:, j:j + 1])
                ts("dve", r_.t[:], s_.t[:], 1.0 / D, EPS, ALU.mult, ALU.add, [s_.b], [r_.b])
                act(r_.t[:], r_.t[:], AF.Sqrt, [r_.b], [r_.b])
                P.op("dve", lambda e, r_=r_: e.reciprocal(out=r_.t[:], in_=r_.t[:]), [r_.b], [r_.b])
                for j in range(4):
                    stt(y_.t[:, j, :], y_.t[:, j, :], r_.t[:, j:j + 1], fgb.t[:], ALU.mult, ALU.mult, [y_.p[j], r_.b, fgb.b], [y_.p[j]])
            wr = [dbuf(ykey, bi)] if ykey else []
            P.dma("sp", y_dst[tok0:tok0 + 512, :].rearrange("(j p) c -> p j c", p=128), y_.t[:], y_.p, wr, y_.b)
        P.barrier()

    import os
    stop = int(os.environ.get("MK_STOP", "1000"))
    nph = [0]

    skip = set(x for x in os.environ.get("MK_SKIP", "").split(",") if x)

    def go(name=""):
        nph[0] += 1
        return nph[0] <= stop and name not in skip

    if go():
        setup()
    seqs = [(SS, xs_d, ys_d, 0, gts_bf, "gts"), (SP, xp_d, yp_d, 1, gtp_bf, "gtp")]
    if os.environ.get("MK_ORDER", "sp") == "ps":
        seqs = seqs[::-1]
    for (S, x_in, y_out, col, gt_bf, gtkey) in seqs:
        for l in range(DEPTH):
            x_src, xkey = (x_in, None) if l == 0 else (Y1, "Y1")
            final = (l == DEPTH - 1)
            if go("p1"):
                phase1(S, x_src, xkey, l, col)
            if go("lru"):
                lru(S, l)
            if go("fou"):
                fourier(S, l, gt_bf, gtkey)
            y_dst, ykey = (y_out, None) if final else (Y1, "Y1")
            if go("p3"):
                phase3(S, x_src, xkey, l, col, y_dst, ykey, final)
    P.emit()
    return nc, P


def _consts(SS, SP):
    c = np.arange(256)[:, None].astype(np.float64)
    m = np.arange(256)[None, :].astype(np.float64)
    th = 2 * np.pi * c * m / 256.0
    cs = np.concatenate([np.cos(th), -np.sin(th)], axis=1).astype(np.float32)
    s1 = np.arange(128)[:, None].astype(np.float64)
    k1 = np.arange(128)[None, :].astype(np.float64)
    th = 2 * np.pi * s1 * k1 / 128.0
    f3 = np.stack([np.cos(th), np.sin(th)], axis=1).astype(np.float32)

    def gtab(S):
        N2 = S // 128
        J = max(1, 128 // N2)
        NSTEP = 128 // J
        M = J * N2
        SBLK = min(16, NSTEP)
        nblk = NSTEP // SBLK
        out = np.zeros((nblk, M, SBLK, 3, M), np.float32)
        s2 = np.arange(N2, dtype=np.float64)[:, None]
        k2 = np.arange(N2, dtype=np.float64)[None, :]
        for pi in range(NSTEP):
            for j in range(J):
                s1_ = pi + NSTEP * j
                ph = (k2 * (s1_ + 128.0 * s2)) % S
                th_ = 2 * np.pi * ph / S
                fr, fi = np.cos(th_), -np.sin(th_)
                blk, sl = pi // SBLK, pi % SBLK
                out[blk, j:M:J, sl, 0, j * N2:(j + 1) * N2] = fr
                out[blk, j:M:J, sl, 1, j * N2:(j + 1) * N2] = fi
                out[blk, j:M:J, sl, 2, j * N2:(j + 1) * N2] = -fi
        return out

    return cs, f3, gtab(SS), gtab(SP)


def _pack_vec(norm_g, b_ada, conv_w, conv_b, b_rg, lam):
    def pm(a):
        a = np.asarray(a, np.float32)
        lead = int(np.prod(a.shape[:-1])) if a.ndim > 1 else 1
        nch = a.shape[-1] // 128
        return a.reshape(lead, nch, 128).transpose(2, 0, 1).reshape(128, lead * nch)
    vec = np.concatenate([pm(norm_g), pm(b_ada), pm(conv_w), pm(conv_b), pm(b_rg), pm(lam)], axis=1)
    assert vec.shape == (128, V_N), vec.shape
    return np.ascontiguousarray(vec)


_CACHE = {}


def run(x_prompt, x_sample, c_prompt, c_sample, norm_g, w_ada, b_ada, w_in, conv_w, conv_b,
        w_rg, b_rg, lam, w_a_out, w_b_out, w_o, final_g, ncores=8):
    SS, SP = x_sample.shape[1], x_prompt.shape[1]
    key = (SS, SP)
    if key not in _CACHE:
        _CACHE[key] = build(SS, SP)[0]
    nc = _CACHE[key]
    cs, f3, gts, gtp = _consts(SS, SP)
    vec = _pack_vec(norm_g, b_ada, conv_w, conv_b, b_rg, lam)
    fgb = np.ascontiguousarray(np.broadcast_to(np.asarray(final_g, np.float32)[None, :], (128, D)))
    ident = np.eye(128, dtype=np.float32)
    f32 = lambda a: np.ascontiguousarray(np.asarray(a, np.float32))
    common = {"vec": vec, "fgb": fgb, "w_ada": f32(w_ada), "w_in": f32(w_in), "w_rg": f32(w_rg),
              "w_a_out": f32(w_a_out), "w_b_out": f32(w_b_out), "w_o": f32(w_o), "ident": ident,
              "cs": cs, "f3": f3, "gts": gts, "gtp": gtp, "xp": f32(x_prompt[0])}
    cp_ = np.asarray(c_prompt, np.float32)[0].reshape(NCH, 128).T
    in_maps = []
    for i in range(ncores):
        csm = np.asarray(c_sample, np.float32)[i].reshape(NCH, 128).T
        cvv = np.ascontiguousarray(np.stack([csm, cp_], axis=2))
        m = dict(common)
        m["xs"] = f32(x_sample[i])
        m["cv"] = cvv
        in_maps.append(m)
    res = run_bass_kernel_spmd(nc, in_maps, core_ids=list(range(ncores)))
    y_sample = np.stack([np.asarray(res.results[i]["ys"], np.float32) for i in range(ncores)], axis=0)
    y_prompt = np.asarray(res.results[0]["yp"], np.float32)[None]
    return y_prompt, y_sample


def kernel(**inputs):
    return run(**inputs)
```

```python
import numpy as np
import concourse.bass as bass
import concourse.mybir as mybir
from concourse.bass_utils import run_bass_kernel_spmd

F32 = mybir.dt.float32
BF16 = mybir.dt.bfloat16
AF = mybir.ActivationFunctionType
ALU = mybir.AluOpType

D = 1024
DEPTH = 2
NCH = 8
EPS = 1e-6
EPOCH = 16000
DMA_EPOCH = 1000


class Buf:
    __slots__ = ("name", "w", "r", "slot", "excl")

    def __init__(self, name, excl=False):
        self.name = name
        self.w = None
        self.r = {}
        self.slot = None
        self.excl = excl


class Slot:
    __slots__ = ("sem", "count", "nobar")

    def __init__(self, sem):
        self.sem = sem
        self.count = 0
        self.nobar = False


class Prog:
    ENGS = ("pe", "act", "dve", "pool", "sp")

    def __init__(self, nc):
        self.nc = nc
        self.ops = {e: [] for e in self.ENGS}
        self.seq = {e: 0 for e in self.ENGS}
        self.esems = {e: [] for e in self.ENGS}
        self.known = {e: {} for e in self.ENGS}
        self.nsem = 0
        self.slots = []
        self.free = []
        self.ninst = 0

    def _newsem(self, name):
        self.nsem += 1
        return self.nc.alloc_semaphore(name)

    def _esem(self, e, k):
        ep = k // EPOCH
        while len(self.esems[e]) <= ep:
            self.esems[e].append(self._newsem(f"s_{e}_{len(self.esems[e])}"))
        return self.esems[e][ep], (k % EPOCH) + 1

    def _slot(self, b):
        if b.slot is None:
            if self.free:
                b.slot = self.free.pop()
            else:
                b.slot = Slot(self._newsem(f"d_{len(self.slots)}"))
                self.slots.append(b.slot)
        return b.slot

    def release(self, b):
        if b.slot is not None:
            if b.slot.count < 3800:
                self.free.append(b.slot)
            b.slot = None

    def _wait(self, e, ev):
        if ev is None:
            return
        if ev[0] == "eng":
            _, f, k = ev
            if f == e and e == "pe":
                return
            key = ("eng", f)
            if self.known[e].get(key, -1) >= k:
                return
            self.known[e][key] = k
            sem, val = self._esem(f, k)
        else:
            _, slot, cnt = ev
            key = ("dma", id(slot))
            if self.known[e].get(key, 0) >= cnt:
                return
            self.known[e][key] = cnt
            sem, val = slot.sem, 16 * cnt
        self.ninst += 1
        self.ops[e].append(lambda eng, sem=sem, val=val: eng.wait_ge(sem, val))

    def _deps(self, e, reads, writes):
        for b in reads:
            self._wait(e, b.w)
        for b in writes:
            self._wait(e, b.w)
            for ev in b.r.values():
                self._wait(e, ev)

    @staticmethod
    def _mark(ev, key, reads, writes):
        for b in reads:
            b.r[key] = ev
        for b in writes:
            b.w = ev
            b.r = {}

    def op(self, e, fn, reads=(), writes=(), inc=True):
        if any(b.excl for b in reads):
            writes = list(writes) + [b for b in reads if b.excl]
            reads = [b for b in reads if not b.excl]
        self._deps(e, reads, writes)
        k = self.seq[e]
        self.ninst += 1
        if inc:
            self.seq[e] += 1
            sem, _ = self._esem(e, k)
            self.ops[e].append(lambda eng, fn=fn, sem=sem: fn(eng).then_inc(sem, 1))
        else:
            self.ops[e].append(lambda eng, fn=fn: fn(eng))
        ev = ("eng", e, k)
        self._mark(ev, ("eng", e), reads, writes)
        return ev

    def dma(self, q, out, in_, reads, writes, sb, **kw):
        self._deps(q, reads, writes)
        slot = self._slot(sb)
        if slot.count > 0:
            self._wait(q, ("dma", slot, slot.count))
        slot.count += 1
        cnt = slot.count
        sem = slot.sem
        self.ninst += 1
        self.ops[q].append(
            lambda eng, out=out, in_=in_, sem=sem, kw=kw: eng.dma_start(out=out, in_=in_, **kw).then_inc(sem, 16))
        ev = ("dma", slot, cnt)
        self._mark(ev, ("dma", id(slot)), reads, writes)
        return ev

    def barrier(self, final=False):
        evs = []
        for f in self.ENGS:
            if self.seq[f] > 0:
                evs.append(("eng", f, self.seq[f] - 1))
        for sl in self.slots:
            if sl.count > 0 and (final or not sl.nobar):
                evs.append(("dma", sl, sl.count))
        for e in self.ENGS:
            for ev in evs:
                self._wait(e, ev)

    def emit(self):
        nc = self.nc
        self.barrier(final=True)
        with nc.Block() as block:
            @block.tensor
            def _(eng):
                for t in self.ops["pe"]:
                    t(eng)

            @block.scalar
            def _(eng):
                for t in self.ops["act"]:
                    t(eng)

            @block.vector
            def _(eng):
                for t in self.ops["dve"]:
                    t(eng)

            @block.gpsimd
            def _(eng):
                for t in self.ops["pool"]:
                    t(eng)

            @block.sync
            def _(eng):
                for t in self.ops["sp"]:
                    t(eng)


class Tl:
    def __init__(self, t, name, nparts=0):
        self.t = t
        self.b = Buf(name)
        self.p = [Buf(f"{name}_{i}") for i in range(nparts)]


V_NG = 0
V_BADA = 16
V_CW = 64
V_CB = 128
V_BRG = 144
V_LAM = 208
V_N = 240


def build(SS, SP):
    nc = bass.Bass("TRN2", target_bir_lowering=False)
    P = Prog(nc)
    SMAX = max(SS, SP)

    def din(name, shape, dt=F32):
        return nc.dram_tensor(name, list(shape), dt, kind="ExternalInput").ap()

    def dout(name, shape, dt=F32):
        return nc.dram_tensor(name, list(shape), dt, kind="ExternalOutput").ap()

    def dscr(name, shape, dt=BF16):
        return nc.dram_tensor(name, list(shape), dt).ap()

    xs_d = din("xs", [SS, D]); xp_d = din("xp", [SP, D])
    cv_d = din("cv", [128, NCH, 2]); vec_d = din("vec", [128, V_N]); fgb_d = din("fgb", [128, D])
    w_ada_d = din("w_ada", [DEPTH, D, 3 * D]); w_in_d = din("w_in", [DEPTH, D, 6 * D])
    w_rg_d = din("w_rg", [DEPTH, 2, 2, NCH, 128, 128])
    w_ao_d = din("w_a_out", [DEPTH, D, D]); w_bo_d = din("w_b_out", [DEPTH, D, D]); w_o_d = din("w_o", [DEPTH, D, D])
    ident_d = din("ident", [128, 128]); cs_d = din("cs", [256, 512]); f3_d = din("f3", [128, 2, 128])
    N2S, N2P = SS // 128, SP // 128
    def gshape(S_):
        N2_ = S_ // 128
        J_ = max(1, 128 // N2_)
        NST_ = 128 // J_
        SB_ = min(16, NST_)
        return [NST_ // SB_, J_ * N2_, SB_, 3, J_ * N2_]
    gts_d = din("gts", gshape(SS)); gtp_d = din("gtp", gshape(SP))
    ys_d = dout("ys", [SS, D]); yp_d = dout("yp", [SP, D])

    w_in_bf = dscr("w_in_bf", [DEPTH, 12, 128, NCH, 512])
    w_ao_bf = dscr("w_ao_bf", [DEPTH, 128, NCH, D]); w_bo_bf = dscr("w_bo_bf", [DEPTH, 128, NCH, D])
    w_o_bf = dscr("w_o_bf", [DEPTH, 128, NCH, D])
    w_rg_bf = dscr("w_rg_bf", [DEPTH, 128, 2, 2, NCH, 128])
    gts_bf = dscr("gts_bf", gshape(SS)); gtp_bf = dscr("gtp_bf", gshape(SP))
    XA = dscr("XA", [NCH, 128, SMAX]); SGA = dscr("SGA", [NCH, 128, SMAX])
    XB = dscr("XB", [NCH, 128, SMAX]); SGB = dscr("SGB", [NCH, 128, SMAX])
    SM = dscr("SM", [16, 128, SMAX]); AT = dscr("AT", [NCH, 128, SMAX]); BT = dscr("BT", [NCH, 128, SMAX])
    FB = dscr("FB", [4, SMAX // 128, 128, 512])
    Y1 = dscr("Y1", [SMAX, D], F32)
    DB = {}

    def dbuf(*key):
        if key not in DB:
            DB[key] = Buf("d" + "_".join(str(k) for k in key))
        return DB[key]

    class Arena:
        def __init__(self, base, limit):
            self.base, self.limit, self.cur, self.n, self.live = base, limit, base, 0, []

        def reset(self):
            self.cur = self.base
            for tl in self.live:
                P.release(tl.b)
                for pb_ in tl.p:
                    P.release(pb_)
            self.live = []

        def alloc(self, name, shape, dt, nparts=0):
            esz = 4 if dt == F32 else 2
            nbytes = int(np.prod(shape[1:])) * esz
            nbytes = (nbytes + 63) // 64 * 64
            assert self.cur + nbytes <= self.limit, (name, self.cur, nbytes, self.limit)
            self.n += 1
            t = nc.alloc_sbuf_tensor_at(f"{name}_{self.n}", list(shape), dt, offset=self.cur)
            self.cur += nbytes
            tl = Tl(t, f"{name}_{self.n}", nparts)
            self.live.append(tl)
            return tl

    PERS = Arena(16640, 33024)
    AR = Arena(33024, 229000)

    psum = nc.alloc_psum_tensor("psum_all", [128, 4096], F32)
    PB = [Buf(f"psb{i}", excl=True) for i in range(8)]

    def bank(i, n=512, p=128):
        return psum[0:p, i * 512:i * 512 + n]

    def act(out, in_, func, reads, writes, scale=1.0, bias=None, accum_out=None):
        kw = {"scale": scale}
        if bias is not None:
            kw["bias"] = bias
        if accum_out is not None:
            kw["accum_out"] = accum_out
        return P.op("act", lambda e: e.activation(out=out, in_=in_, func=func, **kw), reads, writes)

    def mm(out, lhsT, rhs, start, stop, reads, writes, inc):
        return P.op("pe", lambda e: e.matmul(out, lhsT=lhsT, rhs=rhs, start=start, stop=stop), reads, writes, inc=inc)

    def ts(eng, out, in0, s1, s2, op0, op1, reads, writes):
        return P.op(eng, lambda e: e.tensor_scalar(out=out, in0=in0, scalar1=s1, scalar2=s2, op0=op0, op1=op1), reads, writes)

    def tt(eng, out, in0, in1, op, reads, writes):
        return P.op(eng, lambda e: e.tensor_tensor(out=out, in0=in0, in1=in1, op=op), reads, writes)

    def stt(out, in0, scalar, in1, op0, op1, reads, writes):
        return P.op("dve", lambda e: e.scalar_tensor_tensor(out=out, in0=in0, scalar=scalar, in1=in1, op0=op0, op1=op1), reads, writes)

    def cp(eng, out, in_, reads, writes):
        return P.op(eng, lambda e: e.tensor_copy(out=out, in_=in_), reads, writes)

    def ld(out, in_, reads, tl, q="sp", wr=None):
        return P.dma(q, out, in_, reads, wr if wr is not None else [tl.b], tl.b)

    def st(out, in_, rd, tl, dwrites, q="sp"):
        return P.dma(q, out, in_, rd, dwrites, tl.b)

    ident_f = PERS.alloc("ident_f", [128, 128], F32)
    ident_b = PERS.alloc("ident_b", [128, 128], BF16)
    ones_f = PERS.alloc("ones_f", [128, 128], F32)
    vec = PERS.alloc("vec", [128, V_N], F32)
    cv = PERS.alloc("cv", [128, NCH, 2], F32)
    scv = PERS.alloc("scv", [128, NCH, 2], F32)
    hbrg = PERS.alloc("hbrg", [128, 64], F32)
    kk = PERS.alloc("kk", [128, 32], F32)
    kh = PERS.alloc("kh", [128, 32], F32)
    mod = PERS.alloc("mod", [128, DEPTH, 24, 2], F32)
    gm = PERS.alloc("gm", [128, DEPTH, NCH, 2], F32)
    one1 = PERS.alloc("one1", [128, 1], F32)
    half1 = PERS.alloc("half1", [128, 1], F32)
    cs_b = PERS.alloc("cs_b", [128, 2, 512], BF16)
    f3_b = PERS.alloc("f3_b", [128, 2, 128], BF16)
    fgb = PERS.alloc("fgb", [128, D], F32)
    tmpv = [PERS.alloc(f"tmpv{i}", [128, 32], F32) for i in range(4)]

    def setup():
        ld(ident_f.t[:], ident_d, [], ident_f)
        ld(vec.t[:], vec_d, [], vec)
        ld(cv.t[:], cv_d, [], cv)
        ld(fgb.t[:], fgb_d, [], fgb)
        ld(ident_b.t[:], ident_d, [], ident_b, q="pool")
        ld(cs_b.t[:], cs_d.rearrange("(cc p) n -> p cc n", p=128), [], cs_b, q="pool")
        ld(f3_b.t[:], f3_d, [], f3_b, q="pool")
        P.op("dve", lambda e: e.memset(ones_f.t[:], 1.0), [], [ones_f.b])
        P.op("dve", lambda e: e.memset(one1.t[:], 1.0), [], [one1.b])
        P.op("dve", lambda e: e.memset(half1.t[:], 0.5), [], [half1.b])
        wsems = [Buf(f"wconv{i}") for i in range(8)]
        wn = [0]

        class _W:
            pass

        def wsem_next():
            wn[0] += 1
            b_ = wsems[wn[0] % 8]
            P._slot(b_).nobar = True
            return b_
        for l in range(DEPTH):
            for g in range(12):
                src = w_in_d[l, :, g * 512:(g + 1) * 512].rearrange("(ic p) n -> p ic n", p=128)
                P.dma("pool", w_in_bf[l, g], src, [], [dbuf("w_in", l, g)], wsem_next())
            for nm, sd, dd in (("w_ao", w_ao_d, w_ao_bf), ("w_bo", w_bo_d, w_bo_bf), ("w_o", w_o_d, w_o_bf)):
                for h in range(2):
                    src = sd[l, :, h * 512:(h + 1) * 512].rearrange("(ic p) n -> p ic n", p=128)
                    P.dma("pool", dd[l, :, :, h * 512:(h + 1) * 512], src, [], [dbuf(nm, l, h)], wsem_next())
            for d in range(2):
                for g in range(2):
                    src = w_rg_d[l, d, g].rearrange("h i j -> i h j")
                    P.dma("pool", w_rg_bf[l, :, d, g], src, [], [dbuf("w_rg", l, d, g)], wsem_next())
        for blk in range(gshape(SS)[0]):
            P.dma("pool", gts_bf[blk], gts_d[blk], [], [dbuf("gts", blk)], wsem_next())
        for blk in range(gshape(SP)[0]):
            P.dma("pool", gtp_bf[blk], gtp_d[blk], [], [dbuf("gtp", blk)], wsem_next())

        v = vec.t
        ts("dve", hbrg.t[:], v[:, V_BRG:V_BRG + 64], 0.5, None, ALU.mult, ALU.bypass, [vec.b], [hbrg.b])
        e_, z_, z2_, acc_ = tmpv
        act(e_.t[:], v[:, V_LAM:V_LAM + 32], AF.Exp, [vec.b], [e_.b], scale=-1.0)
        ts("dve", z_.t[:], e_.t[:], 2.0, None, ALU.add, ALU.bypass, [e_.b], [z_.b])
        P.op("dve", lambda e: e.reciprocal(out=z_.t[:], in_=z_.t[:]), [z_.b], [z_.b])
        tt("dve", z_.t[:], z_.t[:], e_.t[:], ALU.mult, [z_.b, e_.b], [z_.b])
        tt("dve", z2_.t[:], z_.t[:], z_.t[:], ALU.mult, [z_.b], [z2_.b])
        ts("dve", acc_.t[:], z2_.t[:], 1.0 / 11.0, 1.0 / 9.0, ALU.mult, ALU.add, [z2_.b], [acc_.b])
        for cst in (1.0 / 7.0, 1.0 / 5.0, 1.0 / 3.0, 1.0):
            tt("dve", acc_.t[:], acc_.t[:], z2_.t[:], ALU.mult, [acc_.b, z2_.b], [acc_.b])
            ts("dve", acc_.t[:], acc_.t[:], cst, None, ALU.add, ALU.bypass, [acc_.b], [acc_.b])
        tt("dve", acc_.t[:], acc_.t[:], z_.t[:], ALU.mult, [acc_.b, z_.b], [acc_.b])
        ts("dve", kk.t[:], acc_.t[:], -16.0, None, ALU.mult, ALU.bypass, [acc_.b], [kk.b])
        ts("dve", kh.t[:], acc_.t[:], -8.0, None, ALU.mult, ALU.bypass, [acc_.b], [kh.b])

        act(scv.t[:], cv.t[:], AF.Silu, [cv.b], [scv.b])
        AR.reset()
        wada = [AR.alloc("wada", [128, NCH, D], F32) for _ in range(2)]
        n = 0
        for l in range(DEPTH):
            for third in range(3):
                wt = wada[n % 2]; n += 1
                for h in range(2):
                    ld(wt.t[:, :, h * 512:(h + 1) * 512],
                       w_ada_d[l, :, third * D + h * 512: third * D + (h + 1) * 512].rearrange("(ic p) n -> p ic n", p=128),
                       [], wt)
                for jj in range(8):
                    j = third * 8 + jj
                    for ic in range(NCH):
                        mm(psum[:, 2 * j:2 * j + 2], wt.t[:, ic, jj * 128:(jj + 1) * 128], scv.t[:, ic, :],
                           ic == 0, ic == NCH - 1, [wt.b, scv.b], [PB[0]], inc=(ic == NCH - 1))
            for j in range(24):
                ts("dve", mod.t[:, l, j, :], psum[:, 2 * j:2 * j + 2], v[:, V_BADA + l * 24 + j:V_BADA + l * 24 + j + 1], None,
                   ALU.add, ALU.bypass, [PB[0], vec.b], [mod.b])
            for c in range(NCH):
                ts("dve", gm.t[:, l, c, :], mod.t[:, l, 8 + c, :], 1.0, v[:, V_NG + l * 8 + c:V_NG + l * 8 + c + 1],
                   ALU.add, ALU.mult, [mod.b, vec.b], [gm.b])
        P.barrier()

    def phase1(S, x_src, xkey, l, col):
        AR.reset()
        SBK = min(2048, S)
        NB = SBK // 512
        nsb = S // SBK
        xt = [AR.alloc("xt", [128, 4, D], F32) for _ in range(2)]
        xn = [AR.alloc("xn", [128, 4, D], F32, nparts=4) for _ in range(2)]
        hT = [AR.alloc("hT", [128, NCH, SBK], BF16, nparts=NCH * NB) for _ in range(2)]
        win = [AR.alloc("win", [128, NCH, 512], BF16) for _ in range(3)]
        stg = [AR.alloc("stg", [128, SBK], BF16, nparts=NB) for _ in range(6)]
        junk = AR.alloc("junk", [128, D], BF16)
        ss = [AR.alloc("ss", [128, 4], F32) for _ in range(2)]
        rs = [AR.alloc("rs", [128, 4], F32) for _ in range(2)]
        v = vec.t
        cnt = {"blk": 0, "tb": 0, "pb": 0, "w": 0, "stg": 0}

        def prep_nonpe(sb, blk):
            i = cnt["blk"]
            cnt["blk"] += 1
            x_, n_, s_, r_ = xt[i % 2], xn[i % 2], ss[i % 2], rs[i % 2]
            tok0 = sb * SBK + blk * 512
            rd = [dbuf(xkey, tok0 // 512)] if xkey else []
            ld(x_.t[:], x_src[tok0:tok0 + 512, :].rearrange("(j p) c -> p j c", p=128), rd, x_)
            for j in range(4):
                act(junk.t[:], x_.t[:, j, :], AF.Square, [x_.b], [junk.b, s_.b], accum_out=s_.t[:, j:j + 1])
            ts("dve", r_.t[:], s_.t[:], 1.0 / D, EPS, ALU.mult, ALU.add, [s_.b], [r_.b])
            act(r_.t[:], r_.t[:], AF.Sqrt, [r_.b], [r_.b])
            P.op("dve", lambda e: e.reciprocal(out=r_.t[:], in_=r_.t[:]), [r_.b], [r_.b])
            for j in range(4):
                ts("pool", n_.t[:, j, :], x_.t[:, j, :], r_.t[:, j:j + 1], 1.0, ALU.mult, ALU.mult, [x_.b, r_.b], [n_.p[j]])
            return n_

        def prep_pe(sb, blk, n_):
            h_ = hT[sb % 2]
            for c in range(NCH):
                bk = cnt["tb"] % 2
                cnt["tb"] += 1
                for j in range(4):
                    P.op("pe", lambda e, j=j, c=c, bk=bk: e.transpose(psum[:, bk * 512 + j * 128: bk * 512 + (j + 1) * 128],
                                                                       n_.t[:, j, c * 128:(c + 1) * 128], ident_f.t[:]),
                         [n_.p[j], ident_f.b], [PB[bk]], inc=(j == 3))
                act(h_.t[:, c, blk * 512:(blk + 1) * 512], bank(bk), AF.Identity, [PB[bk], gm.b, mod.b], [h_.p[c * NB + blk]],
                    scale=gm.t[:, l, c, col:col + 1], bias=mod.t[:, l, c, col:col + 1])

        def jc_info(jc):
            if jc < 8:
                return "copy", XA, "XA", jc
            if jc < 16:
                return "silu", SGA, "SGA", jc - 8
            if jc < 24:
                return "copy", XB, "XB", jc - 16
            if jc < 32:
                return "silu", SGB, "SGB", jc - 24
            return "sigm", SM, "SM", jc - 32

        def load_w(idx):
            if idx < nsb * 12:
                g_ = idx % 12
                w__ = win[idx % 3]
                ld(w__.t[:], w_in_bf[l, g_], [dbuf("w_in", l, g_)], w__)

        def proj_group(sb, g):
            h_ = hT[sb % 2]
            idx = sb * 12 + g
            w_ = win[idx % 3]
            if idx == 0:
                load_w(0)
                load_w(1)
            load_w(idx + 2)
            for jcl in range(4):
                jc = g * 4 + jcl
                kind, dten, dname, ch = jc_info(jc)
                s_ = stg[cnt["stg"] % 6]
                cnt["stg"] += 1
                for tile in range(NB):
                    bk = 2 + cnt["pb"] % 6
                    cnt["pb"] += 1
                    for ic in range(NCH):
                        mm(bank(bk), w_.t[:, ic, jcl * 128:(jcl + 1) * 128], h_.t[:, ic, tile * 512:(tile + 1) * 512],
                           ic == 0, ic == NCH - 1, [w_.b, h_.p[ic * NB + tile]], [PB[bk]], inc=(ic == NCH - 1))
                    o = s_.t[:, tile * 512:(tile + 1) * 512]
                    if kind == "copy":
                        cp("dve", o, bank(bk), [PB[bk]], [s_.p[tile]])
                    elif kind == "silu":
                        act(o, bank(bk), AF.Silu, [PB[bk]], [s_.p[tile]])
                    else:
                        act(o, bank(bk), AF.Sigmoid, [PB[bk]], [s_.p[tile]])
                P.dma("sp", dten[ch, :, sb * SBK:(sb + 1) * SBK], s_.t[:], s_.p, [dbuf(dname, ch, sb)], s_.b)

        for blk in range(NB):
            n_ = prep_nonpe(0, blk)
            prep_pe(0, blk, n_)
        gper = 12 // NB
        for sb in range(nsb):
            cur = None
            for g in range(12):
                if sb + 1 < nsb and g % gper == 0:
                    cur = (g // gper, prep_nonpe(sb + 1, g // gper))
                proj_group(sb, g)
                if sb + 1 < nsb and g % gper == gper - 1:
                    prep_pe(sb + 1, cur[0], cur[1])
        P.barrier()

    def lru(S, l):
        AR.reset()
        SL = min(1024, S)
        NT = SL // 512
        nsl = S // SL
        SBK = min(2048, S)
        hb = AR.alloc("hb", [128, S], F32, nparts=nsl)
        wrg = AR.alloc("wrg", [128, 2, 2, NCH, 128], BF16)
        dk = [AR.alloc("dk", [128, 4, 128], BF16) for _ in range(2)]
        xap = [AR.alloc("xap", [128, SL + 4], BF16) for _ in range(3)]
        ring = lambda nm, dt: [AR.alloc(nm, [128, SL], dt) for _ in range(2)]
        xcb = [AR.alloc("xcb", [128, SL], BF16) for _ in range(3)]
        tr, ti, ip, bb = (ring(n, F32) for n in ("tr", "ti", "ip", "bb"))
        ring3 = lambda nm: [AR.alloc(nm, [128, SL], F32) for _ in range(3)]
        aa, m_, u_, hh = ring3("aa"), ring3("m"), ring3("u"), ring3("hh")
        s_ = ring("s", F32)
        sga = [AR.alloc("sga", [128, SL], BF16) for _ in range(6)]
        sA = ring("sA", BF16)
        v = vec.t
        for d in range(2):
            for g in range(2):
                ld(wrg.t[:, d, g], w_rg_bf[l, :, d, g], [dbuf("w_rg", l, d, g)], wrg)
        slabs = []
        for hd in range(NCH):
            for d in (1, 0):
                order = range(nsl - 1, -1, -1) if d == 1 else range(nsl)
                for sl in order:
                    slabs.append((hd, d, sl))
        n = len(slabs)
        st_ = {"prev_h": None}
        fidx = []
        nf = 0
        for (_h, _d, _s) in slabs:
            fidx.append(nf)
            if _d == 0:
                nf += 1

        def ldx(i):
            if i >= n:
                return
            hd, d, sl = slabs[i]
            x_ = xap[i % 3]
            t0 = sl * SL
            lo, hi = max(0, t0 - 2), min(S, t0 + SL + 2)
            off = lo - (t0 - 2)
            if t0 == 0:
                P.op("pool", lambda e, x_=x_: e.memset(x_.t[:, 0:2], 0.0), [], [x_.b])
            if t0 + SL == S:
                P.op("pool", lambda e, x_=x_: e.memset(x_.t[:, SL + 2:SL + 4], 0.0), [], [x_.b])
            rd = [dbuf("XA", hd, sb) for sb in range(lo // SBK, (hi - 1) // SBK + 1)]
            ld(x_.t[:, off:off + (hi - lo)], XA[hd, :, lo:hi], rd, x_)

        def stage_a(i):
            hd, d, sl = slabs[i]
            dk_ = dk[hd % 2]
            if d == 1 and sl == nsl - 1:
                for k in range(4):
                    col = V_CW + (l * 4 + k) * 8 + hd
                    ts("dve", dk_.t[:, k, :], ident_b.t[:], v[:, col:col + 1], None, ALU.mult, ALU.bypass,
                       [ident_b.b, vec.b], [dk_.b])
            ldx(i + 2)
            x_ = xap[i % 3]
            c0 = (i % 2) * 2
            for tile in range(NT):
                for k in range(4):
                    mm(bank(c0 + tile), dk_.t[:, k, :], x_.t[:, tile * 512 + k: tile * 512 + k + 512],
                       k == 0, k == 3, [dk_.b, x_.b], [PB[c0 + tile]], inc=(k == 3))
            cps = psum[:, c0 * 512:c0 * 512 + SL]
            cpb = [PB[c0 + t_] for t_ in range(NT)]
            cbcol = v[:, V_CB + l * 8 + hd:V_CB + l * 8 + hd + 1]
            xcb_ = xcb[i % 3]
            ts("dve", xcb_.t[:], cps, cbcol, None, ALU.add, ALU.bypass, cpb + [vec.b], [xcb_.b])
            if d == 0:
                g_ = sga[fidx[i] % 6]
                t0 = sl * SL
                ld(g_.t[:], SGA[hd, :, t0:t0 + SL], [dbuf("SGA", hd, t0 // SBK)], g_)

        def stage_b0(i):
            hd, d, sl = slabs[i]
            kcol = (l * 2 + d) * 8 + hd
            brc = hbrg.t[:, ((l * 2 + d) * 2 + 0) * 8 + hd:((l * 2 + d) * 2 + 0) * 8 + hd + 1]
            bic = hbrg.t[:, ((l * 2 + d) * 2 + 1) * 8 + hd:((l * 2 + d) * 2 + 1) * 8 + hd + 1]
            xcb_ = xcb[i % 3]
            tr_, ti_, ip_ = tr[i % 2], ti[i % 2], ip[i % 2]
            a_, mm_, uu_ = aa[i % 3], m_[i % 3], u_[i % 3]
            for tile in range(NT):
                mm(bank(4 + tile), wrg.t[:, d, 0, hd, :], xcb_.t[:, tile * 512:(tile + 1) * 512], True, True,
                   [wrg.b, xcb_.b], [PB[4 + tile]], inc=True)
                mm(bank(6 + tile), wrg.t[:, d, 1, hd, :], xcb_.t[:, tile * 512:(tile + 1) * 512], True, True,
                   [wrg.b, xcb_.b], [PB[6 + tile]], inc=True)
            gr = psum[:, 4 * 512:4 * 512 + SL]
            gi = psum[:, 6 * 512:6 * 512 + SL]
            grb = [PB[4 + t_] for t_ in range(NT)]
            gib = [PB[6 + t_] for t_ in range(NT)]
            act(tr_.t[:], gr, AF.Tanh, grb + [hbrg.b], [tr_.b], scale=0.5, bias=brc)
            act(ti_.t[:], gi, AF.Tanh, gib + [hbrg.b], [ti_.b], scale=0.5, bias=bic)
            act(a_.t[:], tr_.t[:], AF.Exp, [tr_.b, kh.b], [a_.b], scale=kh.t[:, kcol:kcol + 1], bias=kh.t[:, kcol:kcol + 1])
            act(mm_.t[:], tr_.t[:], AF.Exp, [tr_.b, kk.b], [mm_.b], scale=kk.t[:, kcol:kcol + 1], bias=kk.t[:, kcol:kcol + 1])
            ts("pool", ip_.t[:], ti_.t[:], 1.0, 1.0, ALU.add, ALU.mult, [ti_.b], [ip_.b])
            tt("pool", uu_.t[:], ip_.t[:], xcb_.t[:], ALU.mult, [ip_.b, xcb_.b], [uu_.b])

        def stage_b(i):
            hd, d, sl = slabs[i]
            t0 = sl * SL
            a_, mm_, uu_, b_ = aa[i % 3], m_[i % 3], u_[i % 3], bb[i % 2]
            act(mm_.t[:], mm_.t[:], AF.Sqrt, [mm_.b, one1.b], [mm_.b], scale=-1.0, bias=one1.t[:, 0:1])
            if d == 1 and sl == nsl - 1:
                P.op("pool", lambda e, mm_=mm_: e.memset(mm_.t[:, SL - 1:SL], 1.0), [], [mm_.b])
            if d == 0 and sl == 0:
                P.op("pool", lambda e, mm_=mm_: e.memset(mm_.t[:, 0:1], 1.0), [], [mm_.b])
            stt(b_.t[:], uu_.t[:], 0.5, mm_.t[:], ALU.mult, ALU.mult, [uu_.b, mm_.b], [b_.b])
            if d == 1:
                if sl == nsl - 1:
                    init, ird = 0.0, []
                else:
                    init, ird = hb.t[:, t0 + SL:t0 + SL + 1], [hb.p[sl + 1]]
                P.op("dve", lambda e, t0=t0, a_=a_, b_=b_, init=init: e.tensor_tensor_scan(
                    out=hb.t[:, t0:t0 + SL][:, ::-1], data0=a_.t[:, ::-1], data1=b_.t[:, ::-1], initial=init,
                    op0=ALU.mult, op1=ALU.add), [a_.b, b_.b] + ird, [hb.p[sl]])
            else:
                f = fidx[i]
                h_ = hh[f % 3]
                if sl == 0:
                    init, ird = 0.0, []
                else:
                    prev_h = st_["prev_h"]
                    init, ird = prev_h.t[:, SL - 1:SL], [prev_h.b]
                P.op("dve", lambda e, h_=h_, a_=a_, b_=b_, init=init: e.tensor_tensor_scan(
                    out=h_.t[:], data0=a_.t[:], data1=b_.t[:], initial=init, op0=ALU.mult, op1=ALU.add),
                    [a_.b, b_.b] + ird, [h_.b])
                st_["prev_h"] = h_

        def stage_b2(i):
            hd, d, sl = slabs[i]
            if d != 0:
                return
            t0 = sl * SL
            f = fidx[i]
            h_ = hh[f % 3]
            g_ = sga[f % 6]
            sum_ = s_[f % 2]
            o_ = sA[f % 2]
            tt("pool", sum_.t[:], h_.t[:], hb.t[:, t0:t0 + SL], ALU.add, [h_.b, hb.p[sl]], [sum_.b])
            tt("dve", o_.t[:], sum_.t[:], g_.t[:], ALU.mult, [sum_.b, g_.b], [o_.b])
            st(AT[hd, :, t0:t0 + SL], o_.t[:], [o_.b], o_, [dbuf("AT", hd, sl)])

        ldx(0)
        ldx(1)
        na = [0]

        def ensure_a(upto):
            while na[0] <= upto and na[0] < n:
                stage_a(na[0])
                na[0] += 1

        pend = []

        def flush():
            while pend:
                stage_b2(pend.pop(0))

        i = 0
        while i < n:
            grp = [i, i + 1] if i + 1 < n else [i]
            for j in grp:
                ensure_a(j + 2)
                stage_b0(j)
            for j in grp:
                if slabs[j][1] == 1:
                    flush()
                stage_b(j)
                flush()
                if slabs[j][1] == 0:
                    pend.append(j)
            i += len(grp)
        flush()
        P.barrier()

    def fourier(S, l, gt_bf, gtkey):
        AR.reset()
        N2 = S // 128
        J = max(1, 128 // N2)
        NSTEP = 128 // J
        M = J * N2
        SBLK = min(16, NSTEP)
        SBK = min(2048, S)
        invn = 1.0 / float(np.sqrt(S * 256.0))
        xbg = [AR.alloc("xbg", [128, 2, S], BF16) for _ in range(2)]
        gt = [AR.alloc("gt", [M, SBLK, 3, M], BF16) for _ in range(2)]
        wsb = [AR.alloc("wsb", [M, 512], BF16) for _ in range(3)]
        bst = [AR.alloc("bst", [M, 4, 512], BF16, nparts=4) for _ in range(3)]
        steps = [(g, pi) for g in range(4) for pi in range(NSTEP)]
        n = len(steps)

        def st_a(i):
            g, pi = steps[i]
            x_ = xbg[g % 2]
            if pi == 0:
                for cc in range(2):
                    ld(x_.t[:, cc, :], XB[2 * g + cc, :, 0:S], [dbuf("XB", 2 * g + cc, sb) for sb in range(S // SBK)], x_)
            if pi % SBLK == 0:
                gslot = gt[(i // SBLK) % 2]
                ld(gslot.t[:], gt_bf[pi // SBLK], [dbuf(gtkey, pi // SBLK)], gslot)
            wb = i % 2
            for cc in range(2):
                lhs = x_.t[:, cc, pi:S:NSTEP]
                mm(bank(wb, 512, M), lhs, cs_b.t[:, cc, :], cc == 0, cc == 1, [x_.b, cs_b.b], [PB[wb]], inc=(cc == 1))
            w_ = wsb[i % 3]
            if i % 2 == 0:
                act(w_.t[:], bank(wb, 512, M), AF.Identity, [PB[wb]], [w_.b])
            else:
                cp("dve", w_.t[:], bank(wb, 512, M), [PB[wb]], [w_.b])

        def st_b(i):
            g, pi = steps[i]
            gslot = gt[(i // SBLK) % 2]
            w_ = wsb[i % 3]
            bbk = 2 + i % 2
            sl16 = pi % SBLK
            Fr, Fi, Fin = gslot.t[:, sl16, 0, :], gslot.t[:, sl16, 1, :], gslot.t[:, sl16, 2, :]
            o_r = psum[0:M, bbk * 512: bbk * 512 + 256]
            o_i = psum[0:M, bbk * 512 + 256: bbk * 512 + 512]
            rd = [gslot.b, w_.b]
            mm(o_r, Fr, w_.t[:, 0:256], True, False, rd, [PB[bbk]], inc=False)
            mm(o_r, Fin, w_.t[:, 256:512], False, True, rd, [PB[bbk]], inc=False)
            mm(o_i, Fr, w_.t[:, 256:512], True, False, rd, [PB[bbk]], inc=False)
            mm(o_i, Fi, w_.t[:, 0:256], False, True, rd, [PB[bbk]], inc=True)
            b_ = bst[(i // 4) % 3]
            if i % 2 == 0:
                cp("dve", b_.t[:, pi % 4, :], bank(bbk, 512, M), [PB[bbk]], [b_.p[pi % 4]])
            else:
                act(b_.t[:, pi % 4, :], bank(bbk, 512, M), AF.Identity, [PB[bbk]], [b_.p[pi % 4]])
            if pi % 4 == 3:
                for j in range(J):
                    P.dma("sp", FB[g, 0:N2, j * NSTEP + pi - 3: j * NSTEP + pi + 1, :], b_.t[j * N2:(j + 1) * N2, :, :],
                          b_.p, [dbuf("FB", g, j, pi // 4)], b_.b)

        st_a(0)
        for i in range(n):
            if i + 1 < n:
                st_a(i + 1)
            st_b(i)
        P.barrier()
        AR.reset()
        sgb = AR.alloc("sgb", [128, 2, S], BF16)
        ybt = AR.alloc("ybt", [128, 2, S], BF16, nparts=2 * (N2 // 4))
        bin_ = [AR.alloc("bin", [128, 4, 512], BF16) for _ in range(3)]
        qi = 0
        for g in range(4):
            for mc in range(2):
                ld(sgb.t[:, mc, :], SGB[2 * g + mc, :, 0:S], [dbuf("SGB", 2 * g + mc, sb) for sb in range(S // SBK)], sgb)
            for q in range(N2 // 4):
                b_ = bin_[qi % 3]
                ld(b_.t[:], FB[g, 4 * q:4 * q + 4, :, :].rearrange("k s n -> s k n"), [dbuf("FB", g, j, i) for j in range(J) for i in range(NSTEP // 4)], b_)
                for mc in range(2):
                    bk = 4 + 2 * mc + (qi % 2)
                    for j in range(4):
                        o = psum[:, bk * 512 + j * 128: bk * 512 + (j + 1) * 128]
                        mm(o, b_.t[:, j, mc * 128:(mc + 1) * 128], f3_b.t[:, 0, :], True, False, [b_.b, f3_b.b], [PB[bk]], inc=False)
                        mm(o, b_.t[:, j, 256 + mc * 128:256 + (mc + 1) * 128], f3_b.t[:, 1, :], False, True, [b_.b, f3_b.b], [PB[bk]],
                           inc=(j == 3))
                    pv = bank(bk).rearrange("p (j k) -> p k j", j=4)
                    ov = ybt.t[:, mc, :].rearrange("p (k n) -> p k n", n=N2)[:, :, 4 * q:4 * q + 4]
                    gv = sgb.t[:, mc, :].rearrange("p (k n) -> p k n", n=N2)[:, :, 4 * q:4 * q + 4]
                    stt(ov, pv, invn, gv, ALU.mult, ALU.mult, [PB[bk], sgb.b], [ybt.p[mc * (N2 // 4) + q]])
                qi += 1
            for mc in range(2):
                st(BT[2 * g + mc, :, 0:S], ybt.t[:, mc, :], ybt.p[mc * (N2 // 4):(mc + 1) * (N2 // 4)], ybt, [dbuf("BT", 2 * g + mc)])
        P.barrier()

    def phase3(S, x_src, xkey, l, col, y_dst, ykey, final):
        AR.reset()
        SBK = min(2048, S)
        SL = min(1024, S)
        nblk = S // 512
        wa = AR.alloc("wa", [128, NCH, D], BF16)
        wb = AR.alloc("wb", [128, NCH, D], BF16)
        wog = AR.alloc("wog", [128, NCH, D], BF16)
        dg = AR.alloc("dg", [128, 128], F32)
        at = [AR.alloc("at", [128, NCH, 512], BF16) for _ in range(2)]
        bt = [AR.alloc("bt", [128, NCH, 512], BF16) for _ in range(2)]
        sm = [AR.alloc("sm", [128, 16, 512], BF16) for _ in range(2)]
        xr = [AR.alloc("xr", [128, 4, D], F32, nparts=4) for _ in range(2)]
        uT = [AR.alloc("uT", [128, NCH, 512], BF16, nparts=NCH) for _ in range(2)]
        ua = [AR.alloc("ua", [128, 512], F32) for _ in range(2)]
        ub = [AR.alloc("ub", [128, 512], F32) for _ in range(2)]
        junk = AR.alloc("junk3", [128, D], BF16)
        ss = [AR.alloc("ss3", [128, 4], F32) for _ in range(2)]
        rs = [AR.alloc("rs3", [128, 4], F32) for _ in range(2)]
        for h in range(2):
            ld(wa.t[:, :, h * 512:(h + 1) * 512], w_ao_bf[l, :, :, h * 512:(h + 1) * 512], [dbuf("w_ao", l, h)], wa)
            ld(wb.t[:, :, h * 512:(h + 1) * 512], w_bo_bf[l, :, :, h * 512:(h + 1) * 512], [dbuf("w_bo", l, h)], wb)
            ld(wog.t[:, :, h * 512:(h + 1) * 512], w_o_bf[l, :, :, h * 512:(h + 1) * 512], [dbuf("w_o", l, h)], wog)
        for c in range(NCH):
            ts("dve", dg.t[:], ident_f.t[:], mod.t[:, l, 16 + c, col:col + 1], None, ALU.mult, ALU.bypass,
               [ident_f.b, mod.b], [dg.b])
            mm(psum[:, c * 128:(c + 1) * 128], ones_f.t[:], dg.t[:], True, True, [ones_f.b, dg.b], [PB[c // 4]], inc=True)
        for ic in range(NCH):
            tt("dve", wog.t[:, ic, :], wog.t[:, ic, :], psum[:, 0:D], ALU.mult, [wog.b, PB[0], PB[1]], [wog.b])

        def loads(bi):
            tok0 = bi * 512
            a_, b_, m_, x_ = at[bi % 2], bt[bi % 2], sm[bi % 2], xr[bi % 2]
            ld(a_.t[:], AT[:, :, tok0:tok0 + 512].rearrange("c p t -> p c t"), [dbuf("AT", hd, tok0 // SL) for hd in range(NCH)], a_)
            ld(b_.t[:], BT[:, :, tok0:tok0 + 512].rearrange("c p t -> p c t"), [dbuf("BT", c) for c in range(NCH)], b_)
            for h in range(2):
                ld(m_.t[:, h * 8:(h + 1) * 8, :], SM[h * 8:(h + 1) * 8, :, tok0:tok0 + 512].rearrange("c p t -> p c t"),
                   [dbuf("SM", c, tok0 // SBK) for c in range(h * 8, (h + 1) * 8)], m_)
            rd = [dbuf(xkey, bi)] if xkey else []
            ld(x_.t[:], x_src[tok0:tok0 + 512, :].rearrange("(j p) c -> p j c", p=128), rd, x_, wr=[x_.b] + x_.p)

        loads(0)
        pb = 0
        for bi in range(nblk):
            if bi + 1 < nblk:
                loads(bi + 1)
            a_, b_, m_, x_ = at[bi % 2], bt[bi % 2], sm[bi % 2], xr[bi % 2]
            u_ = uT[bi % 2]
            y_ = x_
            tok0 = bi * 512
            for jc in range(NCH):
                bka = pb % 8; pb += 1
                bkb = pb % 8; pb += 1
                for ic in range(NCH):
                    mm(bank(bka), wa.t[:, ic, jc * 128:(jc + 1) * 128], a_.t[:, ic, :], ic == 0, ic == NCH - 1,
                       [wa.b, a_.b], [PB[bka]], inc=(ic == NCH - 1))
                for ic in range(NCH):
                    mm(bank(bkb), wb.t[:, ic, jc * 128:(jc + 1) * 128], b_.t[:, ic, :], ic == 0, ic == NCH - 1,
                       [wb.b, b_.b], [PB[bkb]], inc=(ic == NCH - 1))
                ua_, ub_ = ua[jc % 2], ub[jc % 2]
                tt("dve", ua_.t[:], bank(bka), m_.t[:, jc, :], ALU.mult, [PB[bka], m_.b], [ua_.b])
                tt("dve", ub_.t[:], bank(bkb), m_.t[:, 8 + jc, :], ALU.mult, [PB[bkb], m_.b], [ub_.b])
                tt("pool", u_.t[:, jc, :], ua_.t[:], ub_.t[:], ALU.add, [ua_.b, ub_.b], [u_.p[jc]])
            for tile in range(4):
                for half in range(2):
                    bk = pb % 8; pb += 1
                    for ic in range(NCH):
                        mm(bank(bk), u_.t[:, ic, tile * 128:(tile + 1) * 128], wog.t[:, ic, half * 512:(half + 1) * 512],
                           ic == 0, ic == NCH - 1, [u_.p[ic], wog.b], [PB[bk]], inc=(ic == NCH - 1))
                    tt("dve", y_.t[:, tile, half * 512:(half + 1) * 512], bank(bk), x_.t[:, tile, half * 512:(half + 1) * 512],
                       ALU.add, [PB[bk], x_.b], [y_.p[tile]])
            if final:
                s_, r_ = ss[bi % 2], rs[bi % 2]
                for j in range(4):
                    act(junk.t[:], y_.t[:, j, :], AF.Square, [y_.p[j]], [junk.b, s_.b], accum_out=s_.t[:, j:j + 1])
                ts("dve", r_.t[:], s_.t[:], 1.0 / D, EPS, ALU.mult, ALU.add, [s_.b], [r_.b])
                act(r_.t[:], r_.t[:], AF.Sqrt, [r_.b], [r_.b])
                P.op("dve", lambda e, r_=r_: e.reciprocal(out=r_.t[:], in_=r_.t[:]), [r_.b], [r_.b])
                for j in range(4):
                    stt(y_.t[:, j, :], y_.t[:, j, :], r_.t[:, j:j + 1], fgb.t[:], ALU.mult, ALU.mult, [y_.p[j], r_.b, fgb.b], [y_.p[j]])
            wr = [dbuf(ykey, bi)] if ykey else []
            P.dma("sp", y_dst[tok0:tok0 + 512, :].rearrange("(j p) c -> p j c", p=128), y_.t[:], y_.p, wr, y_.b)
        P.barrier()

    import os
    stop = int(os.environ.get("MK_STOP", "1000"))
    nph = [0]

    skip = set(x for x in os.environ.get("MK_SKIP", "").split(",") if x)

    def go(name=""):
        nph[0] += 1
        return nph[0] <= stop and name not in skip

    if go():
        setup()
    seqs = [(SS, xs_d, ys_d, 0, gts_bf, "gts"), (SP, xp_d, yp_d, 1, gtp_bf, "gtp")]
    if os.environ.get("MK_ORDER", "sp") == "ps":
        seqs = seqs[::-1]
    for (S, x_in, y_out, col, gt_bf, gtkey) in seqs:
        for l in range(DEPTH):
            x_src, xkey = (x_in, None) if l == 0 else (Y1, "Y1")
            final = (l == DEPTH - 1)
            if go("p1"):
                phase1(S, x_src, xkey, l, col)
            if go("lru"):
                lru(S, l)
            if go("fou"):
                fourier(S, l, gt_bf, gtkey)
            y_dst, ykey = (y_out, None) if final else (Y1, "Y1")
            if go("p3"):
                phase3(S, x_src, xkey, l, col, y_dst, ykey, final)
    P.emit()
    return nc, P


def _consts(SS, SP):
    c = np.arange(256)[:, None].astype(np.float64)
    m = np.arange(256)[None, :].astype(np.float64)
    th = 2 * np.pi * c * m / 256.0
    cs = np.concatenate([np.cos(th), -np.sin(th)], axis=1).astype(np.float32)
    s1 = np.arange(128)[:, None].astype(np.float64)
    k1 = np.arange(128)[None, :].astype(np.float64)
    th = 2 * np.pi * s1 * k1 / 128.0
    f3 = np.stack([np.cos(th), np.sin(th)], axis=1).astype(np.float32)

    def gtab(S):
        N2 = S // 128
        J = max(1, 128 // N2)
        NSTEP = 128 // J
        M = J * N2
        SBLK = min(16, NSTEP)
        nblk = NSTEP // SBLK
        out = np.zeros((nblk, M, SBLK, 3, M), np.float32)
        s2 = np.arange(N2, dtype=np.float64)[:, None]
        k2 = np.arange(N2, dtype=np.float64)[None, :]
        for pi in range(NSTEP):
            for j in range(J):
                s1_ = pi + NSTEP * j
                ph = (k2 * (s1_ + 128.0 * s2)) % S
                th_ = 2 * np.pi * ph / S
                fr, fi = np.cos(th_), -np.sin(th_)
                blk, sl = pi // SBLK, pi % SBLK
                out[blk, j:M:J, sl, 0, j * N2:(j + 1) * N2] = fr
                out[blk, j:M:J, sl, 1, j * N2:(j + 1) * N2] = fi
                out[blk, j:M:J, sl, 2, j * N2:(j + 1) * N2] = -fi
        return out

    return cs, f3, gtab(SS), gtab(SP)


def _pack_vec(norm_g, b_ada, conv_w, conv_b, b_rg, lam):
    def pm(a):
        a = np.asarray(a, np.float32)
        lead = int(np.prod(a.shape[:-1])) if a.ndim > 1 else 1
        nch = a.shape[-1] // 128
        return a.reshape(lead, nch, 128).transpose(2, 0, 1).reshape(128, lead * nch)
    vec = np.concatenate([pm(norm_g), pm(b_ada), pm(conv_w), pm(conv_b), pm(b_rg), pm(lam)], axis=1)
    assert vec.shape == (128, V_N), vec.shape
    return np.ascontiguousarray(vec)


_CACHE = {}


def run(x_prompt, x_sample, c_prompt, c_sample, norm_g, w_ada, b_ada, w_in, conv_w, conv_b,
        w_rg, b_rg, lam, w_a_out, w_b_out, w_o, final_g, ncores=8):
    SS, SP = x_sample.shape[1], x_prompt.shape[1]
    key = (SS, SP)
    if key not in _CACHE:
        _CACHE[key] = build(SS, SP)[0]
    nc = _CACHE[key]
    cs, f3, gts, gtp = _consts(SS, SP)
    vec = _pack_vec(norm_g, b_ada, conv_w, conv_b, b_rg, lam)
    fgb = np.ascontiguousarray(np.broadcast_to(np.asarray(final_g, np.float32)[None, :], (128, D)))
    ident = np.eye(128, dtype=np.float32)
    f32 = lambda a: np.ascontiguousarray(np.asarray(a, np.float32))
    common = {"vec": vec, "fgb": fgb, "w_ada": f32(w_ada), "w_in": f32(w_in), "w_rg": f32(w_rg),
              "w_a_out": f32(w_a_out), "w_b_out": f32(w_b_out), "w_o": f32(w_o), "ident": ident,
              "cs": cs, "f3": f3, "gts": gts, "gtp": gtp, "xp": f32(x_prompt[0])}
    cp_ = np.asarray(c_prompt, np.float32)[0].reshape(NCH, 128).T
    in_maps = []
    for i in range(ncores):
        csm = np.asarray(c_sample, np.float32)[i].reshape(NCH, 128).T
        cvv = np.ascontiguousarray(np.stack([csm, cp_], axis=2))
        m = dict(common)
        m["xs"] = f32(x_sample[i])
        m["cv"] = cvv
        in_maps.append(m)
    res = run_bass_kernel_spmd(nc, in_maps, core_ids=list(range(ncores)))
    y_sample = np.stack([np.asarray(res.results[i]["ys"], np.float32) for i in range(ncores)], axis=0)
    y_prompt = np.asarray(res.results[0]["yp"], np.float32)[None]
    return y_prompt, y_sample


def kernel(**inputs):
    return run(**inputs)
```

```python
import numpy as np
import concourse.bass as bass
import concourse.mybir as mybir
from concourse.bass_utils import run_bass_kernel_spmd

F32 = mybir.dt.float32
BF16 = mybir.dt.bfloat16
AF = mybir.ActivationFunctionType
ALU = mybir.AluOpType

D = 1024
NCORES = 8
DEPTH = 2
NCH = 8
EPS = 1e-6
EPOCH = 16000
DMA_EPOCH = 1000


class Buf:
    __slots__ = ("name", "w", "r", "slot", "excl")

    def __init__(self, name, excl=False):
        self.name = name
        self.w = None
        self.r = {}
        self.slot = None
        self.excl = excl


class Slot:
    __slots__ = ("sem", "count", "nobar")

    def __init__(self, sem):
        self.sem = sem
        self.count = 0
        self.nobar = False


class Prog:
    ENGS = ("pe", "act", "dve", "pool", "sp")

    def __init__(self, nc):
        self.nc = nc
        self.ops = {e: [] for e in self.ENGS}
        self.seq = {e: 0 for e in self.ENGS}
        self.esems = {e: [] for e in self.ENGS}
        self.known = {e: {} for e in self.ENGS}
        self.nsem = 0
        self.slots = []
        self.free = []
        self.ninst = 0

    def _newsem(self, name):
        self.nsem += 1
        return self.nc.alloc_semaphore(name)

    def _esem(self, e, k):
        ep = k // EPOCH
        while len(self.esems[e]) <= ep:
            self.esems[e].append(self._newsem(f"s_{e}_{len(self.esems[e])}"))
        return self.esems[e][ep], (k % EPOCH) + 1

    def _slot(self, b):
        if b.slot is None:
            if self.free:
                b.slot = self.free.pop()
            else:
                b.slot = Slot(self._newsem(f"d_{len(self.slots)}"))
                self.slots.append(b.slot)
        return b.slot

    def release(self, b):
        if b.slot is not None:
            if b.slot.count < 3800:
                self.free.append(b.slot)
            b.slot = None

    def _wait(self, e, ev):
        if ev is None:
            return
        if ev[0] == "eng":
            _, f, k = ev
            if f == e and e == "pe":
                return
            key = ("eng", f)
            if self.known[e].get(key, -1) >= k:
                return
            self.known[e][key] = k
            sem, val = self._esem(f, k)
        else:
            _, slot, cnt = ev
            key = ("dma", id(slot))
            if self.known[e].get(key, 0) >= cnt:
                return
            self.known[e][key] = cnt
            sem, val = slot.sem, 16 * cnt
        self.ninst += 1
        self.ops[e].append(lambda eng, sem=sem, val=val: eng.wait_ge(sem, val))

    def _deps(self, e, reads, writes):
        for b in reads:
            self._wait(e, b.w)
        for b in writes:
            self._wait(e, b.w)
            for ev in b.r.values():
                self._wait(e, ev)

    @staticmethod
    def _mark(ev, key, reads, writes):
        for b in reads:
            b.r[key] = ev
        for b in writes:
            b.w = ev
            b.r = {}

    def op(self, e, fn, reads=(), writes=(), inc=True):
        if any(b.excl for b in reads):
            writes = list(writes) + [b for b in reads if b.excl]
            reads = [b for b in reads if not b.excl]
        self._deps(e, reads, writes)
        k = self.seq[e]
        self.ninst += 1
        if inc:
            self.seq[e] += 1
            sem, _ = self._esem(e, k)
            self.ops[e].append(lambda eng, fn=fn, sem=sem: fn(eng).then_inc(sem, 1))
        else:
            self.ops[e].append(lambda eng, fn=fn: fn(eng))
        ev = ("eng", e, k)
        self._mark(ev, ("eng", e), reads, writes)
        return ev

    def dma(self, q, out, in_, reads, writes, sb, **kw):
        self._deps(q, reads, writes)
        slot = self._slot(sb)
        if slot.count > 0:
            self._wait(q, ("dma", slot, slot.count))
        slot.count += 1
        cnt = slot.count
        sem = slot.sem
        self.ninst += 1
        self.ops[q].append(
            lambda eng, out=out, in_=in_, sem=sem, kw=kw: eng.dma_start(out=out, in_=in_, **kw).then_inc(sem, 16))
        ev = ("dma", slot, cnt)
        self._mark(ev, ("dma", id(slot)), reads, writes)
        return ev

    def dma_fn(self, q, fn, reads, writes, sb):
        self._deps(q, reads, writes)
        slot = self._slot(sb)
        if slot.count > 0:
            self._wait(q, ("dma", slot, slot.count))
        slot.count += 1
        cnt = slot.count
        sem = slot.sem
        self.ninst += 1
        self.ops[q].append(lambda eng, fn=fn, sem=sem: fn(eng).then_inc(sem, 16))
        ev = ("dma", slot, cnt)
        self._mark(ev, ("dma", id(slot)), reads, writes)
        return ev

    def barrier(self, final=False):
        evs = []
        for f in self.ENGS:
            if self.seq[f] > 0:
                evs.append(("eng", f, self.seq[f] - 1))
        for sl in self.slots:
            if sl.count > 0 and (final or not sl.nobar):
                evs.append(("dma", sl, sl.count))
        for e in self.ENGS:
            for ev in evs:
                self._wait(e, ev)

    def emit(self):
        nc = self.nc
        self.barrier(final=True)
        with nc.Block() as block:
            @block.tensor
            def _(eng):
                for t in self.ops["pe"]:
                    t(eng)

            @block.scalar
            def _(eng):
                for t in self.ops["act"]:
                    t(eng)

            @block.vector
            def _(eng):
                for t in self.ops["dve"]:
                    t(eng)

            @block.gpsimd
            def _(eng):
                for t in self.ops["pool"]:
                    t(eng)

            @block.sync
            def _(eng):
                for t in self.ops["sp"]:
                    t(eng)


class Tl:
    def __init__(self, t, name, nparts=0):
        self.t = t
        self.b = Buf(name)
        self.p = [Buf(f"{name}_{i}") for i in range(nparts)]


V_NG = 0
V_BADA = 16
V_CW = 64
V_CB = 128
V_BRG = 144
V_LAM = 208
V_N = 240


def build(SS, SP):
    nc = bass.Bass("TRN2", target_bir_lowering=False)
    P = Prog(nc)
    SMAX = max(SS, SP)

    def din(name, shape, dt=F32):
        return nc.dram_tensor(name, list(shape), dt, kind="ExternalInput").ap()

    def dout(name, shape, dt=F32):
        return nc.dram_tensor(name, list(shape), dt, kind="ExternalOutput").ap()

    def dscr(name, shape, dt=BF16):
        return nc.dram_tensor(name, list(shape), dt).ap()

    xs_d = din("xs", [SS, D]); xp_d = din("xp", [SP, D])
    cv_d = din("cv", [128, NCH, 2]); vec_d = din("vec", [128, V_N]); fgb_d = din("fgb", [128, D])
    w_ada_d = din("w_ada", [DEPTH, D, 3 * D]); w_in_d = din("w_in", [DEPTH, D, 6 * D])
    w_rg_d = din("w_rg", [DEPTH, 2, 2, NCH, 128, 128])
    w_ao_d = din("w_a_out", [DEPTH, D, D]); w_bo_d = din("w_b_out", [DEPTH, D, D]); w_o_d = din("w_o", [DEPTH, D, D])
    ident_d = din("ident", [128, 128]); cs_d = din("cs", [256, 512]); f3_d = din("f3", [128, 2, 128])
    N2S, N2P = SS // 128, SP // 128
    def gshape(S_):
        N2_ = S_ // 128
        J_ = max(1, 128 // N2_)
        NST_ = 128 // J_
        SB_ = min(16, NST_)
        return [NST_ // SB_, J_ * N2_, SB_, 3, J_ * N2_]
    gts_d = din("gts", gshape(SS)); gtp_d = din("gtp", gshape(SP))
    ys_d = dout("ys", [SS, D]); yp_d = dout("yp", [SP // NCORES, D])

    w_in_bf = dscr("w_in_bf", [DEPTH, 12, 128, NCH, 512])
    w_ao_bf = dscr("w_ao_bf", [DEPTH, 128, NCH, D]); w_bo_bf = dscr("w_bo_bf", [DEPTH, 128, NCH, D])
    w_o_bf = dscr("w_o_bf", [DEPTH, 128, NCH, D])
    w_rg_bf = dscr("w_rg_bf", [DEPTH, 128, 2, 2, NCH, 128])
    gts_bf = dscr("gts_bf", gshape(SS)); gtp_bf = dscr("gtp_bf", gshape(SP))
    XA = dscr("XA", [NCH, 128, SMAX]); SGA = dscr("SGA", [NCH, 128, SMAX])
    XB = dscr("XB", [NCH, 128, SMAX]); SGB = dscr("SGB", [NCH, 128, SMAX])
    SM = dscr("SM", [16, 128, SMAX]); AT = dscr("AT", [NCH, 128, SMAX]); BT = dscr("BT", [NCH, 128, SMAX])
    FB = dscr("FB", [4, SMAX // 128, 128, 512])
    Y1 = dscr("Y1", [SMAX, D], F32)
    DB = {}

    def dbuf(*key):
        if key not in DB:
            DB[key] = Buf("d" + "_".join(str(k) for k in key))
        return DB[key]

    class Arena:
        def __init__(self, base, limit):
            self.base, self.limit, self.cur, self.n, self.live = base, limit, base, 0, []

        def reset(self):
            self.cur = self.base
            for tl in self.live:
                P.release(tl.b)
                for pb_ in tl.p:
                    P.release(pb_)
            self.live = []

        def alloc(self, name, shape, dt, nparts=0):
            esz = 4 if dt == F32 else 2
            nbytes = int(np.prod(shape[1:])) * esz
            nbytes = (nbytes + 63) // 64 * 64
            assert self.cur + nbytes <= self.limit, (name, self.cur, nbytes, self.limit)
            self.n += 1
            t = nc.alloc_sbuf_tensor_at(f"{name}_{self.n}", list(shape), dt, offset=self.cur)
            self.cur += nbytes
            tl = Tl(t, f"{name}_{self.n}", nparts)
            self.live.append(tl)
            return tl

    PERS = Arena(16640, 33024)
    AR = Arena(33024, 229000)

    psum = nc.alloc_psum_tensor("psum_all", [128, 4096], F32)
    PB = [Buf(f"psb{i}", excl=True) for i in range(8)]

    def bank(i, n=512, p=128):
        return psum[0:p, i * 512:i * 512 + n]

    def act(out, in_, func, reads, writes, scale=1.0, bias=None, accum_out=None):
        kw = {"scale": scale}
        if bias is not None:
            kw["bias"] = bias
        if accum_out is not None:
            kw["accum_out"] = accum_out
        return P.op("act", lambda e: e.activation(out=out, in_=in_, func=func, **kw), reads, writes)

    def mm(out, lhsT, rhs, start, stop, reads, writes, inc):
        return P.op("pe", lambda e: e.matmul(out, lhsT=lhsT, rhs=rhs, start=start, stop=stop), reads, writes, inc=inc)

    def ts(eng, out, in0, s1, s2, op0, op1, reads, writes):
        return P.op(eng, lambda e: e.tensor_scalar(out=out, in0=in0, scalar1=s1, scalar2=s2, op0=op0, op1=op1), reads, writes)

    def tt(eng, out, in0, in1, op, reads, writes):
        return P.op(eng, lambda e: e.tensor_tensor(out=out, in0=in0, in1=in1, op=op), reads, writes)

    def stt(out, in0, scalar, in1, op0, op1, reads, writes):
        return P.op("dve", lambda e: e.scalar_tensor_tensor(out=out, in0=in0, scalar=scalar, in1=in1, op0=op0, op1=op1), reads, writes)

    def cp(eng, out, in_, reads, writes):
        return P.op(eng, lambda e: e.tensor_copy(out=out, in_=in_), reads, writes)

    def ld(out, in_, reads, tl, q="sp", wr=None):
        return P.dma(q, out, in_, reads, wr if wr is not None else [tl.b], tl.b)

    def st(out, in_, rd, tl, dwrites, q="sp"):
        return P.dma(q, out, in_, rd, dwrites, tl.b)

    ident_f = PERS.alloc("ident_f", [128, 128], F32)
    ident_b = PERS.alloc("ident_b", [128, 128], BF16)
    ones_f = PERS.alloc("ones_f", [128, 128], F32)
    vec = PERS.alloc("vec", [128, V_N], F32)
    cv = PERS.alloc("cv", [128, NCH, 2], F32)
    scv = PERS.alloc("scv", [128, NCH, 2], F32)
    hbrg = PERS.alloc("hbrg", [128, 64], F32)
    kk = PERS.alloc("kk", [128, 32], F32)
    kh = PERS.alloc("kh", [128, 32], F32)
    mod = PERS.alloc("mod", [128, DEPTH, 24, 2], F32)
    gm = PERS.alloc("gm", [128, DEPTH, NCH, 2], F32)
    one1 = PERS.alloc("one1", [128, 1], F32)
    half1 = PERS.alloc("half1", [128, 1], F32)
    cs_b = PERS.alloc("cs_b", [128, 2, 512], BF16)
    f3_b = PERS.alloc("f3_b", [128, 2, 128], BF16)
    fgb = PERS.alloc("fgb", [128, D], F32)
    tmpv = [PERS.alloc(f"tmpv{i}", [128, 32], F32) for i in range(4)]

    def setup():
        ld(ident_f.t[:], ident_d, [], ident_f)
        ld(vec.t[:], vec_d, [], vec)
        ld(cv.t[:], cv_d, [], cv)
        ld(fgb.t[:], fgb_d, [], fgb)
        ld(ident_b.t[:], ident_d, [], ident_b, q="pool")
        ld(cs_b.t[:], cs_d.rearrange("(cc p) n -> p cc n", p=128), [], cs_b, q="pool")
        ld(f3_b.t[:], f3_d, [], f3_b, q="pool")
        P.op("dve", lambda e: e.memset(ones_f.t[:], 1.0), [], [ones_f.b])
        P.op("dve", lambda e: e.memset(one1.t[:], 1.0), [], [one1.b])
        P.op("dve", lambda e: e.memset(half1.t[:], 0.5), [], [half1.b])
        wsems = [Buf(f"wconv{i}") for i in range(8)]
        wn = [0]

        class _W:
            pass

        def wsem_next():
            wn[0] += 1
            b_ = wsems[wn[0] % 8]
            P._slot(b_).nobar = True
            return b_
        for l in range(DEPTH):
            for g in range(12):
                src = w_in_d[l, :, g * 512:(g + 1) * 512].rearrange("(ic p) n -> p ic n", p=128)
                P.dma("pool", w_in_bf[l, g], src, [], [dbuf("w_in", l, g)], wsem_next())
            for nm, sd, dd in (("w_ao", w_ao_d, w_ao_bf), ("w_bo", w_bo_d, w_bo_bf), ("w_o", w_o_d, w_o_bf)):
                for h in range(2):
                    src = sd[l, :, h * 512:(h + 1) * 512].rearrange("(ic p) n -> p ic n", p=128)
                    P.dma("pool", dd[l, :, :, h * 512:(h + 1) * 512], src, [], [dbuf(nm, l, h)], wsem_next())
            for d in range(2):
                for g in range(2):
                    src = w_rg_d[l, d, g].rearrange("h i j -> i h j")
                    P.dma("pool", w_rg_bf[l, :, d, g], src, [], [dbuf("w_rg", l, d, g)], wsem_next())
        for blk in range(gshape(SS)[0]):
            P.dma("pool", gts_bf[blk], gts_d[blk], [], [dbuf("gts", blk)], wsem_next())
        for blk in range(gshape(SP)[0]):
            P.dma("pool", gtp_bf[blk], gtp_d[blk], [], [dbuf("gtp", blk)], wsem_next())

        v = vec.t
        ts("dve", hbrg.t[:], v[:, V_BRG:V_BRG + 64], 0.5, None, ALU.mult, ALU.bypass, [vec.b], [hbrg.b])
        e_, z_, z2_, acc_ = tmpv
        act(e_.t[:], v[:, V_LAM:V_LAM + 32], AF.Exp, [vec.b], [e_.b], scale=-1.0)
        ts("dve", z_.t[:], e_.t[:], 2.0, None, ALU.add, ALU.bypass, [e_.b], [z_.b])
        P.op("dve", lambda e: e.reciprocal(out=z_.t[:], in_=z_.t[:]), [z_.b], [z_.b])
        tt("dve", z_.t[:], z_.t[:], e_.t[:], ALU.mult, [z_.b, e_.b], [z_.b])
        tt("dve", z2_.t[:], z_.t[:], z_.t[:], ALU.mult, [z_.b], [z2_.b])
        ts("dve", acc_.t[:], z2_.t[:], 1.0 / 11.0, 1.0 / 9.0, ALU.mult, ALU.add, [z2_.b], [acc_.b])
        for cst in (1.0 / 7.0, 1.0 / 5.0, 1.0 / 3.0, 1.0):
            tt("dve", acc_.t[:], acc_.t[:], z2_.t[:], ALU.mult, [acc_.b, z2_.b], [acc_.b])
            ts("dve", acc_.t[:], acc_.t[:], cst, None, ALU.add, ALU.bypass, [acc_.b], [acc_.b])
        tt("dve", acc_.t[:], acc_.t[:], z_.t[:], ALU.mult, [acc_.b, z_.b], [acc_.b])
        ts("dve", kk.t[:], acc_.t[:], -16.0, None, ALU.mult, ALU.bypass, [acc_.b], [kk.b])
        ts("dve", kh.t[:], acc_.t[:], -8.0, None, ALU.mult, ALU.bypass, [acc_.b], [kh.b])

        act(scv.t[:], cv.t[:], AF.Silu, [cv.b], [scv.b])
        AR.reset()
        wada = [AR.alloc("wada", [128, NCH, D], F32) for _ in range(2)]
        n = 0
        for l in range(DEPTH):
            for third in range(3):
                wt = wada[n % 2]; n += 1
                for h in range(2):
                    ld(wt.t[:, :, h * 512:(h + 1) * 512],
                       w_ada_d[l, :, third * D + h * 512: third * D + (h + 1) * 512].rearrange("(ic p) n -> p ic n", p=128),
                       [], wt)
                for jj in range(8):
                    j = third * 8 + jj
                    for ic in range(NCH):
                        mm(psum[:, 2 * j:2 * j + 2], wt.t[:, ic, jj * 128:(jj + 1) * 128], scv.t[:, ic, :],
                           ic == 0, ic == NCH - 1, [wt.b, scv.b], [PB[0]], inc=(ic == NCH - 1))
            for j in range(24):
                ts("dve", mod.t[:, l, j, :], psum[:, 2 * j:2 * j + 2], v[:, V_BADA + l * 24 + j:V_BADA + l * 24 + j + 1], None,
                   ALU.add, ALU.bypass, [PB[0], vec.b], [mod.b])
            for c in range(NCH):
                ts("dve", gm.t[:, l, c, :], mod.t[:, l, 8 + c, :], 1.0, v[:, V_NG + l * 8 + c:V_NG + l * 8 + c + 1],
                   ALU.add, ALU.mult, [mod.b, vec.b], [gm.b])
        P.barrier()

    def phase1(S, x_src, xkey, l, col):
        AR.reset()
        SBK = min(2048, S)
        NB = SBK // 512
        nsb = S // SBK
        xt = [AR.alloc("xt", [128, 4, D], F32) for _ in range(2)]
        xn = [AR.alloc("xn", [128, 4, D], F32, nparts=4) for _ in range(2)]
        hT = [AR.alloc("hT", [128, NCH, SBK], BF16, nparts=NCH * NB) for _ in range(2)]
        win = [AR.alloc("win", [128, NCH, 512], BF16) for _ in range(3)]
        stg = [AR.alloc("stg", [128, SBK], BF16, nparts=NB) for _ in range(6)]
        junk = AR.alloc("junk", [128, D], BF16)
        ss = [AR.alloc("ss", [128, 4], F32) for _ in range(2)]
        rs = [AR.alloc("rs", [128, 4], F32) for _ in range(2)]
        v = vec.t
        cnt = {"blk": 0, "tb": 0, "pb": 0, "w": 0, "stg": 0}

        def prep_nonpe(sb, blk):
            i = cnt["blk"]
            cnt["blk"] += 1
            x_, n_, s_, r_ = xt[i % 2], xn[i % 2], ss[i % 2], rs[i % 2]
            tok0 = sb * SBK + blk * 512
            rd = [dbuf(xkey, tok0 // 512)] if xkey else []
            ld(x_.t[:], x_src[tok0:tok0 + 512, :].rearrange("(j p) c -> p j c", p=128), rd, x_)
            for j in range(4):
                act(junk.t[:], x_.t[:, j, :], AF.Square, [x_.b], [junk.b, s_.b], accum_out=s_.t[:, j:j + 1])
            ts("dve", r_.t[:], s_.t[:], 1.0 / D, EPS, ALU.mult, ALU.add, [s_.b], [r_.b])
            act(r_.t[:], r_.t[:], AF.Sqrt, [r_.b], [r_.b])
            P.op("dve", lambda e: e.reciprocal(out=r_.t[:], in_=r_.t[:]), [r_.b], [r_.b])
            for j in range(4):
                ts("pool", n_.t[:, j, :], x_.t[:, j, :], r_.t[:, j:j + 1], 1.0, ALU.mult, ALU.mult, [x_.b, r_.b], [n_.p[j]])
            return n_

        def prep_pe(sb, blk, n_):
            h_ = hT[sb % 2]
            for c in range(NCH):
                bk = cnt["tb"] % 2
                cnt["tb"] += 1
                for j in range(4):
                    P.op("pe", lambda e, j=j, c=c, bk=bk: e.transpose(psum[:, bk * 512 + j * 128: bk * 512 + (j + 1) * 128],
                                                                       n_.t[:, j, c * 128:(c + 1) * 128], ident_f.t[:]),
                         [n_.p[j], ident_f.b], [PB[bk]], inc=(j == 3))
                act(h_.t[:, c, blk * 512:(blk + 1) * 512], bank(bk), AF.Identity, [PB[bk], gm.b, mod.b], [h_.p[c * NB + blk]],
                    scale=gm.t[:, l, c, col:col + 1], bias=mod.t[:, l, c, col:col + 1])

        def jc_info(jc):
            if jc < 8:
                return "copy", XA, "XA", jc
            if jc < 16:
                return "silu", SGA, "SGA", jc - 8
            if jc < 24:
                return "copy", XB, "XB", jc - 16
            if jc < 32:
                return "silu", SGB, "SGB", jc - 24
            return "sigm", SM, "SM", jc - 32

        def load_w(idx):
            if idx < nsb * 12:
                g_ = idx % 12
                w__ = win[idx % 3]
                ld(w__.t[:], w_in_bf[l, g_], [dbuf("w_in", l, g_)], w__)

        def proj_group(sb, g):
            h_ = hT[sb % 2]
            idx = sb * 12 + g
            w_ = win[idx % 3]
            if idx == 0:
                load_w(0)
                load_w(1)
            load_w(idx + 2)
            for jcl in range(4):
                jc = g * 4 + jcl
                kind, dten, dname, ch = jc_info(jc)
                s_ = stg[cnt["stg"] % 6]
                cnt["stg"] += 1
                for tile in range(NB):
                    bk = 2 + cnt["pb"] % 6
                    cnt["pb"] += 1
                    for ic in range(NCH):
                        mm(bank(bk), w_.t[:, ic, jcl * 128:(jcl + 1) * 128], h_.t[:, ic, tile * 512:(tile + 1) * 512],
                           ic == 0, ic == NCH - 1, [w_.b, h_.p[ic * NB + tile]], [PB[bk]], inc=(ic == NCH - 1))
                    o = s_.t[:, tile * 512:(tile + 1) * 512]
                    if kind == "copy":
                        cp("dve", o, bank(bk), [PB[bk]], [s_.p[tile]])
                    elif kind == "silu":
                        act(o, bank(bk), AF.Silu, [PB[bk]], [s_.p[tile]])
                    else:
                        act(o, bank(bk), AF.Sigmoid, [PB[bk]], [s_.p[tile]])
                P.dma("sp", dten[ch, :, sb * SBK:(sb + 1) * SBK], s_.t[:], s_.p, [dbuf(dname, ch, sb)], s_.b)

        for blk in range(NB):
            n_ = prep_nonpe(0, blk)
            prep_pe(0, blk, n_)
        gper = 12 // NB
        for sb in range(nsb):
            cur = None
            for g in range(12):
                if sb + 1 < nsb and g % gper == 0:
                    cur = (g // gper, prep_nonpe(sb + 1, g // gper))
                proj_group(sb, g)
                if sb + 1 < nsb and g % gper == gper - 1:
                    prep_pe(sb + 1, cur[0], cur[1])
        P.barrier()

    def lru(S, l):
        AR.reset()
        SL = min(1024, S)
        NT = SL // 512
        nsl = S // SL
        SBK = min(2048, S)
        hb = AR.alloc("hb", [128, S], F32, nparts=nsl)
        wrg = AR.alloc("wrg", [128, 2, 2, NCH, 128], BF16)
        dk = [AR.alloc("dk", [128, 4, 128], BF16) for _ in range(2)]
        xap = [AR.alloc("xap", [128, SL + 4], BF16) for _ in range(3)]
        ring = lambda nm, dt: [AR.alloc(nm, [128, SL], dt) for _ in range(2)]
        xcb = [AR.alloc("xcb", [128, SL], BF16) for _ in range(3)]
        tr, ti, ip, bb = (ring(n, F32) for n in ("tr", "ti", "ip", "bb"))
        ring3 = lambda nm: [AR.alloc(nm, [128, SL], F32) for _ in range(3)]
        aa, m_, u_, hh = ring3("aa"), ring3("m"), ring3("u"), ring3("hh")
        s_ = ring("s", F32)
        sga = [AR.alloc("sga", [128, SL], BF16) for _ in range(6)]
        sA = ring("sA", BF16)
        v = vec.t
        for d in range(2):
            for g in range(2):
                ld(wrg.t[:, d, g], w_rg_bf[l, :, d, g], [dbuf("w_rg", l, d, g)], wrg)
        slabs = []
        for hd in range(NCH):
            for d in (1, 0):
                order = range(nsl - 1, -1, -1) if d == 1 else range(nsl)
                for sl in order:
                    slabs.append((hd, d, sl))
        n = len(slabs)
        st_ = {"prev_h": None}
        fidx = []
        nf = 0
        for (_h, _d, _s) in slabs:
            fidx.append(nf)
            if _d == 0:
                nf += 1

        def ldx(i):
            if i >= n:
                return
            hd, d, sl = slabs[i]
            x_ = xap[i % 3]
            t0 = sl * SL
            lo, hi = max(0, t0 - 2), min(S, t0 + SL + 2)
            off = lo - (t0 - 2)
            if t0 == 0:
                P.op("pool", lambda e, x_=x_: e.memset(x_.t[:, 0:2], 0.0), [], [x_.b])
            if t0 + SL == S:
                P.op("pool", lambda e, x_=x_: e.memset(x_.t[:, SL + 2:SL + 4], 0.0), [], [x_.b])
            rd = [dbuf("XA", hd, sb) for sb in range(lo // SBK, (hi - 1) // SBK + 1)]
            ld(x_.t[:, off:off + (hi - lo)], XA[hd, :, lo:hi], rd, x_)

        def stage_a(i):
            hd, d, sl = slabs[i]
            dk_ = dk[hd % 2]
            if d == 1 and sl == nsl - 1:
                for k in range(4):
                    col = V_CW + (l * 4 + k) * 8 + hd
                    ts("dve", dk_.t[:, k, :], ident_b.t[:], v[:, col:col + 1], None, ALU.mult, ALU.bypass,
                       [ident_b.b, vec.b], [dk_.b])
            ldx(i + 2)
            x_ = xap[i % 3]
            c0 = (i % 2) * 2
            for tile in range(NT):
                for k in range(4):
                    mm(bank(c0 + tile), dk_.t[:, k, :], x_.t[:, tile * 512 + k: tile * 512 + k + 512],
                       k == 0, k == 3, [dk_.b, x_.b], [PB[c0 + tile]], inc=(k == 3))
            cps = psum[:, c0 * 512:c0 * 512 + SL]
            cpb = [PB[c0 + t_] for t_ in range(NT)]
            cbcol = v[:, V_CB + l * 8 + hd:V_CB + l * 8 + hd + 1]
            xcb_ = xcb[i % 3]
            ts("dve", xcb_.t[:], cps, cbcol, None, ALU.add, ALU.bypass, cpb + [vec.b], [xcb_.b])
            if d == 0:
                g_ = sga[fidx[i] % 6]
                t0 = sl * SL
                ld(g_.t[:], SGA[hd, :, t0:t0 + SL], [dbuf("SGA", hd, t0 // SBK)], g_)

        def stage_b0(i):
            hd, d, sl = slabs[i]
            kcol = (l * 2 + d) * 8 + hd
            brc = hbrg.t[:, ((l * 2 + d) * 2 + 0) * 8 + hd:((l * 2 + d) * 2 + 0) * 8 + hd + 1]
            bic = hbrg.t[:, ((l * 2 + d) * 2 + 1) * 8 + hd:((l * 2 + d) * 2 + 1) * 8 + hd + 1]
            xcb_ = xcb[i % 3]
            tr_, ti_, ip_ = tr[i % 2], ti[i % 2], ip[i % 2]
            a_, mm_, uu_ = aa[i % 3], m_[i % 3], u_[i % 3]
            for tile in range(NT):
                mm(bank(4 + tile), wrg.t[:, d, 0, hd, :], xcb_.t[:, tile * 512:(tile + 1) * 512], True, True,
                   [wrg.b, xcb_.b], [PB[4 + tile]], inc=True)
                mm(bank(6 + tile), wrg.t[:, d, 1, hd, :], xcb_.t[:, tile * 512:(tile + 1) * 512], True, True,
                   [wrg.b, xcb_.b], [PB[6 + tile]], inc=True)
            gr = psum[:, 4 * 512:4 * 512 + SL]
            gi = psum[:, 6 * 512:6 * 512 + SL]
            grb = [PB[4 + t_] for t_ in range(NT)]
            gib = [PB[6 + t_] for t_ in range(NT)]
            act(tr_.t[:], gr, AF.Tanh, grb + [hbrg.b], [tr_.b], scale=0.5, bias=brc)
            act(ti_.t[:], gi, AF.Tanh, gib + [hbrg.b], [ti_.b], scale=0.5, bias=bic)
            act(a_.t[:], tr_.t[:], AF.Exp, [tr_.b, kh.b], [a_.b], scale=kh.t[:, kcol:kcol + 1], bias=kh.t[:, kcol:kcol + 1])
            act(mm_.t[:], tr_.t[:], AF.Exp, [tr_.b, kk.b], [mm_.b], scale=kk.t[:, kcol:kcol + 1], bias=kk.t[:, kcol:kcol + 1])
            ts("pool", ip_.t[:], ti_.t[:], 1.0, 1.0, ALU.add, ALU.mult, [ti_.b], [ip_.b])
            tt("pool", uu_.t[:], ip_.t[:], xcb_.t[:], ALU.mult, [ip_.b, xcb_.b], [uu_.b])

        def stage_b(i):
            hd, d, sl = slabs[i]
            t0 = sl * SL
            a_, mm_, uu_, b_ = aa[i % 3], m_[i % 3], u_[i % 3], bb[i % 2]
            act(mm_.t[:], mm_.t[:], AF.Sqrt, [mm_.b, one1.b], [mm_.b], scale=-1.0, bias=one1.t[:, 0:1])
            if d == 1 and sl == nsl - 1:
                P.op("pool", lambda e, mm_=mm_: e.memset(mm_.t[:, SL - 1:SL], 1.0), [], [mm_.b])
            if d == 0 and sl == 0:
                P.op("pool", lambda e, mm_=mm_: e.memset(mm_.t[:, 0:1], 1.0), [], [mm_.b])
            stt(b_.t[:], uu_.t[:], 0.5, mm_.t[:], ALU.mult, ALU.mult, [uu_.b, mm_.b], [b_.b])
            if d == 1:
                if sl == nsl - 1:
                    init, ird = 0.0, []
                else:
                    init, ird = hb.t[:, t0 + SL:t0 + SL + 1], [hb.p[sl + 1]]
                P.op("dve", lambda e, t0=t0, a_=a_, b_=b_, init=init: e.tensor_tensor_scan(
                    out=hb.t[:, t0:t0 + SL][:, ::-1], data0=a_.t[:, ::-1], data1=b_.t[:, ::-1], initial=init,
                    op0=ALU.mult, op1=ALU.add), [a_.b, b_.b] + ird, [hb.p[sl]])
            else:
                f = fidx[i]
                h_ = hh[f % 3]
                if sl == 0:
                    init, ird = 0.0, []
                else:
                    prev_h = st_["prev_h"]
                    init, ird = prev_h.t[:, SL - 1:SL], [prev_h.b]
                P.op("dve", lambda e, h_=h_, a_=a_, b_=b_, init=init: e.tensor_tensor_scan(
                    out=h_.t[:], data0=a_.t[:], data1=b_.t[:], initial=init, op0=ALU.mult, op1=ALU.add),
                    [a_.b, b_.b] + ird, [h_.b])
                st_["prev_h"] = h_

        def stage_b2(i):
            hd, d, sl = slabs[i]
            if d != 0:
                return
            t0 = sl * SL
            f = fidx[i]
            h_ = hh[f % 3]
            g_ = sga[f % 6]
            sum_ = s_[f % 2]
            o_ = sA[f % 2]
            tt("pool", sum_.t[:], h_.t[:], hb.t[:, t0:t0 + SL], ALU.add, [h_.b, hb.p[sl]], [sum_.b])
            tt("dve", o_.t[:], sum_.t[:], g_.t[:], ALU.mult, [sum_.b, g_.b], [o_.b])
            st(AT[hd, :, t0:t0 + SL], o_.t[:], [o_.b], o_, [dbuf("AT", hd, sl)])

        ldx(0)
        ldx(1)
        na = [0]

        def ensure_a(upto):
            while na[0] <= upto and na[0] < n:
                stage_a(na[0])
                na[0] += 1

        pend = []

        def flush():
            while pend:
                stage_b2(pend.pop(0))

        i = 0
        while i < n:
            grp = [i, i + 1] if i + 1 < n else [i]
            for j in grp:
                ensure_a(j + 2)
                stage_b0(j)
            for j in grp:
                if slabs[j][1] == 1:
                    flush()
                stage_b(j)
                flush()
                if slabs[j][1] == 0:
                    pend.append(j)
            i += len(grp)
        flush()
        P.barrier()

    def fourier(S, l, gt_bf, gtkey):
        AR.reset()
        N2 = S // 128
        J = max(1, 128 // N2)
        NSTEP = 128 // J
        M = J * N2
        SBLK = min(16, NSTEP)
        SBK = min(2048, S)
        invn = 1.0 / float(np.sqrt(S * 256.0))
        xbg = [AR.alloc("xbg", [128, 2, S], BF16) for _ in range(2)]
        gt = [AR.alloc("gt", [M, SBLK, 3, M], BF16) for _ in range(2)]
        wsb = [AR.alloc("wsb", [M, 512], BF16) for _ in range(3)]
        bst = [AR.alloc("bst", [M, 4, 512], BF16, nparts=4) for _ in range(3)]
        steps = [(g, pi) for g in range(4) for pi in range(NSTEP)]
        n = len(steps)

        def st_a(i):
            g, pi = steps[i]
            x_ = xbg[g % 2]
            if pi == 0:
                for cc in range(2):
                    ld(x_.t[:, cc, :], XB[2 * g + cc, :, 0:S], [dbuf("XB", 2 * g + cc, sb) for sb in range(S // SBK)], x_)
            if pi % SBLK == 0:
                gslot = gt[(i // SBLK) % 2]
                ld(gslot.t[:], gt_bf[pi // SBLK], [dbuf(gtkey, pi // SBLK)], gslot)
            wb = i % 2
            for cc in range(2):
                lhs = x_.t[:, cc, pi:S:NSTEP]
                mm(bank(wb, 512, M), lhs, cs_b.t[:, cc, :], cc == 0, cc == 1, [x_.b, cs_b.b], [PB[wb]], inc=(cc == 1))
            w_ = wsb[i % 3]
            if i % 2 == 0:
                act(w_.t[:], bank(wb, 512, M), AF.Identity, [PB[wb]], [w_.b])
            else:
                cp("dve", w_.t[:], bank(wb, 512, M), [PB[wb]], [w_.b])

        def st_b(i):
            g, pi = steps[i]
            gslot = gt[(i // SBLK) % 2]
            w_ = wsb[i % 3]
            bbk = 2 + i % 2
            sl16 = pi % SBLK
            Fr, Fi, Fin = gslot.t[:, sl16, 0, :], gslot.t[:, sl16, 1, :], gslot.t[:, sl16, 2, :]
            o_r = psum[0:M, bbk * 512: bbk * 512 + 256]
            o_i = psum[0:M, bbk * 512 + 256: bbk * 512 + 512]
            rd = [gslot.b, w_.b]
            mm(o_r, Fr, w_.t[:, 0:256], True, False, rd, [PB[bbk]], inc=False)
            mm(o_r, Fin, w_.t[:, 256:512], False, True, rd, [PB[bbk]], inc=False)
            mm(o_i, Fr, w_.t[:, 256:512], True, False, rd, [PB[bbk]], inc=False)
            mm(o_i, Fi, w_.t[:, 0:256], False, True, rd, [PB[bbk]], inc=True)
            b_ = bst[(i // 4) % 3]
            if i % 2 == 0:
                cp("dve", b_.t[:, pi % 4, :], bank(bbk, 512, M), [PB[bbk]], [b_.p[pi % 4]])
            else:
                act(b_.t[:, pi % 4, :], bank(bbk, 512, M), AF.Identity, [PB[bbk]], [b_.p[pi % 4]])
            if pi % 4 == 3:
                for j in range(J):
                    P.dma("sp", FB[g, 0:N2, j * NSTEP + pi - 3: j * NSTEP + pi + 1, :], b_.t[j * N2:(j + 1) * N2, :, :],
                          b_.p, [dbuf("FB", g, j, pi // 4)], b_.b)

        st_a(0)
        for i in range(n):
            if i + 1 < n:
                st_a(i + 1)
            st_b(i)
        P.barrier()
        AR.reset()
        sgb = AR.alloc("sgb", [128, 2, S], BF16)
        ybt = AR.alloc("ybt", [128, 2, S], BF16, nparts=2 * (N2 // 4))
        bin_ = [AR.alloc("bin", [128, 4, 512], BF16) for _ in range(3)]
        qi = 0
        for g in range(4):
            for mc in range(2):
                ld(sgb.t[:, mc, :], SGB[2 * g + mc, :, 0:S], [dbuf("SGB", 2 * g + mc, sb) for sb in range(S // SBK)], sgb)
            for q in range(N2 // 4):
                b_ = bin_[qi % 3]
                ld(b_.t[:], FB[g, 4 * q:4 * q + 4, :, :].rearrange("k s n -> s k n"), [dbuf("FB", g, j, i) for j in range(J) for i in range(NSTEP // 4)], b_)
                for mc in range(2):
                    bk = 4 + 2 * mc + (qi % 2)
                    for j in range(4):
                        o = psum[:, bk * 512 + j * 128: bk * 512 + (j + 1) * 128]
                        mm(o, b_.t[:, j, mc * 128:(mc + 1) * 128], f3_b.t[:, 0, :], True, False, [b_.b, f3_b.b], [PB[bk]], inc=False)
                        mm(o, b_.t[:, j, 256 + mc * 128:256 + (mc + 1) * 128], f3_b.t[:, 1, :], False, True, [b_.b, f3_b.b], [PB[bk]],
                           inc=(j == 3))
                    pv = bank(bk).rearrange("p (j k) -> p k j", j=4)
                    ov = ybt.t[:, mc, :].rearrange("p (k n) -> p k n", n=N2)[:, :, 4 * q:4 * q + 4]
                    gv = sgb.t[:, mc, :].rearrange("p (k n) -> p k n", n=N2)[:, :, 4 * q:4 * q + 4]
                    stt(ov, pv, invn, gv, ALU.mult, ALU.mult, [PB[bk], sgb.b], [ybt.p[mc * (N2 // 4) + q]])
                qi += 1
            for mc in range(2):
                st(BT[2 * g + mc, :, 0:S], ybt.t[:, mc, :], ybt.p[mc * (N2 // 4):(mc + 1) * (N2 // 4)], ybt, [dbuf("BT", 2 * g + mc)])
        P.barrier()

    def phase3(S, x_src, xkey, l, col, y_dst, ykey, final, shard=False):
        AR.reset()
        SBK = min(2048, S)
        SL = min(1024, S)
        PSH = S // NCORES
        pid_cache = {}
        nblk = (PSH if shard else S) // 512
        wa = AR.alloc("wa", [128, NCH, D], BF16)
        wb = AR.alloc("wb", [128, NCH, D], BF16)
        wog = AR.alloc("wog", [128, NCH, D], BF16)
        dg = AR.alloc("dg", [128, 128], F32)
        at = [AR.alloc("at", [128, NCH, 512], BF16) for _ in range(2)]
        bt = [AR.alloc("bt", [128, NCH, 512], BF16) for _ in range(2)]
        sm = [AR.alloc("sm", [128, 16, 512], BF16) for _ in range(2)]
        xr = [AR.alloc("xr", [128, 4, D], F32, nparts=4) for _ in range(2)]
        uT = [AR.alloc("uT", [128, NCH, 512], BF16, nparts=NCH) for _ in range(2)]
        ua = [AR.alloc("ua", [128, 512], F32) for _ in range(2)]
        ub = [AR.alloc("ub", [128, 512], F32) for _ in range(2)]
        junk = AR.alloc("junk3", [128, D], BF16)
        ss = [AR.alloc("ss3", [128, 4], F32) for _ in range(2)]
        rs = [AR.alloc("rs3", [128, 4], F32) for _ in range(2)]
        for h in range(2):
            ld(wa.t[:, :, h * 512:(h + 1) * 512], w_ao_bf[l, :, :, h * 512:(h + 1) * 512], [dbuf("w_ao", l, h)], wa)
            ld(wb.t[:, :, h * 512:(h + 1) * 512], w_bo_bf[l, :, :, h * 512:(h + 1) * 512], [dbuf("w_bo", l, h)], wb)
            ld(wog.t[:, :, h * 512:(h + 1) * 512], w_o_bf[l, :, :, h * 512:(h + 1) * 512], [dbuf("w_o", l, h)], wog)
        for c in range(NCH):
            ts("dve", dg.t[:], ident_f.t[:], mod.t[:, l, 16 + c, col:col + 1], None, ALU.mult, ALU.bypass,
               [ident_f.b, mod.b], [dg.b])
            mm(psum[:, c * 128:(c + 1) * 128], ones_f.t[:], dg.t[:], True, True, [ones_f.b, dg.b], [PB[c // 4]], inc=True)
        for ic in range(NCH):
            tt("dve", wog.t[:, ic, :], wog.t[:, ic, :], psum[:, 0:D], ALU.mult, [wog.b, PB[0], PB[1]], [wog.b])

        def loads(bi):
            tok0 = bi * 512
            a_, b_, m_, x_ = at[bi % 2], bt[bi % 2], sm[bi % 2], xr[bi % 2]
            if shard:
                def tb(eng):
                    if "pid" not in pid_cache:
                        pid_cache["pid"] = eng.partition_id()
                    return bass.ds(pid_cache["pid"] * PSH + tok0, 512)
                rd_at = [dbuf("AT", hd, sl_) for hd in range(NCH) for sl_ in range(S // SL)]
                rd_bt = [dbuf("BT", c) for c in range(NCH)]
                P.dma_fn("sp", lambda eng: eng.dma_start(out=a_.t[:], in_=AT[:, :, tb(eng)].rearrange("c p t -> p c t")),
                         rd_at, [a_.b], a_.b)
                P.dma_fn("sp", lambda eng: eng.dma_start(out=b_.t[:], in_=BT[:, :, tb(eng)].rearrange("c p t -> p c t")),
                         rd_bt, [b_.b], b_.b)
                for h in range(2):
                    rd_sm = [dbuf("SM", c, sb_) for c in range(h * 8, (h + 1) * 8) for sb_ in range(S // SBK)]
                    P.dma_fn("sp", lambda eng, h=h: eng.dma_start(
                        out=m_.t[:, h * 8:(h + 1) * 8, :],
                        in_=SM[h * 8:(h + 1) * 8, :, tb(eng)].rearrange("c p t -> p c t")), rd_sm, [m_.b], m_.b)
                rd = [dbuf(xkey, i_) for i_ in range(S // 512)] if xkey else []
                P.dma_fn("sp", lambda eng: eng.dma_start(
                    out=x_.t[:], in_=x_src[tb(eng), :].rearrange("(j p) c -> p j c", p=128)), rd, [x_.b] + x_.p, x_.b)
                return
            ld(a_.t[:], AT[:, :, tok0:tok0 + 512].rearrange("c p t -> p c t"), [dbuf("AT", hd, tok0 // SL) for hd in range(NCH)], a_)
            ld(b_.t[:], BT[:, :, tok0:tok0 + 512].rearrange("c p t -> p c t"), [dbuf("BT", c) for c in range(NCH)], b_)
            for h in range(2):
                ld(m_.t[:, h * 8:(h + 1) * 8, :], SM[h * 8:(h + 1) * 8, :, tok0:tok0 + 512].rearrange("c p t -> p c t"),
                   [dbuf("SM", c, tok0 // SBK) for c in range(h * 8, (h + 1) * 8)], m_)
            rd = [dbuf(xkey, bi)] if xkey else []
            ld(x_.t[:], x_src[tok0:tok0 + 512, :].rearrange("(j p) c -> p j c", p=128), rd, x_, wr=[x_.b] + x_.p)

        loads(0)
        pb = 0
        for bi in range(nblk):
            if bi + 1 < nblk:
                loads(bi + 1)
            a_, b_, m_, x_ = at[bi % 2], bt[bi % 2], sm[bi % 2], xr[bi % 2]
            u_ = uT[bi % 2]
            y_ = x_
            tok0 = bi * 512
            for jc in range(NCH):
                bka = pb % 8; pb += 1
                bkb = pb % 8; pb += 1
                for ic in range(NCH):
                    mm(bank(bka), wa.t[:, ic, jc * 128:(jc + 1) * 128], a_.t[:, ic, :], ic == 0, ic == NCH - 1,
                       [wa.b, a_.b], [PB[bka]], inc=(ic == NCH - 1))
                for ic in range(NCH):
                    mm(bank(bkb), wb.t[:, ic, jc * 128:(jc + 1) * 128], b_.t[:, ic, :], ic == 0, ic == NCH - 1,
                       [wb.b, b_.b], [PB[bkb]], inc=(ic == NCH - 1))
                ua_, ub_ = ua[jc % 2], ub[jc % 2]
                tt("dve", ua_.t[:], bank(bka), m_.t[:, jc, :], ALU.mult, [PB[bka], m_.b], [ua_.b])
                tt("dve", ub_.t[:], bank(bkb), m_.t[:, 8 + jc, :], ALU.mult, [PB[bkb], m_.b], [ub_.b])
                tt("pool", u_.t[:, jc, :], ua_.t[:], ub_.t[:], ALU.add, [ua_.b, ub_.b], [u_.p[jc]])
            for tile in range(4):
                for half in range(2):
                    bk = pb % 8; pb += 1
                    for ic in range(NCH):
                        mm(bank(bk), u_.t[:, ic, tile * 128:(tile + 1) * 128], wog.t[:, ic, half * 512:(half + 1) * 512],
                           ic == 0, ic == NCH - 1, [u_.p[ic], wog.b], [PB[bk]], inc=(ic == NCH - 1))
                    tt("dve", y_.t[:, tile, half * 512:(half + 1) * 512], bank(bk), x_.t[:, tile, half * 512:(half + 1) * 512],
                       ALU.add, [PB[bk], x_.b], [y_.p[tile]])
            if final:
                s_, r_ = ss[bi % 2], rs[bi % 2]
                for j in range(4):
                    act(junk.t[:], y_.t[:, j, :], AF.Square, [y_.p[j]], [junk.b, s_.b], accum_out=s_.t[:, j:j + 1])
                ts("dve", r_.t[:], s_.t[:], 1.0 / D, EPS, ALU.mult, ALU.add, [s_.b], [r_.b])
                act(r_.t[:], r_.t[:], AF.Sqrt, [r_.b], [r_.b])
                P.op("dve", lambda e, r_=r_: e.reciprocal(out=r_.t[:], in_=r_.t[:]), [r_.b], [r_.b])
                for j in range(4):
                    stt(y_.t[:, j, :], y_.t[:, j, :], r_.t[:, j:j + 1], fgb.t[:], ALU.mult, ALU.mult, [y_.p[j], r_.b, fgb.b], [y_.p[j]])
            wr = [dbuf(ykey, bi)] if ykey else []
            P.dma("sp", y_dst[tok0:tok0 + 512, :].rearrange("(j p) c -> p j c", p=128), y_.t[:], y_.p, wr, y_.b)
        P.barrier()

    import os
    stop = int(os.environ.get("MK_STOP", "1000"))
    nph = [0]

    skip = set(x for x in os.environ.get("MK_SKIP", "").split(",") if x)

    def go(name=""):
        nph[0] += 1
        return nph[0] <= stop and name not in skip

    if go():
        setup()
    seqs = [(SS, xs_d, ys_d, 0, gts_bf, "gts"), (SP, xp_d, yp_d, 1, gtp_bf, "gtp")]
    if os.environ.get("MK_ORDER", "sp") == "ps":
        seqs = seqs[::-1]
    for (S, x_in, y_out, col, gt_bf, gtkey) in seqs:
        for l in range(DEPTH):
            x_src, xkey = (x_in, None) if l == 0 else (Y1, "Y1")
            final = (l == DEPTH - 1)
            if go("p1"):
                phase1(S, x_src, xkey, l, col)
            if go("lru"):
                lru(S, l)
            if go("fou"):
                fourier(S, l, gt_bf, gtkey)
            y_dst, ykey = (y_out, None) if final else (Y1, "Y1")
            if go("p3"):
                phase3(S, x_src, xkey, l, col, y_dst, ykey, final, shard=(final and col == 1))
    P.emit()
    return nc, P


def _consts(SS, SP):
    c = np.arange(256)[:, None].astype(np.float64)
    m = np.arange(256)[None, :].astype(np.float64)
    th = 2 * np.pi * c * m / 256.0
    cs = np.concatenate([np.cos(th), -np.sin(th)], axis=1).astype(np.float32)
    s1 = np.arange(128)[:, None].astype(np.float64)
    k1 = np.arange(128)[None, :].astype(np.float64)
    th = 2 * np.pi * s1 * k1 / 128.0
    f3 = np.stack([np.cos(th), np.sin(th)], axis=1).astype(np.float32)

    def gtab(S):
        N2 = S // 128
        J = max(1, 128 // N2)
        NSTEP = 128 // J
        M = J * N2
        SBLK = min(16, NSTEP)
        nblk = NSTEP // SBLK
        out = np.zeros((nblk, M, SBLK, 3, M), np.float32)
        s2 = np.arange(N2, dtype=np.float64)[:, None]
        k2 = np.arange(N2, dtype=np.float64)[None, :]
        for pi in range(NSTEP):
            for j in range(J):
                s1_ = pi + NSTEP * j
                ph = (k2 * (s1_ + 128.0 * s2)) % S
                th_ = 2 * np.pi * ph / S
                fr, fi = np.cos(th_), -np.sin(th_)
                blk, sl = pi // SBLK, pi % SBLK
                out[blk, j:M:J, sl, 0, j * N2:(j + 1) * N2] = fr
                out[blk, j:M:J, sl, 1, j * N2:(j + 1) * N2] = fi
                out[blk, j:M:J, sl, 2, j * N2:(j + 1) * N2] = -fi
        return out

    return cs, f3, gtab(SS), gtab(SP)


def _pack_vec(norm_g, b_ada, conv_w, conv_b, b_rg, lam):
    def pm(a):
        a = np.asarray(a, np.float32)
        lead = int(np.prod(a.shape[:-1])) if a.ndim > 1 else 1
        nch = a.shape[-1] // 128
        return a.reshape(lead, nch, 128).transpose(2, 0, 1).reshape(128, lead * nch)
    vec = np.concatenate([pm(norm_g), pm(b_ada), pm(conv_w), pm(conv_b), pm(b_rg), pm(lam)], axis=1)
    assert vec.shape == (128, V_N), vec.shape
    return np.ascontiguousarray(vec)


_CACHE = {}


def run(x_prompt, x_sample, c_prompt, c_sample, norm_g, w_ada, b_ada, w_in, conv_w, conv_b,
        w_rg, b_rg, lam, w_a_out, w_b_out, w_o, final_g, ncores=8):
    SS, SP = x_sample.shape[1], x_prompt.shape[1]
    key = (SS, SP)
    if key not in _CACHE:
        _CACHE[key] = build(SS, SP)[0]
    nc = _CACHE[key]
    cs, f3, gts, gtp = _consts(SS, SP)
    vec = _pack_vec(norm_g, b_ada, conv_w, conv_b, b_rg, lam)
    fgb = np.ascontiguousarray(np.broadcast_to(np.asarray(final_g, np.float32)[None, :], (128, D)))
    ident = np.eye(128, dtype=np.float32)
    f32 = lambda a: np.ascontiguousarray(np.asarray(a, np.float32))
    common = {"vec": vec, "fgb": fgb, "w_ada": f32(w_ada), "w_in": f32(w_in), "w_rg": f32(w_rg),
              "w_a_out": f32(w_a_out), "w_b_out": f32(w_b_out), "w_o": f32(w_o), "ident": ident,
              "cs": cs, "f3": f3, "gts": gts, "gtp": gtp, "xp": f32(x_prompt[0])}
    cp_ = np.asarray(c_prompt, np.float32)[0].reshape(NCH, 128).T
    in_maps = []
    for i in range(ncores):
        csm = np.asarray(c_sample, np.float32)[i].reshape(NCH, 128).T
        cvv = np.ascontiguousarray(np.stack([csm, cp_], axis=2))
        m = dict(common)
        m["xs"] = f32(x_sample[i])
        m["cv"] = cvv
        in_maps.append(m)
    res = run_bass_kernel_spmd(nc, in_maps, core_ids=list(range(ncores)))
    y_sample = np.stack([np.asarray(res.results[i]["ys"], np.float32) for i in range(ncores)], axis=0)
    y_prompt = np.concatenate([np.asarray(res.results[i]["yp"], np.float32) for i in range(ncores)], axis=0)[None]
    return y_prompt, y_sample


def kernel(**inputs):
    return run(**inputs)
```

```python
import numpy as np
import concourse.bass as bass
import concourse.mybir as mybir
from concourse.bass_utils import run_bass_kernel_spmd

F32 = mybir.dt.float32
BF16 = mybir.dt.bfloat16
AF = mybir.ActivationFunctionType
ALU = mybir.AluOpType

D = 1024
NCORES = 8
DEPTH = 2
NCH = 8
EPS = 1e-6
EPOCH = 16000
DMA_EPOCH = 1000


class Buf:
    __slots__ = ("name", "w", "r", "slot", "excl")

    def __init__(self, name, excl=False):
        self.name = name
        self.w = None
        self.r = {}
        self.slot = None
        self.excl = excl


class Slot:
    __slots__ = ("sem", "count", "nobar")

    def __init__(self, sem):
        self.sem = sem
        self.count = 0
        self.nobar = False


class Prog:
    ENGS = ("pe", "act", "dve", "pool", "sp")

    def __init__(self, nc):
        self.nc = nc
        self.ops = {e: [] for e in self.ENGS}
        self.seq = {e: 0 for e in self.ENGS}
        self.esems = {e: [] for e in self.ENGS}
        self.known = {e: {} for e in self.ENGS}
        self.nsem = 0
        self.slots = []
        self.free = []
        self.ninst = 0

    def _newsem(self, name):
        self.nsem += 1
        return self.nc.alloc_semaphore(name)

    def _esem(self, e, k):
        ep = k // EPOCH
        while len(self.esems[e]) <= ep:
            self.esems[e].append(self._newsem(f"s_{e}_{len(self.esems[e])}"))
        return self.esems[e][ep], (k % EPOCH) + 1

    def _slot(self, b):
        if b.slot is None:
            if self.free:
                b.slot = self.free.pop()
            else:
                b.slot = Slot(self._newsem(f"d_{len(self.slots)}"))
                self.slots.append(b.slot)
        return b.slot

    def release(self, b):
        if b.slot is not None:
            if b.slot.count < 3800:
                self.free.append(b.slot)
            b.slot = None

    def _wait(self, e, ev):
        if ev is None:
            return
        if ev[0] == "eng":
            _, f, k = ev
            if f == e and e == "pe":
                return
            key = ("eng", f)
            if self.known[e].get(key, -1) >= k:
                return
            self.known[e][key] = k
            sem, val = self._esem(f, k)
        else:
            _, slot, cnt = ev
            key = ("dma", id(slot))
            if self.known[e].get(key, 0) >= cnt:
                return
            self.known[e][key] = cnt
            sem, val = slot.sem, 16 * cnt
        self.ninst += 1
        self.ops[e].append(lambda eng, sem=sem, val=val: eng.wait_ge(sem, val))

    def _deps(self, e, reads, writes):
        for b in reads:
            self._wait(e, b.w)
        for b in writes:
            self._wait(e, b.w)
            for ev in b.r.values():
                self._wait(e, ev)

    @staticmethod
    def _mark(ev, key, reads, writes):
        for b in reads:
            b.r[key] = ev
        for b in writes:
            b.w = ev
            b.r = {}

    def op(self, e, fn, reads=(), writes=(), inc=True):
        if any(b.excl for b in reads):
            writes = list(writes) + [b for b in reads if b.excl]
            reads = [b for b in reads if not b.excl]
        self._deps(e, reads, writes)
        k = self.seq[e]
        self.ninst += 1
        if inc:
            self.seq[e] += 1
            sem, _ = self._esem(e, k)
            self.ops[e].append(lambda eng, fn=fn, sem=sem: fn(eng).then_inc(sem, 1))
        else:
            self.ops[e].append(lambda eng, fn=fn: fn(eng))
        ev = ("eng", e, k)
        self._mark(ev, ("eng", e), reads, writes)
        return ev

    def dma(self, q, out, in_, reads, writes, sb, **kw):
        self._deps(q, reads, writes)
        slot = self._slot(sb)
        if slot.count > 0:
            self._wait(q, ("dma", slot, slot.count))
        slot.count += 1
        cnt = slot.count
        sem = slot.sem
        self.ninst += 1
        self.ops[q].append(
            lambda eng, out=out, in_=in_, sem=sem, kw=kw: eng.dma_start(out=out, in_=in_, **kw).then_inc(sem, 16))
        ev = ("dma", slot, cnt)
        self._mark(ev, ("dma", id(slot)), reads, writes)
        return ev

    def dma_fn(self, q, fn, reads, writes, sb):
        self._deps(q, reads, writes)
        slot = self._slot(sb)
        if slot.count > 0:
            self._wait(q, ("dma", slot, slot.count))
        slot.count += 1
        cnt = slot.count
        sem = slot.sem
        self.ninst += 1
        self.ops[q].append(lambda eng, fn=fn, sem=sem: fn(eng).then_inc(sem, 16))
        ev = ("dma", slot, cnt)
        self._mark(ev, ("dma", id(slot)), reads, writes)
        return ev

    def barrier(self, final=False):
        evs = []
        for f in self.ENGS:
            if self.seq[f] > 0:
                evs.append(("eng", f, self.seq[f] - 1))
        for sl in self.slots:
            if sl.count > 0 and (final or not sl.nobar):
                evs.append(("dma", sl, sl.count))
        for e in self.ENGS:
            for ev in evs:
                self._wait(e, ev)

    def emit(self):
        nc = self.nc
        self.barrier(final=True)
        with nc.Block() as block:
            @block.tensor
            def _(eng):
                for t in self.ops["pe"]:
                    t(eng)

            @block.scalar
            def _(eng):
                for t in self.ops["act"]:
                    t(eng)

            @block.vector
            def _(eng):
                for t in self.ops["dve"]:
                    t(eng)

            @block.gpsimd
            def _(eng):
                for t in self.ops["pool"]:
                    t(eng)

            @block.sync
            def _(eng):
                for t in self.ops["sp"]:
                    t(eng)


class Tl:
    def __init__(self, t, name, nparts=0):
        self.t = t
        self.b = Buf(name)
        self.p = [Buf(f"{name}_{i}") for i in range(nparts)]


V_NG = 0
V_BADA = 16
V_CW = 64
V_CB = 128
V_BRG = 144
V_LAM = 208
V_N = 240


def build(SS, SP):
    nc = bass.Bass("TRN2", target_bir_lowering=False)
    P = Prog(nc)
    SMAX = max(SS, SP)

    def din(name, shape, dt=F32):
        return nc.dram_tensor(name, list(shape), dt, kind="ExternalInput").ap()

    def dout(name, shape, dt=F32):
        return nc.dram_tensor(name, list(shape), dt, kind="ExternalOutput").ap()

    def dscr(name, shape, dt=BF16):
        return nc.dram_tensor(name, list(shape), dt).ap()

    xs_d = din("xs", [SS, D]); xp_d = din("xp", [SP, D])
    cv_d = din("cv", [128, NCH, 2]); vec_d = din("vec", [128, V_N]); fgb_d = din("fgb", [128, D])
    w_ada_d = din("w_ada", [DEPTH, D, 3 * D]); w_in_d = din("w_in", [DEPTH, D, 6 * D])
    w_rg_d = din("w_rg", [DEPTH, 2, 2, NCH, 128, 128])
    w_ao_d = din("w_a_out", [DEPTH, D, D]); w_bo_d = din("w_b_out", [DEPTH, D, D]); w_o_d = din("w_o", [DEPTH, D, D])
    ident_d = din("ident", [128, 128]); cs_d = din("cs", [256, 512]); f3_d = din("f3", [128, 2, 128])
    N2S, N2P = SS // 128, SP // 128
    def gshape(S_):
        N2_ = S_ // 128
        J_ = max(1, 128 // N2_)
        NST_ = 128 // J_
        SB_ = min(16, NST_)
        return [NST_ // SB_, J_ * N2_, SB_, 3, J_ * N2_]
    gts_d = din("gts", gshape(SS)); gtp_d = din("gtp", gshape(SP))
    ys_d = dout("ys", [SS, D]); yp_d = dout("yp", [SP // NCORES, D])

    w_in_bf = dscr("w_in_bf", [DEPTH, 12, 128, NCH, 512])
    w_ao_bf = dscr("w_ao_bf", [DEPTH, 128, NCH, D]); w_bo_bf = dscr("w_bo_bf", [DEPTH, 128, NCH, D])
    w_o_bf = dscr("w_o_bf", [DEPTH, 128, NCH, D])
    w_rg_bf = dscr("w_rg_bf", [DEPTH, 128, 2, 2, NCH, 128])
    gts_bf = dscr("gts_bf", gshape(SS)); gtp_bf = dscr("gtp_bf", gshape(SP))
    XA = dscr("XA", [NCH, 128, SMAX]); SGA = dscr("SGA", [NCH, 128, SMAX])
    XB = dscr("XB", [NCH, 128, SMAX]); SGB = dscr("SGB", [NCH, 128, SMAX])
    SM = dscr("SM", [16, 128, SMAX]); AT = dscr("AT", [NCH, 128, SMAX]); BT = dscr("BT", [NCH, 128, SMAX])
    SML = dscr("SML", [16, 128, SMAX // NCORES])
    FB = dscr("FB", [4, SMAX // 128, 128, 512])
    Y1 = dscr("Y1", [SMAX, D], F32)
    DB = {}

    def dbuf(*key):
        if key not in DB:
            DB[key] = Buf("d" + "_".join(str(k) for k in key))
        return DB[key]

    class Arena:
        def __init__(self, base, limit):
            self.base, self.limit, self.cur, self.n, self.live = base, limit, base, 0, []

        def reset(self):
            self.cur = self.base
            for tl in self.live:
                P.release(tl.b)
                for pb_ in tl.p:
                    P.release(pb_)
            self.live = []

        def alloc(self, name, shape, dt, nparts=0):
            esz = 4 if dt == F32 else 2
            nbytes = int(np.prod(shape[1:])) * esz
            nbytes = (nbytes + 63) // 64 * 64
            assert self.cur + nbytes <= self.limit, (name, self.cur, nbytes, self.limit)
            self.n += 1
            t = nc.alloc_sbuf_tensor_at(f"{name}_{self.n}", list(shape), dt, offset=self.cur)
            self.cur += nbytes
            tl = Tl(t, f"{name}_{self.n}", nparts)
            self.live.append(tl)
            return tl

    PERS = Arena(16640, 33024)
    AR = Arena(33024, 229000)

    psum = nc.alloc_psum_tensor("psum_all", [128, 4096], F32)
    PB = [Buf(f"psb{i}", excl=True) for i in range(8)]

    def bank(i, n=512, p=128):
        return psum[0:p, i * 512:i * 512 + n]

    def act(out, in_, func, reads, writes, scale=1.0, bias=None, accum_out=None):
        kw = {"scale": scale}
        if bias is not None:
            kw["bias"] = bias
        if accum_out is not None:
            kw["accum_out"] = accum_out
        return P.op("act", lambda e: e.activation(out=out, in_=in_, func=func, **kw), reads, writes)

    def mm(out, lhsT, rhs, start, stop, reads, writes, inc):
        return P.op("pe", lambda e: e.matmul(out, lhsT=lhsT, rhs=rhs, start=start, stop=stop), reads, writes, inc=inc)

    def ts(eng, out, in0, s1, s2, op0, op1, reads, writes):
        return P.op(eng, lambda e: e.tensor_scalar(out=out, in0=in0, scalar1=s1, scalar2=s2, op0=op0, op1=op1), reads, writes)

    def tt(eng, out, in0, in1, op, reads, writes):
        return P.op(eng, lambda e: e.tensor_tensor(out=out, in0=in0, in1=in1, op=op), reads, writes)

    def stt(out, in0, scalar, in1, op0, op1, reads, writes):
        return P.op("dve", lambda e: e.scalar_tensor_tensor(out=out, in0=in0, scalar=scalar, in1=in1, op0=op0, op1=op1), reads, writes)

    def cp(eng, out, in_, reads, writes):
        return P.op(eng, lambda e: e.tensor_copy(out=out, in_=in_), reads, writes)

    def ld(out, in_, reads, tl, q="sp", wr=None):
        return P.dma(q, out, in_, reads, wr if wr is not None else [tl.b], tl.b)

    def st(out, in_, rd, tl, dwrites, q="sp"):
        return P.dma(q, out, in_, rd, dwrites, tl.b)

    ident_f = PERS.alloc("ident_f", [128, 128], F32)
    ident_b = PERS.alloc("ident_b", [128, 128], BF16)
    ones_f = PERS.alloc("ones_f", [128, 128], F32)
    vec = PERS.alloc("vec", [128, V_N], F32)
    cv = PERS.alloc("cv", [128, NCH, 2], F32)
    scv = PERS.alloc("scv", [128, NCH, 2], F32)
    hbrg = PERS.alloc("hbrg", [128, 64], F32)
    kk = PERS.alloc("kk", [128, 32], F32)
    kh = PERS.alloc("kh", [128, 32], F32)
    mod = PERS.alloc("mod", [128, DEPTH, 24, 2], F32)
    gm = PERS.alloc("gm", [128, DEPTH, NCH, 2], F32)
    one1 = PERS.alloc("one1", [128, 1], F32)
    half1 = PERS.alloc("half1", [128, 1], F32)
    cs_b = PERS.alloc("cs_b", [128, 2, 512], BF16)
    f3_b = PERS.alloc("f3_b", [128, 2, 128], BF16)
    fgb = PERS.alloc("fgb", [128, D], F32)
    tmpv = [PERS.alloc(f"tmpv{i}", [128, 32], F32) for i in range(4)]

    def setup():
        ld(ident_f.t[:], ident_d, [], ident_f)
        ld(vec.t[:], vec_d, [], vec)
        ld(cv.t[:], cv_d, [], cv)
        ld(fgb.t[:], fgb_d, [], fgb)
        ld(ident_b.t[:], ident_d, [], ident_b, q="pool")
        ld(cs_b.t[:], cs_d.rearrange("(cc p) n -> p cc n", p=128), [], cs_b, q="pool")
        ld(f3_b.t[:], f3_d, [], f3_b, q="pool")
        P.op("dve", lambda e: e.memset(ones_f.t[:], 1.0), [], [ones_f.b])
        P.op("dve", lambda e: e.memset(one1.t[:], 1.0), [], [one1.b])
        P.op("dve", lambda e: e.memset(half1.t[:], 0.5), [], [half1.b])
        wsems = [Buf(f"wconv{i}") for i in range(8)]
        wn = [0]

        class _W:
            pass

        def wsem_next():
            wn[0] += 1
            b_ = wsems[wn[0] % 8]
            P._slot(b_).nobar = True
            return b_
        for l in range(DEPTH):
            for g in range(12):
                src = w_in_d[l, :, g * 512:(g + 1) * 512].rearrange("(ic p) n -> p ic n", p=128)
                P.dma("pool", w_in_bf[l, g], src, [], [dbuf("w_in", l, g)], wsem_next())
            for nm, sd, dd in (("w_ao", w_ao_d, w_ao_bf), ("w_bo", w_bo_d, w_bo_bf), ("w_o", w_o_d, w_o_bf)):
                for h in range(2):
                    src = sd[l, :, h * 512:(h + 1) * 512].rearrange("(ic p) n -> p ic n", p=128)
                    P.dma("pool", dd[l, :, :, h * 512:(h + 1) * 512], src, [], [dbuf(nm, l, h)], wsem_next())
            for d in range(2):
                for g in range(2):
                    src = w_rg_d[l, d, g].rearrange("h i j -> i h j")
                    P.dma("pool", w_rg_bf[l, :, d, g], src, [], [dbuf("w_rg", l, d, g)], wsem_next())
        for blk in range(gshape(SS)[0]):
            P.dma("pool", gts_bf[blk], gts_d[blk], [], [dbuf("gts", blk)], wsem_next())
        for blk in range(gshape(SP)[0]):
            P.dma("pool", gtp_bf[blk], gtp_d[blk], [], [dbuf("gtp", blk)], wsem_next())

        v = vec.t
        ts("dve", hbrg.t[:], v[:, V_BRG:V_BRG + 64], 0.5, None, ALU.mult, ALU.bypass, [vec.b], [hbrg.b])
        e_, z_, z2_, acc_ = tmpv
        act(e_.t[:], v[:, V_LAM:V_LAM + 32], AF.Exp, [vec.b], [e_.b], scale=-1.0)
        ts("dve", z_.t[:], e_.t[:], 2.0, None, ALU.add, ALU.bypass, [e_.b], [z_.b])
        P.op("dve", lambda e: e.reciprocal(out=z_.t[:], in_=z_.t[:]), [z_.b], [z_.b])
        tt("dve", z_.t[:], z_.t[:], e_.t[:], ALU.mult, [z_.b, e_.b], [z_.b])
        tt("dve", z2_.t[:], z_.t[:], z_.t[:], ALU.mult, [z_.b], [z2_.b])
        ts("dve", acc_.t[:], z2_.t[:], 1.0 / 11.0, 1.0 / 9.0, ALU.mult, ALU.add, [z2_.b], [acc_.b])
        for cst in (1.0 / 7.0, 1.0 / 5.0, 1.0 / 3.0, 1.0):
            tt("dve", acc_.t[:], acc_.t[:], z2_.t[:], ALU.mult, [acc_.b, z2_.b], [acc_.b])
            ts("dve", acc_.t[:], acc_.t[:], cst, None, ALU.add, ALU.bypass, [acc_.b], [acc_.b])
        tt("dve", acc_.t[:], acc_.t[:], z_.t[:], ALU.mult, [acc_.b, z_.b], [acc_.b])
        ts("dve", kk.t[:], acc_.t[:], -16.0, None, ALU.mult, ALU.bypass, [acc_.b], [kk.b])
        ts("dve", kh.t[:], acc_.t[:], -8.0, None, ALU.mult, ALU.bypass, [acc_.b], [kh.b])

        act(scv.t[:], cv.t[:], AF.Silu, [cv.b], [scv.b])
        AR.reset()
        wada = [AR.alloc("wada", [128, NCH, D], F32) for _ in range(2)]
        n = 0
        for l in range(DEPTH):
            for third in range(3):
                wt = wada[n % 2]; n += 1
                for h in range(2):
                    ld(wt.t[:, :, h * 512:(h + 1) * 512],
                       w_ada_d[l, :, third * D + h * 512: third * D + (h + 1) * 512].rearrange("(ic p) n -> p ic n", p=128),
                       [], wt)
                for jj in range(8):
                    j = third * 8 + jj
                    for ic in range(NCH):
                        mm(psum[:, 2 * j:2 * j + 2], wt.t[:, ic, jj * 128:(jj + 1) * 128], scv.t[:, ic, :],
                           ic == 0, ic == NCH - 1, [wt.b, scv.b], [PB[0]], inc=(ic == NCH - 1))
            for j in range(24):
                ts("dve", mod.t[:, l, j, :], psum[:, 2 * j:2 * j + 2], v[:, V_BADA + l * 24 + j:V_BADA + l * 24 + j + 1], None,
                   ALU.add, ALU.bypass, [PB[0], vec.b], [mod.b])
            for c in range(NCH):
                ts("dve", gm.t[:, l, c, :], mod.t[:, l, 8 + c, :], 1.0, v[:, V_NG + l * 8 + c:V_NG + l * 8 + c + 1],
                   ALU.add, ALU.mult, [mod.b, vec.b], [gm.b])
        P.barrier()

    def phase1(S, x_src, xkey, l, col, glist=None, shard=False):
        AR.reset()
        glist = list(range(12)) if glist is None else list(glist)
        NG = len(glist)
        PSH = S // NCORES
        T = PSH if shard else S
        SBK = min(2048, T)
        NB = SBK // 512
        nsb = T // SBK
        pid_cache = {}
        xt = [AR.alloc("xt", [128, 4, D], F32) for _ in range(2)]
        xn = [AR.alloc("xn", [128, 4, D], F32, nparts=4) for _ in range(2)]
        hT = [AR.alloc("hT", [128, NCH, SBK], BF16, nparts=NCH * NB) for _ in range(2)]
        win = [AR.alloc("win", [128, NCH, 512], BF16) for _ in range(3)]
        stg = [AR.alloc("stg", [128, SBK], BF16, nparts=NB) for _ in range(6)]
        junk = AR.alloc("junk", [128, D], BF16)
        ss = [AR.alloc("ss", [128, 4], F32) for _ in range(2)]
        rs = [AR.alloc("rs", [128, 4], F32) for _ in range(2)]
        v = vec.t
        cnt = {"blk": 0, "tb": 0, "pb": 0, "w": 0, "stg": 0}

        def prep_nonpe(sb, blk):
            i = cnt["blk"]
            cnt["blk"] += 1
            x_, n_, s_, r_ = xt[i % 2], xn[i % 2], ss[i % 2], rs[i % 2]
            tok0 = sb * SBK + blk * 512
            if shard:
                def tb(eng):
                    if "pid" not in pid_cache:
                        pid_cache["pid"] = eng.partition_id()
                    return bass.ds(pid_cache["pid"] * PSH + tok0, 512)
                rd = [dbuf(xkey, i_) for i_ in range(S // 512)] if xkey else []
                P.dma_fn("sp", lambda eng: eng.dma_start(
                    out=x_.t[:], in_=x_src[tb(eng), :].rearrange("(j p) c -> p j c", p=128)), rd, [x_.b], x_.b)
            else:
                rd = [dbuf(xkey, tok0 // 512)] if xkey else []
                ld(x_.t[:], x_src[tok0:tok0 + 512, :].rearrange("(j p) c -> p j c", p=128), rd, x_)
            for j in range(4):
                act(junk.t[:], x_.t[:, j, :], AF.Square, [x_.b], [junk.b, s_.b], accum_out=s_.t[:, j:j + 1])
            ts("dve", r_.t[:], s_.t[:], 1.0 / D, EPS, ALU.mult, ALU.add, [s_.b], [r_.b])
            act(r_.t[:], r_.t[:], AF.Sqrt, [r_.b], [r_.b])
            P.op("dve", lambda e: e.reciprocal(out=r_.t[:], in_=r_.t[:]), [r_.b], [r_.b])
            for j in range(4):
                ts("pool", n_.t[:, j, :], x_.t[:, j, :], r_.t[:, j:j + 1], 1.0, ALU.mult, ALU.mult, [x_.b, r_.b], [n_.p[j]])
            return n_

        def prep_pe(sb, blk, n_):
            h_ = hT[sb % 2]
            for c in range(NCH):
                bk = cnt["tb"] % 2
                cnt["tb"] += 1
                for j in range(4):
                    P.op("pe", lambda e, j=j, c=c, bk=bk: e.transpose(psum[:, bk * 512 + j * 128: bk * 512 + (j + 1) * 128],
                                                                       n_.t[:, j, c * 128:(c + 1) * 128], ident_f.t[:]),
                         [n_.p[j], ident_f.b], [PB[bk]], inc=(j == 3))
                act(h_.t[:, c, blk * 512:(blk + 1) * 512], bank(bk), AF.Identity, [PB[bk], gm.b, mod.b], [h_.p[c * NB + blk]],
                    scale=gm.t[:, l, c, col:col + 1], bias=mod.t[:, l, c, col:col + 1])

        def jc_info(jc):
            if jc < 8:
                return "copy", XA, "XA", jc
            if jc < 16:
                return "silu", SGA, "SGA", jc - 8
            if jc < 24:
                return "copy", XB, "XB", jc - 16
            if jc < 32:
                return "silu", SGB, "SGB", jc - 24
            if shard:
                return "sigm", SML, "SML", jc - 32
            return "sigm", SM, "SM", jc - 32

        def load_w(idx):
            if idx < nsb * NG:
                g_ = glist[idx % NG]
                w__ = win[idx % 3]
                ld(w__.t[:], w_in_bf[l, g_], [dbuf("w_in", l, g_)], w__)

        def proj_group(sb, gi):
            g = glist[gi]
            h_ = hT[sb % 2]
            idx = sb * NG + gi
            w_ = win[idx % 3]
            if idx == 0:
                load_w(0)
                load_w(1)
            load_w(idx + 2)
            for jcl in range(4):
                jc = g * 4 + jcl
                kind, dten, dname, ch = jc_info(jc)
                s_ = stg[cnt["stg"] % 6]
                cnt["stg"] += 1
                for tile in range(NB):
                    bk = 2 + cnt["pb"] % 6
                    cnt["pb"] += 1
                    for ic in range(NCH):
                        mm(bank(bk), w_.t[:, ic, jcl * 128:(jcl + 1) * 128], h_.t[:, ic, tile * 512:(tile + 1) * 512],
                           ic == 0, ic == NCH - 1, [w_.b, h_.p[ic * NB + tile]], [PB[bk]], inc=(ic == NCH - 1))
                    o = s_.t[:, tile * 512:(tile + 1) * 512]
                    if kind == "copy":
                        cp("dve", o, bank(bk), [PB[bk]], [s_.p[tile]])
                    elif kind == "silu":
                        act(o, bank(bk), AF.Silu, [PB[bk]], [s_.p[tile]])
                    else:
                        act(o, bank(bk), AF.Sigmoid, [PB[bk]], [s_.p[tile]])
                P.dma("sp", dten[ch, :, sb * SBK:(sb + 1) * SBK], s_.t[:], s_.p, [dbuf(dname, ch, sb)], s_.b)

        for blk in range(NB):
            n_ = prep_nonpe(0, blk)
            prep_pe(0, blk, n_)
        gper = max(1, NG // NB)
        for sb in range(nsb):
            cur = None
            for g in range(NG):
                if sb + 1 < nsb and g % gper == 0 and g // gper < NB:
                    cur = (g // gper, prep_nonpe(sb + 1, g // gper))
                proj_group(sb, g)
                if sb + 1 < nsb and g % gper == gper - 1 and g // gper < NB:
                    prep_pe(sb + 1, cur[0], cur[1])
        P.barrier()

    def lru(S, l):
        AR.reset()
        SL = min(1024, S)
        NT = SL // 512
        nsl = S // SL
        SBK = min(2048, S)
        hb = AR.alloc("hb", [128, S], F32, nparts=nsl)
        wrg = AR.alloc("wrg", [128, 2, 2, NCH, 128], BF16)
        dk = [AR.alloc("dk", [128, 4, 128], BF16) for _ in range(2)]
        xap = [AR.alloc("xap", [128, SL + 4], BF16) for _ in range(3)]
        ring = lambda nm, dt: [AR.alloc(nm, [128, SL], dt) for _ in range(2)]
        xcb = [AR.alloc("xcb", [128, SL], BF16) for _ in range(3)]
        tr, ti, ip, bb = (ring(n, F32) for n in ("tr", "ti", "ip", "bb"))
        ring3 = lambda nm: [AR.alloc(nm, [128, SL], F32) for _ in range(3)]
        aa, m_, u_, hh = ring3("aa"), ring3("m"), ring3("u"), ring3("hh")
        s_ = ring("s", F32)
        sga = [AR.alloc("sga", [128, SL], BF16) for _ in range(6)]
        sA = ring("sA", BF16)
        v = vec.t
        for d in range(2):
            for g in range(2):
                ld(wrg.t[:, d, g], w_rg_bf[l, :, d, g], [dbuf("w_rg", l, d, g)], wrg)
        slabs = []
        for hd in range(NCH):
            for d in (1, 0):
                order = range(nsl - 1, -1, -1) if d == 1 else range(nsl)
                for sl in order:
                    slabs.append((hd, d, sl))
        n = len(slabs)
        st_ = {"prev_h": None}
        fidx = []
        nf = 0
        for (_h, _d, _s) in slabs:
            fidx.append(nf)
            if _d == 0:
                nf += 1

        def ldx(i):
            if i >= n:
                return
            hd, d, sl = slabs[i]
            x_ = xap[i % 3]
            t0 = sl * SL
            lo, hi = max(0, t0 - 2), min(S, t0 + SL + 2)
            off = lo - (t0 - 2)
            if t0 == 0:
                P.op("pool", lambda e, x_=x_: e.memset(x_.t[:, 0:2], 0.0), [], [x_.b])
            if t0 + SL == S:
                P.op("pool", lambda e, x_=x_: e.memset(x_.t[:, SL + 2:SL + 4], 0.0), [], [x_.b])
            rd = [dbuf("XA", hd, sb) for sb in range(lo // SBK, (hi - 1) // SBK + 1)]
            ld(x_.t[:, off:off + (hi - lo)], XA[hd, :, lo:hi], rd, x_)

        def stage_a(i):
            hd, d, sl = slabs[i]
            dk_ = dk[hd % 2]
            if d == 1 and sl == nsl - 1:
                for k in range(4):
                    col = V_CW + (l * 4 + k) * 8 + hd
                    ts("dve", dk_.t[:, k, :], ident_b.t[:], v[:, col:col + 1], None, ALU.mult, ALU.bypass,
                       [ident_b.b, vec.b], [dk_.b])
            ldx(i + 2)
            x_ = xap[i % 3]
            c0 = (i % 2) * 2
            for tile in range(NT):
                for k in range(4):
                    mm(bank(c0 + tile), dk_.t[:, k, :], x_.t[:, tile * 512 + k: tile * 512 + k + 512],
                       k == 0, k == 3, [dk_.b, x_.b], [PB[c0 + tile]], inc=(k == 3))
            cps = psum[:, c0 * 512:c0 * 512 + SL]
            cpb = [PB[c0 + t_] for t_ in range(NT)]
            cbcol = v[:, V_CB + l * 8 + hd:V_CB + l * 8 + hd + 1]
            xcb_ = xcb[i % 3]
            ts("dve", xcb_.t[:], cps, cbcol, None, ALU.add, ALU.bypass, cpb + [vec.b], [xcb_.b])
            if d == 0:
                g_ = sga[fidx[i] % 6]
                t0 = sl * SL
                ld(g_.t[:], SGA[hd, :, t0:t0 + SL], [dbuf("SGA", hd, t0 // SBK)], g_)

        def stage_b0(i):
            hd, d, sl = slabs[i]
            kcol = (l * 2 + d) * 8 + hd
            brc = hbrg.t[:, ((l * 2 + d) * 2 + 0) * 8 + hd:((l * 2 + d) * 2 + 0) * 8 + hd + 1]
            bic = hbrg.t[:, ((l * 2 + d) * 2 + 1) * 8 + hd:((l * 2 + d) * 2 + 1) * 8 + hd + 1]
            xcb_ = xcb[i % 3]
            tr_, ti_, ip_ = tr[i % 2], ti[i % 2], ip[i % 2]
            a_, mm_, uu_ = aa[i % 3], m_[i % 3], u_[i % 3]
            for tile in range(NT):
                mm(bank(4 + tile), wrg.t[:, d, 0, hd, :], xcb_.t[:, tile * 512:(tile + 1) * 512], True, True,
                   [wrg.b, xcb_.b], [PB[4 + tile]], inc=True)
                mm(bank(6 + tile), wrg.t[:, d, 1, hd, :], xcb_.t[:, tile * 512:(tile + 1) * 512], True, True,
                   [wrg.b, xcb_.b], [PB[6 + tile]], inc=True)
            gr = psum[:, 4 * 512:4 * 512 + SL]
            gi = psum[:, 6 * 512:6 * 512 + SL]
            grb = [PB[4 + t_] for t_ in range(NT)]
            gib = [PB[6 + t_] for t_ in range(NT)]
            act(tr_.t[:], gr, AF.Tanh, grb + [hbrg.b], [tr_.b], scale=0.5, bias=brc)
            act(ti_.t[:], gi, AF.Tanh, gib + [hbrg.b], [ti_.b], scale=0.5, bias=bic)
            act(a_.t[:], tr_.t[:], AF.Exp, [tr_.b, kh.b], [a_.b], scale=kh.t[:, kcol:kcol + 1], bias=kh.t[:, kcol:kcol + 1])
            act(mm_.t[:], tr_.t[:], AF.Exp, [tr_.b, kk.b], [mm_.b], scale=kk.t[:, kcol:kcol + 1], bias=kk.t[:, kcol:kcol + 1])
            ts("pool", ip_.t[:], ti_.t[:], 1.0, 1.0, ALU.add, ALU.mult, [ti_.b], [ip_.b])
            tt("pool", uu_.t[:], ip_.t[:], xcb_.t[:], ALU.mult, [ip_.b, xcb_.b], [uu_.b])

        def stage_b(i):
            hd, d, sl = slabs[i]
            t0 = sl * SL
            a_, mm_, uu_, b_ = aa[i % 3], m_[i % 3], u_[i % 3], bb[i % 2]
            act(mm_.t[:], mm_.t[:], AF.Sqrt, [mm_.b, one1.b], [mm_.b], scale=-1.0, bias=one1.t[:, 0:1])
            if d == 1 and sl == nsl - 1:
                P.op("pool", lambda e, mm_=mm_: e.memset(mm_.t[:, SL - 1:SL], 1.0), [], [mm_.b])
            if d == 0 and sl == 0:
                P.op("pool", lambda e, mm_=mm_: e.memset(mm_.t[:, 0:1], 1.0), [], [mm_.b])
            stt(b_.t[:], uu_.t[:], 0.5, mm_.t[:], ALU.mult, ALU.mult, [uu_.b, mm_.b], [b_.b])
            if d == 1:
                if sl == nsl - 1:
                    init, ird = 0.0, []
                else:
                    init, ird = hb.t[:, t0 + SL:t0 + SL + 1], [hb.p[sl + 1]]
                P.op("dve", lambda e, t0=t0, a_=a_, b_=b_, init=init: e.tensor_tensor_scan(
                    out=hb.t[:, t0:t0 + SL][:, ::-1], data0=a_.t[:, ::-1], data1=b_.t[:, ::-1], initial=init,
                    op0=ALU.mult, op1=ALU.add), [a_.b, b_.b] + ird, [hb.p[sl]])
            else:
                f = fidx[i]
                h_ = hh[f % 3]
                if sl == 0:
                    init, ird = 0.0, []
                else:
                    prev_h = st_["prev_h"]
                    init, ird = prev_h.t[:, SL - 1:SL], [prev_h.b]
                P.op("dve", lambda e, h_=h_, a_=a_, b_=b_, init=init: e.tensor_tensor_scan(
                    out=h_.t[:], data0=a_.t[:], data1=b_.t[:], initial=init, op0=ALU.mult, op1=ALU.add),
                    [a_.b, b_.b] + ird, [h_.b])
                st_["prev_h"] = h_

        def stage_b2(i):
            hd, d, sl = slabs[i]
            if d != 0:
                return
            t0 = sl * SL
            f = fidx[i]
            h_ = hh[f % 3]
            g_ = sga[f % 6]
            sum_ = s_[f % 2]
            o_ = sA[f % 2]
            tt("pool", sum_.t[:], h_.t[:], hb.t[:, t0:t0 + SL], ALU.add, [h_.b, hb.p[sl]], [sum_.b])
            tt("dve", o_.t[:], sum_.t[:], g_.t[:], ALU.mult, [sum_.b, g_.b], [o_.b])
            st(AT[hd, :, t0:t0 + SL], o_.t[:], [o_.b], o_, [dbuf("AT", hd, sl)])

        ldx(0)
        ldx(1)
        na = [0]

        def ensure_a(upto):
            while na[0] <= upto and na[0] < n:
                stage_a(na[0])
                na[0] += 1

        pend = []

        def flush():
            while pend:
                stage_b2(pend.pop(0))

        i = 0
        while i < n:
            grp = [i, i + 1] if i + 1 < n else [i]
            for j in grp:
                ensure_a(j + 2)
                stage_b0(j)
            for j in grp:
                if slabs[j][1] == 1:
                    flush()
                stage_b(j)
                flush()
                if slabs[j][1] == 0:
                    pend.append(j)
            i += len(grp)
        flush()
        P.barrier()

    def fourier(S, l, gt_bf, gtkey):
        AR.reset()
        N2 = S // 128
        J = max(1, 128 // N2)
        NSTEP = 128 // J
        M = J * N2
        SBLK = min(16, NSTEP)
        SBK = min(2048, S)
        invn = 1.0 / float(np.sqrt(S * 256.0))
        xbg = [AR.alloc("xbg", [128, 2, S], BF16) for _ in range(2)]
        gt = [AR.alloc("gt", [M, SBLK, 3, M], BF16) for _ in range(2)]
        wsb = [AR.alloc("wsb", [M, 512], BF16) for _ in range(3)]
        bst = [AR.alloc("bst", [M, 4, 512], BF16, nparts=4) for _ in range(3)]
        steps = [(g, pi) for g in range(4) for pi in range(NSTEP)]
        n = len(steps)

        def st_a(i):
            g, pi = steps[i]
            x_ = xbg[g % 2]
            if pi == 0:
                for cc in range(2):
                    ld(x_.t[:, cc, :], XB[2 * g + cc, :, 0:S], [dbuf("XB", 2 * g + cc, sb) for sb in range(S // SBK)], x_)
            if pi % SBLK == 0:
                gslot = gt[(i // SBLK) % 2]
                ld(gslot.t[:], gt_bf[pi // SBLK], [dbuf(gtkey, pi // SBLK)], gslot)
            wb = i % 2
            for cc in range(2):
                lhs = x_.t[:, cc, pi:S:NSTEP]
                mm(bank(wb, 512, M), lhs, cs_b.t[:, cc, :], cc == 0, cc == 1, [x_.b, cs_b.b], [PB[wb]], inc=(cc == 1))
            w_ = wsb[i % 3]
            if i % 2 == 0:
                act(w_.t[:], bank(wb, 512, M), AF.Identity, [PB[wb]], [w_.b])
            else:
                cp("dve", w_.t[:], bank(wb, 512, M), [PB[wb]], [w_.b])

        def st_b(i):
            g, pi = steps[i]
            gslot = gt[(i // SBLK) % 2]
            w_ = wsb[i % 3]
            bbk = 2 + i % 2
            sl16 = pi % SBLK
            Fr, Fi, Fin = gslot.t[:, sl16, 0, :], gslot.t[:, sl16, 1, :], gslot.t[:, sl16, 2, :]
            o_r = psum[0:M, bbk * 512: bbk * 512 + 256]
            o_i = psum[0:M, bbk * 512 + 256: bbk * 512 + 512]
            rd = [gslot.b, w_.b]
            mm(o_r, Fr, w_.t[:, 0:256], True, False, rd, [PB[bbk]], inc=False)
            mm(o_r, Fin, w_.t[:, 256:512], False, True, rd, [PB[bbk]], inc=False)
            mm(o_i, Fr, w_.t[:, 256:512], True, False, rd, [PB[bbk]], inc=False)
            mm(o_i, Fi, w_.t[:, 0:256], False, True, rd, [PB[bbk]], inc=True)
            b_ = bst[(i // 4) % 3]
            if i % 2 == 0:
                cp("dve", b_.t[:, pi % 4, :], bank(bbk, 512, M), [PB[bbk]], [b_.p[pi % 4]])
            else:
                act(b_.t[:, pi % 4, :], bank(bbk, 512, M), AF.Identity, [PB[bbk]], [b_.p[pi % 4]])
            if pi % 4 == 3:
                for j in range(J):
                    P.dma("sp", FB[g, 0:N2, j * NSTEP + pi - 3: j * NSTEP + pi + 1, :], b_.t[j * N2:(j + 1) * N2, :, :],
                          b_.p, [dbuf("FB", g, j, pi // 4)], b_.b)

        st_a(0)
        for i in range(n):
            if i + 1 < n:
                st_a(i + 1)
            st_b(i)
        P.barrier()
        AR.reset()
        sgb = AR.alloc("sgb", [128, 2, S], BF16)
        ybt = AR.alloc("ybt", [128, 2, S], BF16, nparts=2 * (N2 // 4))
        bin_ = [AR.alloc("bin", [128, 4, 512], BF16) for _ in range(3)]
        qi = 0
        for g in range(4):
            for mc in range(2):
                ld(sgb.t[:, mc, :], SGB[2 * g + mc, :, 0:S], [dbuf("SGB", 2 * g + mc, sb) for sb in range(S // SBK)], sgb)
            for q in range(N2 // 4):
                b_ = bin_[qi % 3]
                ld(b_.t[:], FB[g, 4 * q:4 * q + 4, :, :].rearrange("k s n -> s k n"), [dbuf("FB", g, j, i) for j in range(J) for i in range(NSTEP // 4)], b_)
                for mc in range(2):
                    bk = 4 + 2 * mc + (qi % 2)
                    for j in range(4):
                        o = psum[:, bk * 512 + j * 128: bk * 512 + (j + 1) * 128]
                        mm(o, b_.t[:, j, mc * 128:(mc + 1) * 128], f3_b.t[:, 0, :], True, False, [b_.b, f3_b.b], [PB[bk]], inc=False)
                        mm(o, b_.t[:, j, 256 + mc * 128:256 + (mc + 1) * 128], f3_b.t[:, 1, :], False, True, [b_.b, f3_b.b], [PB[bk]],
                           inc=(j == 3))
                    pv = bank(bk).rearrange("p (j k) -> p k j", j=4)
                    ov = ybt.t[:, mc, :].rearrange("p (k n) -> p k n", n=N2)[:, :, 4 * q:4 * q + 4]
                    gv = sgb.t[:, mc, :].rearrange("p (k n) -> p k n", n=N2)[:, :, 4 * q:4 * q + 4]
                    stt(ov, pv, invn, gv, ALU.mult, ALU.mult, [PB[bk], sgb.b], [ybt.p[mc * (N2 // 4) + q]])
                qi += 1
            for mc in range(2):
                st(BT[2 * g + mc, :, 0:S], ybt.t[:, mc, :], ybt.p[mc * (N2 // 4):(mc + 1) * (N2 // 4)], ybt, [dbuf("BT", 2 * g + mc)])
        P.barrier()

    def phase3(S, x_src, xkey, l, col, y_dst, ykey, final, shard=False):
        AR.reset()
        SBK = min(2048, S)
        SL = min(1024, S)
        PSH = S // NCORES
        pid_cache = {}
        nblk = (PSH if shard else S) // 512
        wa = AR.alloc("wa", [128, NCH, D], BF16)
        wb = AR.alloc("wb", [128, NCH, D], BF16)
        wog = AR.alloc("wog", [128, NCH, D], BF16)
        dg = AR.alloc("dg", [128, 128], F32)
        at = [AR.alloc("at", [128, NCH, 512], BF16) for _ in range(2)]
        bt = [AR.alloc("bt", [128, NCH, 512], BF16) for _ in range(2)]
        sm = [AR.alloc("sm", [128, 16, 512], BF16) for _ in range(2)]
        xr = [AR.alloc("xr", [128, 4, D], F32, nparts=4) for _ in range(2)]
        uT = [AR.alloc("uT", [128, NCH, 512], BF16, nparts=NCH) for _ in range(2)]
        ua = [AR.alloc("ua", [128, 512], F32) for _ in range(2)]
        ub = [AR.alloc("ub", [128, 512], F32) for _ in range(2)]
        junk = AR.alloc("junk3", [128, D], BF16)
        ss = [AR.alloc("ss3", [128, 4], F32) for _ in range(2)]
        rs = [AR.alloc("rs3", [128, 4], F32) for _ in range(2)]
        for h in range(2):
            ld(wa.t[:, :, h * 512:(h + 1) * 512], w_ao_bf[l, :, :, h * 512:(h + 1) * 512], [dbuf("w_ao", l, h)], wa)
            ld(wb.t[:, :, h * 512:(h + 1) * 512], w_bo_bf[l, :, :, h * 512:(h + 1) * 512], [dbuf("w_bo", l, h)], wb)
            ld(wog.t[:, :, h * 512:(h + 1) * 512], w_o_bf[l, :, :, h * 512:(h + 1) * 512], [dbuf("w_o", l, h)], wog)
        for c in range(NCH):
            ts("dve", dg.t[:], ident_f.t[:], mod.t[:, l, 16 + c, col:col + 1], None, ALU.mult, ALU.bypass,
               [ident_f.b, mod.b], [dg.b])
            mm(psum[:, c * 128:(c + 1) * 128], ones_f.t[:], dg.t[:], True, True, [ones_f.b, dg.b], [PB[c // 4]], inc=True)
        for ic in range(NCH):
            tt("dve", wog.t[:, ic, :], wog.t[:, ic, :], psum[:, 0:D], ALU.mult, [wog.b, PB[0], PB[1]], [wog.b])

        def loads(bi):
            tok0 = bi * 512
            a_, b_, m_, x_ = at[bi % 2], bt[bi % 2], sm[bi % 2], xr[bi % 2]
            if shard:
                def tb(eng):
                    if "pid" not in pid_cache:
                        pid_cache["pid"] = eng.partition_id()
                    return bass.ds(pid_cache["pid"] * PSH + tok0, 512)
                rd_at = [dbuf("AT", hd, sl_) for hd in range(NCH) for sl_ in range(S // SL)]
                rd_bt = [dbuf("BT", c) for c in range(NCH)]
                P.dma_fn("sp", lambda eng: eng.dma_start(out=a_.t[:], in_=AT[:, :, tb(eng)].rearrange("c p t -> p c t")),
                         rd_at, [a_.b], a_.b)
                P.dma_fn("sp", lambda eng: eng.dma_start(out=b_.t[:], in_=BT[:, :, tb(eng)].rearrange("c p t -> p c t")),
                         rd_bt, [b_.b], b_.b)
                SBKL = min(2048, PSH)
                for h in range(2):
                    ld(m_.t[:, h * 8:(h + 1) * 8, :], SML[h * 8:(h + 1) * 8, :, tok0:tok0 + 512].rearrange("c p t -> p c t"),
                       [dbuf("SML", c, tok0 // SBKL) for c in range(h * 8, (h + 1) * 8)], m_)
                rd = [dbuf(xkey, i_) for i_ in range(S // 512)] if xkey else []
                P.dma_fn("sp", lambda eng: eng.dma_start(
                    out=x_.t[:], in_=x_src[tb(eng), :].rearrange("(j p) c -> p j c", p=128)), rd, [x_.b] + x_.p, x_.b)
                return
            ld(a_.t[:], AT[:, :, tok0:tok0 + 512].rearrange("c p t -> p c t"), [dbuf("AT", hd, tok0 // SL) for hd in range(NCH)], a_)
            ld(b_.t[:], BT[:, :, tok0:tok0 + 512].rearrange("c p t -> p c t"), [dbuf("BT", c) for c in range(NCH)], b_)
            for h in range(2):
                ld(m_.t[:, h * 8:(h + 1) * 8, :], SM[h * 8:(h + 1) * 8, :, tok0:tok0 + 512].rearrange("c p t -> p c t"),
                   [dbuf("SM", c, tok0 // SBK) for c in range(h * 8, (h + 1) * 8)], m_)
            rd = [dbuf(xkey, bi)] if xkey else []
            ld(x_.t[:], x_src[tok0:tok0 + 512, :].rearrange("(j p) c -> p j c", p=128), rd, x_, wr=[x_.b] + x_.p)

        loads(0)
        pb = 0
        for bi in range(nblk):
            if bi + 1 < nblk:
                loads(bi + 1)
            a_, b_, m_, x_ = at[bi % 2], bt[bi % 2], sm[bi % 2], xr[bi % 2]
            u_ = uT[bi % 2]
            y_ = x_
            tok0 = bi * 512
            for jc in range(NCH):
                bka = pb % 8; pb += 1
                bkb = pb % 8; pb += 1
                for ic in range(NCH):
                    mm(bank(bka), wa.t[:, ic, jc * 128:(jc + 1) * 128], a_.t[:, ic, :], ic == 0, ic == NCH - 1,
                       [wa.b, a_.b], [PB[bka]], inc=(ic == NCH - 1))
                for ic in range(NCH):
                    mm(bank(bkb), wb.t[:, ic, jc * 128:(jc + 1) * 128], b_.t[:, ic, :], ic == 0, ic == NCH - 1,
                       [wb.b, b_.b], [PB[bkb]], inc=(ic == NCH - 1))
                ua_, ub_ = ua[jc % 2], ub[jc % 2]
                tt("dve", ua_.t[:], bank(bka), m_.t[:, jc, :], ALU.mult, [PB[bka], m_.b], [ua_.b])
                tt("dve", ub_.t[:], bank(bkb), m_.t[:, 8 + jc, :], ALU.mult, [PB[bkb], m_.b], [ub_.b])
                tt("pool", u_.t[:, jc, :], ua_.t[:], ub_.t[:], ALU.add, [ua_.b, ub_.b], [u_.p[jc]])
            for tile in range(4):
                for half in range(2):
                    bk = pb % 8; pb += 1
                    for ic in range(NCH):
                        mm(bank(bk), u_.t[:, ic, tile * 128:(tile + 1) * 128], wog.t[:, ic, half * 512:(half + 1) * 512],
                           ic == 0, ic == NCH - 1, [u_.p[ic], wog.b], [PB[bk]], inc=(ic == NCH - 1))
                    tt("dve", y_.t[:, tile, half * 512:(half + 1) * 512], bank(bk), x_.t[:, tile, half * 512:(half + 1) * 512],
                       ALU.add, [PB[bk], x_.b], [y_.p[tile]])
            if final:
                s_, r_ = ss[bi % 2], rs[bi % 2]
                for j in range(4):
                    act(junk.t[:], y_.t[:, j, :], AF.Square, [y_.p[j]], [junk.b, s_.b], accum_out=s_.t[:, j:j + 1])
                ts("dve", r_.t[:], s_.t[:], 1.0 / D, EPS, ALU.mult, ALU.add, [s_.b], [r_.b])
                act(r_.t[:], r_.t[:], AF.Sqrt, [r_.b], [r_.b])
                P.op("dve", lambda e, r_=r_: e.reciprocal(out=r_.t[:], in_=r_.t[:]), [r_.b], [r_.b])
                for j in range(4):
                    stt(y_.t[:, j, :], y_.t[:, j, :], r_.t[:, j:j + 1], fgb.t[:], ALU.mult, ALU.mult, [y_.p[j], r_.b, fgb.b], [y_.p[j]])
            wr = [dbuf(ykey, bi)] if ykey else []
            P.dma("sp", y_dst[tok0:tok0 + 512, :].rearrange("(j p) c -> p j c", p=128), y_.t[:], y_.p, wr, y_.b)
        P.barrier()

    import os
    stop = int(os.environ.get("MK_STOP", "1000"))
    nph = [0]

    skip = set(x for x in os.environ.get("MK_SKIP", "").split(",") if x)

    def go(name=""):
        nph[0] += 1
        return nph[0] <= stop and name not in skip

    if go():
        setup()
    seqs = [(SS, xs_d, ys_d, 0, gts_bf, "gts"), (SP, xp_d, yp_d, 1, gtp_bf, "gtp")]
    if os.environ.get("MK_ORDER", "sp") == "ps":
        seqs = seqs[::-1]
    for (S, x_in, y_out, col, gt_bf, gtkey) in seqs:
        for l in range(DEPTH):
            x_src, xkey = (x_in, None) if l == 0 else (Y1, "Y1")
            final = (l == DEPTH - 1)
            shard = final and col == 1
            if go("p1"):
                if shard:
                    phase1(S, x_src, xkey, l, col, glist=range(8))
                    phase1(S, x_src, xkey, l, col, glist=range(8, 12), shard=True)
                else:
                    phase1(S, x_src, xkey, l, col)
            if go("lru"):
                lru(S, l)
            if go("fou"):
                fourier(S, l, gt_bf, gtkey)
            y_dst, ykey = (y_out, None) if final else (Y1, "Y1")
            if go("p3"):
                phase3(S, x_src, xkey, l, col, y_dst, ykey, final, shard=shard)
    P.emit()
    return nc, P


def _consts(SS, SP):
    c = np.arange(256)[:, None].astype(np.float64)
    m = np.arange(256)[None, :].astype(np.float64)
    th = 2 * np.pi * c * m / 256.0
    cs = np.concatenate([np.cos(th), -np.sin(th)], axis=1).astype(np.float32)
    s1 = np.arange(128)[:, None].astype(np.float64)
    k1 = np.arange(128)[None, :].astype(np.float64)
    th = 2 * np.pi * s1 * k1 / 128.0
    f3 = np.stack([np.cos(th), np.sin(th)], axis=1).astype(np.float32)

    def gtab(S):
        N2 = S // 128
        J = max(1, 128 // N2)
        NSTEP = 128 // J
        M = J * N2
        SBLK = min(16, NSTEP)
        nblk = NSTEP // SBLK
        out = np.zeros((nblk, M, SBLK, 3, M), np.float32)
        s2 = np.arange(N2, dtype=np.float64)[:, None]
        k2 = np.arange(N2, dtype=np.float64)[None, :]
        for pi in range(NSTEP):
            for j in range(J):
                s1_ = pi + NSTEP * j
                ph = (k2 * (s1_ + 128.0 * s2)) % S
                th_ = 2 * np.pi * ph / S
                fr, fi = np.cos(th_), -np.sin(th_)
                blk, sl = pi // SBLK, pi % SBLK
                out[blk, j:M:J, sl, 0, j * N2:(j + 1) * N2] = fr
                out[blk, j:M:J, sl, 1, j * N2:(j + 1) * N2] = fi
                out[blk, j:M:J, sl, 2, j * N2:(j + 1) * N2] = -fi
        return out

    return cs, f3, gtab(SS), gtab(SP)


def _pack_vec(norm_g, b_ada, conv_w, conv_b, b_rg, lam):
    def pm(a):
        a = np.asarray(a, np.float32)
        lead = int(np.prod(a.shape[:-1])) if a.ndim > 1 else 1
        nch = a.shape[-1] // 128
        return a.reshape(lead, nch, 128).transpose(2, 0, 1).reshape(128, lead * nch)
    vec = np.concatenate([pm(norm_g), pm(b_ada), pm(conv_w), pm(conv_b), pm(b_rg), pm(lam)], axis=1)
    assert vec.shape == (128, V_N), vec.shape
    return np.ascontiguousarray(vec)


_CACHE = {}


def run(x_prompt, x_sample, c_prompt, c_sample, norm_g, w_ada, b_ada, w_in, conv_w, conv_b,
        w_rg, b_rg, lam, w_a_out, w_b_out, w_o, final_g, ncores=8):
    SS, SP = x_sample.shape[1], x_prompt.shape[1]
    key = (SS, SP)
    if key not in _CACHE:
        _CACHE[key] = build(SS, SP)[0]
    nc = _CACHE[key]
    cs, f3, gts, gtp = _consts(SS, SP)
    vec = _pack_vec(norm_g, b_ada, conv_w, conv_b, b_rg, lam)
    fgb = np.ascontiguousarray(np.broadcast_to(np.asarray(final_g, np.float32)[None, :], (128, D)))
    ident = np.eye(128, dtype=np.float32)
    f32 = lambda a: np.ascontiguousarray(np.asarray(a, np.float32))
    common = {"vec": vec, "fgb": fgb, "w_ada": f32(w_ada), "w_in": f32(w_in), "w_rg": f32(w_rg),
              "w_a_out": f32(w_a_out), "w_b_out": f32(w_b_out), "w_o": f32(w_o), "ident": ident,
              "cs": cs, "f3": f3, "gts": gts, "gtp": gtp, "xp": f32(x_prompt[0])}
    cp_ = np.asarray(c_prompt, np.float32)[0].reshape(NCH, 128).T
    in_maps = []
    for i in range(ncores):
        csm = np.asarray(c_sample, np.float32)[i].reshape(NCH, 128).T
        cvv = np.ascontiguousarray(np.stack([csm, cp_], axis=2))
        m = dict(common)
        m["xs"] = f32(x_sample[i])
        m["cv"] = cvv
        in_maps.append(m)
    res = run_bass_kernel_spmd(nc, in_maps, core_ids=list(range(ncores)))
    y_sample = np.stack([np.asarray(res.results[i]["ys"], np.float32) for i in range(ncores)], axis=0)
    y_prompt = np.concatenate([np.asarray(res.results[i]["yp"], np.float32) for i in range(ncores)], axis=0)[None]
    return y_prompt, y_sample


def kernel(**inputs):
    return run(**inputs)
```
